# Optimizing a Trainium2 kernel written in Bass

```python
import jax, jax.numpy as jnp
from jax import lax
import numpy as np

D_MODEL = 1024
BATCH = 8
SEQ = 8192
DEPTH = 1

HEAD_DIM = 64
D_MIX = D_MODEL
D_CONV = D_MIX // 2
D_ATTN = D_MIX - D_CONV
N_HEADS = D_ATTN // HEAD_DIM
CONV_WIDTH = 3
DILATED_BRANCHES = ((128, 1), (512, 4), (2048, 16))
BLOCK = 128
D_FF = ((-(-8 * D_MODEL // 3) + 255) // 256) * 256
D_IN = 3 * D_CONV + 3 * D_ATTN
EPS = 1e-6

kernel_name = "hybrid_shortconv_dilated_swa_swiglu"


def rms_norm(x, g):
    xf = x.astype(jnp.float32)
    y = xf * lax.rsqrt(jnp.mean(xf * xf, axis=-1, keepdims=True) + EPS)
    return (y * g.astype(jnp.float32)).astype(x.dtype)


def short_conv(u, w):
    return lax.conv_general_dilated(
        u, w[:, None, :].astype(u.dtype), window_strides=(1,),
        padding=[(CONV_WIDTH - 1, 0)], dimension_numbers=("NWC", "WIO", "NWC"),
        feature_group_count=u.shape[-1])


def dilated_branch(q, k, v, window, dil):
    B, S, H, dh = q.shape
    M = S // dil
    L = window // dil
    nb = -(-M // BLOCK)
    Mp = nb * BLOCK
    pad = Mp - M

    def to_sub(t):
        return t.reshape(B, M, dil, H, dh).transpose(0, 2, 1, 3, 4).reshape(B * dil, M, H, dh)

    qs, ks, vs = to_sub(q), to_sub(k), to_sub(v)
    Bd = B * dil
    qb = jnp.pad(qs, ((0, 0), (0, pad), (0, 0), (0, 0))).reshape(Bd, nb, BLOCK, H, dh)

    def band(t):
        tp = jnp.pad(t, ((0, 0), (BLOCK, pad), (0, 0), (0, 0)))
        prev = tp[:, :Mp].reshape(Bd, nb, BLOCK, H, dh)
        cur = tp[:, BLOCK:].reshape(Bd, nb, BLOCK, H, dh)
        return jnp.concatenate([prev, cur], axis=2)

    kw, vw = band(ks), band(vs)
    s = jnp.einsum("bnqhd,bnkhd->bnhqk", qb, kw, preferred_element_type=jnp.float32)

    i = jnp.arange(BLOCK)[:, None]
    j = jnp.arange(2 * BLOCK)[None, :]
    dist = BLOCK + i - j
    kpos = (jnp.arange(nb)[:, None, None] - 1) * BLOCK + j[None]
    valid = ((dist >= 0) & (dist <= L))[None] & (kpos >= 0)
    s = jnp.where(valid[None, :, None], s, -jnp.inf)

    m = jnp.max(s, axis=-1, keepdims=True)
    e = jnp.exp(s - m)
    den = jnp.sum(e, axis=-1, keepdims=True)
    p = e / den
    lse = (m + jnp.log(den))[..., 0]
    o = jnp.einsum("bnhqk,bnkhd->bnqhd", p.astype(vw.dtype), vw)

    o = o.reshape(Bd, Mp, H, dh)[:, :M]
    lse = lse.transpose(0, 1, 3, 2).reshape(Bd, Mp, H)[:, :M]
    o = o.reshape(B, dil, M, H, dh).transpose(0, 2, 1, 3, 4).reshape(B, S, H, dh)
    lse = lse.reshape(B, dil, M, H).transpose(0, 2, 1, 3).reshape(B, S, H)
    return o, lse


def dilated_mixture(q, k, v):
    outs, lses = [], []
    for window, dil in DILATED_BRANCHES:
        o, lse = dilated_branch(q, k, v, window, dil)
        outs.append(o)
        lses.append(lse)
    w = jax.nn.softmax(jnp.stack(lses, axis=0), axis=0)
    o = jnp.sum(w[..., None] * jnp.stack(outs, axis=0).astype(jnp.float32), axis=0)
    return o.astype(q.dtype)


def setup_inputs(seed: int = 0) -> dict:
    key = jax.random.key(seed)
    ks = jax.random.split(key, 13)
    f32 = jnp.float32

    def gain(k_, n):
        return 1.0 + 0.05 * jax.random.normal(k_, (DEPTH, n), f32)

    return {
        "x": jax.random.normal(ks[0], (BATCH, SEQ, D_MODEL), f32),
        "g_mix": gain(ks[1], D_MODEL),
        "w_in": jax.random.normal(ks[2], (DEPTH, D_MODEL, D_IN), f32) * D_MODEL ** -0.5,
        "conv_w": jax.random.normal(ks[3], (DEPTH, CONV_WIDTH, D_CONV), f32) * CONV_WIDTH ** -0.5,
        "g_q": gain(ks[4], HEAD_DIM),
        "g_k": gain(ks[5], HEAD_DIM),
        "g_conv_out": gain(ks[6], D_CONV),
        "g_attn_out": gain(ks[7], D_ATTN),
        "w_out": jax.random.normal(ks[8], (DEPTH, D_MIX, D_MODEL), f32) * D_MIX ** -0.5,
        "g_ffn": gain(ks[9], D_MODEL),
        "w_gate": jax.random.normal(ks[10], (DEPTH, D_MODEL, D_FF), f32) * D_MODEL ** -0.5,
        "w_up": jax.random.normal(ks[11], (DEPTH, D_MODEL, D_FF), f32) * D_MODEL ** -0.5,
        "w_down": jax.random.normal(ks[12], (DEPTH, D_FF, D_MODEL), f32) * D_FF ** -0.5,
    }


def reference(x, g_mix, w_in, conv_w, g_q, g_k, g_conv_out, g_attn_out, w_out,
              g_ffn, w_gate, w_up, w_down):
    B, S, _ = x.shape
    splits = [D_CONV, 2 * D_CONV, 3 * D_CONV, 3 * D_CONV + D_ATTN, 3 * D_CONV + 2 * D_ATTN]
    for l in range(DEPTH):
        h = rms_norm(x, g_mix[l])
        z = h @ w_in[l]
        u, gb, gc, q, k, v = jnp.split(z, splits, axis=-1)

        y_conv = gb * short_conv(gc * u, conv_w[l])

        q = rms_norm(q.reshape(B, S, N_HEADS, HEAD_DIM), g_q[l]) * (HEAD_DIM ** -0.5)
        k = rms_norm(k.reshape(B, S, N_HEADS, HEAD_DIM), g_k[l])
        v = v.reshape(B, S, N_HEADS, HEAD_DIM)
        y_attn = dilated_mixture(q, k, v).reshape(B, S, D_ATTN)

        mix = jnp.concatenate([rms_norm(y_conv, g_conv_out[l]),
                               rms_norm(y_attn, g_attn_out[l])], axis=-1)
        x = x + mix @ w_out[l]

        h = rms_norm(x, g_ffn[l])
        x = x + (jax.nn.silu(h @ w_gate[l]) * (h @ w_up[l])) @ w_down[l]
    return x
```

```python
import numpy as np
import concourse.bass as bass
import concourse.mybir as mybir
from concourse.bass_utils import run_bass_kernel_spmd

F32 = mybir.dt.float32
BF16 = mybir.dt.bfloat16
AF = mybir.ActivationFunctionType
ALU = mybir.AluOpType

S_LEN = 8192
D = 1024
D_IN = 3072
D_FF = 2816
NFT = D_FF // 128
EPS = 1e-6
NCORES = 8
_LIMITS = {}
VROW = 8 * 65

C_GMIX, C_GFFN, C_GCO, C_CW, C_GQ, C_GK, C_GAO, C_N = 0, 8, 16, 20, 32, 33, 34, 42


class Buf:
    __slots__ = ("w", "r", "name")

    def __init__(self, name=""):
        self.w = None
        self.r = []
        self.name = name


class Op:
    __slots__ = ("eng", "fn", "deps", "sig", "val", "dma", "sem", "prev")


class Sched:
    ENGS = ("pe", "act", "dve", "pool", "sp")

    def __init__(self, nc, sems, dsems):
        self.nc = nc
        self.sem = sems
        self.dsem = dsems
        self.cnt = {e: 0 for e in sems}
        self.dcnt = {q: [0] * len(l) for q, l in dsems.items()}
        self.dnext = {q: 0 for q in dsems}
        self.waited = {e: {} for e in self.ENGS}
        self.ops = {e: [] for e in self.ENGS}
        self.bufs = []
        self.barrier_tokens = []

    def buf(self, name=""):
        b = Buf(name)
        self.bufs.append(b)
        return b

    def add(self, eng, fn, reads=(), writes=(), dma=False):
        o = Op()
        o.eng, o.fn, o.dma, o.sig, o.val, o.sem, o.prev = eng, fn, dma, False, None, None, 0
        deps = []
        seen = set()

        def adddep(d):
            if d is None or id(d) in seen:
                return
            seen.add(id(d))
            if (not d.dma) and (not dma) and d.eng == eng and eng == "pe":
                return
            deps.append(d)

        for b in reads:
            adddep(b.w)
        for b in writes:
            adddep(b.w)
            for r in b.r:
                adddep(r)
        o.deps = deps
        for d in deps:
            d.sig = True
        for b in reads:
            b.r.append(o)
        for b in writes:
            b.w = o
            b.r = []
        self.ops[eng].append(o)
        return o

    def flush(self, final=False):
        nc = self.nc
        tokens = []
        for e in self.ENGS:
            lastc = None
            for o in self.ops[e]:
                if not o.dma:
                    lastc = o
            if lastc is not None:
                lastc.sig = True
        for e in self.ENGS:
            for o in self.ops[e]:
                if o.dma:
                    k = self.dnext[e]
                    n = len(self.dsem[e])
                    i = k % n
                    o.sem = self.dsem[e][i]
                    o.prev = self.dcnt[e][i]
                    self.dcnt[e][i] += 16
                    o.val = self.dcnt[e][i]
                    self.dnext[e] = k + 1
                elif o.sig:
                    self.cnt[e] += 1
                    o.val = self.cnt[e]
                    o.sem = self.sem[e]
        for e in self.sem:
            if self.cnt[e] > 0:
                tokens.append((self.sem[e], self.cnt[e]))
        for q in self.dsem:
            for i, s in enumerate(self.dsem[q]):
                if self.dcnt[q][i] > 0:
                    tokens.append((s, self.dcnt[q][i]))

        def run(engname, e):
            W = self.waited[engname]
            for o in self.ops[engname]:
                for d in o.deps:
                    if W.get(id(d.sem), 0) < d.val:
                        e.wait_ge(d.sem, d.val)
                        W[id(d.sem)] = d.val
                if o.dma:
                    if o.prev > 0 and W.get(id(o.sem), 0) < o.prev:
                        e.wait_ge(o.sem, o.prev)
                        W[id(o.sem)] = o.prev
                    o.fn(e).then_inc(o.sem, 16)
                else:
                    ins = o.fn(e)
                    if o.sig:
                        ins.then_inc(o.sem, 1)
            for (s, v) in tokens:
                if W.get(id(s), 0) < v:
                    e.wait_ge(s, v)
                    W[id(s)] = v

        with nc.Block() as block:
            @block.tensor
            def _(e):
                run("pe", e)

            @block.scalar
            def _(e):
                run("act", e)

            @block.vector
            def _(e):
                run("dve", e)

            @block.gpsimd
            def _(e):
                run("pool", e)

            @block.sync
            def _(e):
                run("sp", e)

        self.ops = {e: [] for e in self.ENGS}
        for b in self.bufs:
            b.w = None
            b.r = []
        self.bufs = []


def build(debug=False, phases="abc"):
    nc = bass.Bass("TRN2", target_bir_lowering=False)
    x = nc.dram_tensor("x", [S_LEN, D], F32, kind="ExternalInput").ap()
    w_in = nc.dram_tensor("w_in", [D, D_IN], F32, kind="ExternalInput").ap()
    w_out = nc.dram_tensor("w_out", [D, D], F32, kind="ExternalInput").ap()
    w_gate = nc.dram_tensor("w_gate", [D, D_FF], F32, kind="ExternalInput").ap()
    w_up = nc.dram_tensor("w_up", [D, D_FF], F32, kind="ExternalInput").ap()
    w_down = nc.dram_tensor("w_down", [D_FF, D], F32, kind="ExternalInput").ap()
    cols_d = nc.dram_tensor("cols", [128, C_N], F32, kind="ExternalInput").ap()
    ident_d = nc.dram_tensor("ident", [128, 128], F32, kind="ExternalInput").ap()
    mask_d = nc.dram_tensor("masks", [128, 1536], F32, kind="ExternalInput").ap()
    out = nc.dram_tensor("out", [S_LEN, D], F32, kind="ExternalOutput").ap()
    sk = "ExternalOutput" if debug else "Internal"
    qkT_d = nc.dram_tensor("qkT_d", [8, 128, S_LEN], BF16, kind=sk).ap()
    V_d = nc.dram_tensor("V_d", [S_LEN, VROW], BF16, kind=sk).ap()
    mixT_d = nc.dram_tensor("mixT_d", [8, 128, S_LEN], BF16, kind=sk).ap()

    wo_b = nc.dram_tensor("wo_b", [D, D], BF16, kind="Internal").ap()
    wg_b = nc.dram_tensor("wg_b", [D, D_FF], BF16, kind="Internal").ap()
    wu_b = nc.dram_tensor("wu_b", [D, D_FF], BF16, kind="Internal").ap()
    wd_b = nc.dram_tensor("wd_b", [D_FF, D], BF16, kind="Internal").ap()
    wscr = (w_out, w_gate, w_up, w_down, wo_b, wg_b, wu_b, wd_b)

    from contextlib import ExitStack

    with ExitStack() as gstack:
        def sem(name):
            return gstack.enter_context(nc.semaphore(name))

        sems = {e: sem("s_" + e) for e in ("pe", "act", "dve", "pool")}
        dsems = {"sp": [sem("d_sp%d" % i) for i in range(8)],
                 "pool": [sem("d_pl%d" % i) for i in range(8)]}
        S = Sched(nc, sems, dsems)

        def gsb(name, shape, dt):
            return gstack.enter_context(nc.sbuf_tensor(name, shape, dt))

        cols = gsb("cols_sb", [128, C_N], F32)
        gq8 = gsb("gq8", [128, 1], F32)
        ident = gsb("ident_bf", [128, 128], BF16)
        masks = gsb("masks_bf", [128, 1536], BF16)
        ones_bf = gsb("ones_bf", [128, 128], BF16)
        bones_bf = gsb("bones_bf", [128, 128], BF16)
        ones_f = gsb("ones_f", [128, 64], F32)

        with ExitStack() as st:
            stage = st.enter_context(nc.sbuf_tensor("stage", [128, 1536 + 128], F32))
            b_cols, b_stage = S.buf(), S.buf()
            b_c = S.buf()
            S.add("sp", lambda e: e.dma_start(out=cols[:, :], in_=cols_d), writes=[b_cols], dma=True)
            S.add("sp", lambda e: e.dma_start(out=stage[:, 0:1536], in_=mask_d), writes=[b_stage], dma=True)
            S.add("sp", lambda e: e.dma_start(out=stage[:, 1536:1664], in_=ident_d), writes=[b_stage], dma=True)
            S.add("dve", lambda e: e.tensor_copy(out=masks[:, :], in_=stage[:, 0:1536]), reads=[b_stage], writes=[b_c])
            S.add("dve", lambda e: e.tensor_copy(out=ident[:, :], in_=stage[:, 1536:1664]), reads=[b_stage], writes=[b_c])
            S.add("dve", lambda e: e.tensor_scalar(out=gq8[:, :], in0=cols[:, C_GQ:C_GQ + 1], scalar1=0.125,
                                                     scalar2=None, op0=ALU.mult), reads=[b_cols], writes=[b_c])
            S.add("pool", lambda e: e.memset(ones_bf[:, :], 1.0), writes=[b_c])
            S.add("pool", lambda e: e.memset(bones_bf[:, :], 0.0), writes=[b_c])
            S.add("pool", lambda e: e.memset(bones_bf[0:64, 0:64], 1.0), writes=[b_c])
            S.add("pool", lambda e: e.memset(bones_bf[64:128, 64:128], 1.0), writes=[b_c])
            S.add("pool", lambda e: e.memset(ones_f[:, :], 1.0), writes=[b_c])
            S.flush()

        if "a" in phases:
            phase_a(nc, S, x, w_in, qkT_d, V_d, mixT_d, cols, gq8, ident, ones_bf, bones_bf,
                    wscr if "c" in phases else None)
        if "b" in phases:
            phase_b(nc, S, qkT_d, V_d, mixT_d, cols, masks, ones_bf, ones_f, None)
        if "c" in phases:
            phase_c(nc, S, x, out, wo_b, wg_b, wu_b, wd_b, mixT_d, cols, ident,
                    precast=None if "a" in phases else wscr)
    return nc


def phase_a(nc, S, x, w_in, qkT_d, V_d, mixT_d, cols, gq8, ident, ones_bf, bones_bf, wscr=None):
    NT = _LIMITS.get('a', S_LEN // 512)
    with ExitStack_() as st:
        def sb(name, shape, dt):
            return st.enter_context(nc.sbuf_tensor(name, shape, dt))

        def ps(name, shape, dt=F32):
            return st.enter_context(nc.psum_tensor(name, shape, dt))

        Win = sb("Win", [128, 8, D_IN], BF16)
        xbuf = [sb("xbuf%d" % i, [128, 4, 1024], F32) for i in range(2)]
        junk = sb("junk", [128, 1024], BF16)
        ss = sb("ss", [128, 2, 4], F32)
        rstd = sb("rstd", [128, 2, 4], F32)
        h = sb("h", [128, 4, 1024], BF16)
        hT = [sb("hT%d" % i, [128, 8, 512], BF16) for i in range(2)]
        qko = [sb("qko%d" % i, [128, 8, 512], BF16) for i in range(2)]
        vst = [sb("vst%d" % i, [128, 4, VROW], BF16) for i in range(2)]
        mixo = [sb("mixo%d" % i, [128, 4, 512], BF16) for i in range(2)]
        usb = [sb("usb%d" % i, [128, 512], F32) for i in range(4)]
        gbsb = [sb("gbsb%d" % i, [128, 512], F32) for i in range(4)]
        pbuf = [sb("pbuf%d" % i, [128, 514], F32) for i in range(4)]
        cbuf = [sb("cbuf%d" % i, [128, 512], F32) for i in range(4)]
        ybuf = [sb("ybuf%d" % i, [128, 512], F32) for i in range(4)]
        NR = 3
        sqt = [sb("sqt%d" % i, [128, 512], BF16) for i in range(NR)]
        rst = [sb("rst%d" % i, [128, 512], F32) for i in range(NR)]
        rsc = sb("rsc", [128, 512], F32)

        psT = [ps("psT%d" % i, [128, 1024], BF16) for i in range(2)]
        SSqk = ps("SSqk", [128, 512])
        SSc = SSqk
        NM = 5
        M = [ps("M%d" % i, [128, 512]) for i in range(NM)]
        sqc = [sb("sqc%d" % i, [128, 512], BF16) for i in range(4)]

        B = S.buf
        b_Win = [B() for _ in range(8)]
        b_x = [[B() for _ in range(4)] for _ in range(2)]
        b_junk, b_ss, b_rstd = B(), [B(), B()], [B(), B()]
        b_h = [B() for _ in range(4)]
        b_hT = [[B() for _ in range(8)] for _ in range(2)]
        b_qko, b_vst, b_mixo = [B(), B()], [B(), B()], [B(), B()]
        b_usb, b_gbsb, b_p, b_c, b_y = ([B() for _ in range(4)] for _ in range(5))
        b_usb, b_gbsb, b_p, b_c, b_y = list(b_usb), list(b_gbsb), list(b_p), list(b_c), list(b_y)
        b_sq, b_rs = [B() for _ in range(NR)], [B() for _ in range(NR)]
        b_rsc = B()
        b_psT = [B(), B()]
        b_SSqk = B()
        b_SSc = b_SSqk
        b_sqc = [B() for _ in range(4)]
        b_M = [B() for _ in range(NM)]
        b_init = B()

        w_in_v = w_in.rearrange("(k p) n -> p k n", p=128)
        b_Win = [B() for _ in range(24)]
        order = []
        for f in range(4):
            order += [f, 8 + f, 4 + f]
        order += list(range(12, 24))
        for g in order:
            S.add("pool", lambda e, g=g: e.dma_start(out=Win[:, :, g * 128:(g + 1) * 128], in_=w_in_v[:, :, g * 128:(g + 1) * 128]),
                  writes=[b_Win[g]], dma=True)
        for s in range(2):
            S.add("pool", lambda e, s=s: e.memset(vst[s][:, :, :], 1.0), writes=[b_vst[s]])
        for f in range(4):
            S.add("pool", lambda e, f=f: e.memset(pbuf[f][:, 0:2], 0.0), writes=[b_p[f]])

        mrot = [0]
        rrot = [0]

        def next_bank():
            i = mrot[0] % NM
            mrot[0] += 1
            return i

        def load_x(t):
            s = t % 2
            src = x[t * 512:(t + 1) * 512, :].rearrange("(b p) d -> p b d", p=128)
            S.add("sp", lambda e: e.dma_start(out=xbuf[s][:, :, :], in_=src), writes=b_x[s], dma=True)

        def stats_steps(t):
            s = t % 2
            st_ = []
            for blk in range(4):
                st_.append(lambda blk=blk: S.add("act", lambda e: e.activation(out=junk[:, :], in_=xbuf[s][:, blk, :], func=AF.Square,
                                                                                accum_out=ss[:, s, blk:blk + 1]),
                                                 reads=[b_x[s][blk]], writes=[b_junk, b_ss[s]]))
            st_.append(lambda: S.add("act", lambda e: e.activation(out=rstd[:, s, :], in_=ss[:, s, :], func=AF.Ln, scale=1.0 / D, bias=EPS),
                                     reads=[b_ss[s]], writes=[b_rstd[s]]))
            st_.append(lambda: S.add("act", lambda e: e.activation(out=rstd[:, s, :], in_=rstd[:, s, :], func=AF.Exp, scale=-0.5),
                                     reads=[b_rstd[s]], writes=[b_rstd[s]]))
            for blk in range(4):
                if blk % 2 == 0:
                    st_.append(lambda blk=blk: S.add("act", lambda e: e.activation(out=h[:, blk, :], in_=xbuf[s][:, blk, :], func=AF.Copy,
                                                                                    scale=rstd[:, s, blk:blk + 1]),
                                                     reads=[b_x[s][blk], b_rstd[s]], writes=[b_h[blk]]))
                else:
                    st_.append(lambda blk=blk: S.add("dve", lambda e: e.tensor_scalar(out=h[:, blk, :], in0=xbuf[s][:, blk, :],
                                                                                       scalar1=rstd[:, s, blk:blk + 1], scalar2=None,
                                                                                       op0=ALU.mult),
                                                     reads=[b_x[s][blk], b_rstd[s]], writes=[b_h[blk]]))
            return st_

        def transpose_steps(t):
            s = t % 2
            st_ = []
            for kp in range(4):
                def rnd(kp=kp):
                    pb = kp % 2
                    for kk in range(2):
                        kc = 2 * kp + kk
                        for blk in range(4):
                            S.add("pe", lambda e, kc=kc, blk=blk, pb=pb, kk=kk: e.transpose(
                                out=psT[pb][:, kk * 512 + blk * 128: kk * 512 + (blk + 1) * 128],
                                in_=h[:, blk, kc * 128:(kc + 1) * 128], identity=ident[:, :]),
                                reads=[b_h[blk]], writes=[b_psT[pb]])
                    kc0, kc1 = 2 * kp, 2 * kp + 1
                    S.add("act", lambda e, kc=kc0, pb=pb: e.activation(out=hT[s][:, kc, :], in_=psT[pb][:, 0:512],
                                                                       func=AF.Copy, scale=cols[:, C_GMIX + kc:C_GMIX + kc + 1]),
                          reads=[b_psT[pb]], writes=[b_hT[s][kc0]])
                    S.add("act", lambda e, kc=kc1, pb=pb: e.activation(out=hT[s][:, kc, :], in_=psT[pb][:, 512:1024],
                                                                       func=AF.Copy, scale=cols[:, C_GMIX + kc:C_GMIX + kc + 1]),
                          reads=[b_psT[pb]], writes=[b_hT[s][kc1]])
                st_.append(rnd)
            return st_

        asteps = []

        def pop_astep():
            if asteps:
                asteps.pop(0)()

        def group_fm(t, col0):
            s = t % 2
            bi = next_bank()
            for kc in range(8):
                S.add("pe", lambda e, kc=kc: e.matmul(M[bi][:, :], lhsT=Win[:, kc, col0:col0 + 128], rhs=hT[s][:, kc, :],
                                                       start=(kc == 0), stop=(kc == 7)),
                      reads=[b_Win[col0 // 128], b_hT[s][kc]], writes=[b_M[bi]])
            pop_astep()
            return bi

        pending = []

        def run_pending():
            while pending:
                pending.pop(0)()

        def conv_part(t, f):
            s = t % 2
            bu = group_fm(t, f * 128)
            S.add("act", lambda e: e.activation(out=usb[f][:, :], in_=M[bu][:, :], func=AF.Copy),
                  reads=[b_M[bu]], writes=[b_usb[f]])
            run_pending()
            bc = group_fm(t, 1024 + f * 128)
            S.add("dve", lambda e: e.tensor_tensor(out=pbuf[f][:, 2:514], in0=M[bc][:, :], in1=usb[f][:, :], op=ALU.mult),
                  reads=[b_M[bc], b_usb[f]], writes=[b_p[f]])
            bg = group_fm(t, 512 + f * 128)
            S.add("act", lambda e: e.activation(out=gbsb[f][:, :], in_=M[bg][:, :], func=AF.Copy),
                  reads=[b_M[bg]], writes=[b_gbsb[f]])
            cw = C_CW + 3 * f
            S.add("act", lambda e: e.activation(out=cbuf[f][:, :], in_=pbuf[f][:, 0:512], func=AF.Copy,
                                                scale=cols[:, cw:cw + 1]),
                  reads=[b_p[f]], writes=[b_c[f]])
            S.add("dve", lambda e: e.scalar_tensor_tensor(out=cbuf[f][:, :], in0=pbuf[f][:, 1:513], scalar=cols[:, cw + 1:cw + 2],
                                                           in1=cbuf[f][:, :], op0=ALU.mult, op1=ALU.add),
                  reads=[b_p[f], b_c[f]], writes=[b_c[f]])
            S.add("dve", lambda e: e.scalar_tensor_tensor(out=cbuf[f][:, :], in0=pbuf[f][:, 2:514], scalar=cols[:, cw + 2:cw + 3],
                                                           in1=cbuf[f][:, :], op0=ALU.mult, op1=ALU.add),
                  reads=[b_p[f], b_c[f]], writes=[b_c[f]])
            S.add("act", lambda e: e.activation(out=pbuf[f][:, 0:2], in_=pbuf[f][:, 512:514], func=AF.Copy),
                  reads=[b_p[f]], writes=[b_p[f]])
            S.add("dve", lambda e: e.tensor_tensor(out=ybuf[f][:, :], in0=cbuf[f][:, :], in1=gbsb[f][:, :], op=ALU.mult),
                  reads=[b_c[f], b_gbsb[f]], writes=[b_y[f]])
            S.add("act", lambda e: e.activation(out=sqc[f][:, :], in_=ybuf[f][:, :], func=AF.Square),
                  reads=[b_y[f]], writes=[b_sqc[f]])

            def part2():
                if f == 3:
                    for ff in range(4):
                        S.add("pe", lambda e, ff=ff: e.matmul(SSc[:, :], lhsT=ones_bf[:, :], rhs=sqc[ff][:, :],
                                                               start=(ff == 0), stop=(ff == 3)),
                              reads=[b_sqc[ff]], writes=[b_SSc])
                    S.add("act", lambda e: e.activation(out=rsc[:, :], in_=SSc[:, :], func=AF.Ln, scale=1.0 / 512, bias=EPS),
                          reads=[b_SSc], writes=[b_rsc])
                    S.add("act", lambda e: e.activation(out=rsc[:, :], in_=rsc[:, :], func=AF.Exp, scale=-0.5),
                          reads=[b_rsc], writes=[b_rsc])
                    for ff in range(4):
                        S.add("dve", lambda e, ff=ff: e.scalar_tensor_tensor(
                            out=mixo[s][:, ff, :], in0=ybuf[ff][:, :], scalar=cols[:, C_GCO + ff:C_GCO + ff + 1],
                            in1=rsc[:, :], op0=ALU.mult, op1=ALU.mult),
                            reads=[b_y[ff], b_rsc], writes=[b_mixo[s]])
                    dst = mixT_d[0:4, :, t * 512:(t + 1) * 512].rearrange("f p t -> p f t")
                    S.add("pool", lambda e: e.dma_start(out=dst, in_=mixo[s][:, :, :]), reads=[b_mixo[s]], dma=True)
            pending.append(part2)

        def qk_part(t, ft):
            s = t % 2
            bi = group_fm(t, 1536 + ft * 128)
            ri = rrot[0] % NR
            rrot[0] += 1
            S.add("act", lambda e: e.activation(out=sqt[ri][:, :], in_=M[bi][:, :], func=AF.Square),
                  reads=[b_M[bi]], writes=[b_sq[ri]])
            run_pending()
            gcol = gq8[:, 0:1] if ft < 4 else cols[:, C_GK:C_GK + 1]

            def part2():
                S.add("pe", lambda e: e.matmul(SSqk[:, :], lhsT=bones_bf[:, :], rhs=sqt[ri][:, :], start=True, stop=True),
                      reads=[b_sq[ri]], writes=[b_SSqk])
                S.add("act", lambda e: e.activation(out=rst[ri][:, :], in_=SSqk[:, :], func=AF.Ln, scale=1.0 / 64, bias=EPS),
                      reads=[b_SSqk], writes=[b_rs[ri]])
                S.add("act", lambda e: e.activation(out=rst[ri][:, :], in_=rst[ri][:, :], func=AF.Exp, scale=-0.5),
                      reads=[b_rs[ri]], writes=[b_rs[ri]])
                S.add("dve", lambda e: e.scalar_tensor_tensor(out=qko[s][:, ft, :], in0=M[bi][:, :], scalar=gcol,
                                                              in1=rst[ri][:, :], op0=ALU.mult, op1=ALU.mult),
                      reads=[b_M[bi], b_rs[ri]], writes=[b_qko[s]])
                if ft == 7:
                    dst = qkT_d[:, :, t * 512:(t + 1) * 512].rearrange("j p t -> p j t")
                    S.add("pool", lambda e: e.dma_start(out=dst, in_=qko[s][:, :, :]), reads=[b_qko[s]], dma=True)
            pending.append(part2)

        def v_part(t, blk):
            s = t % 2
            bi = next_bank()
            for kc in range(8):
                S.add("pe", lambda e, kc=kc: e.matmul(M[bi][:, :], lhsT=hT[s][:, kc, blk * 128:(blk + 1) * 128],
                                                       rhs=Win[:, kc, 2560:3072], start=(kc == 0), stop=(kc == 7)),
                      reads=[b_Win[20], b_Win[21], b_Win[22], b_Win[23], b_hT[s][kc]], writes=[b_M[bi]])
            pop_astep()
            dstv = vst[s][:, blk, :].rearrange("p (h e) -> p h e", e=65)[:, :, 0:64]
            srcv = M[bi][:, :].rearrange("p (h d) -> p h d", d=64)
            S.add("act", lambda e: e.activation(out=dstv, in_=srcv, func=AF.Copy), reads=[b_M[bi]], writes=[b_vst[s]])
            run_pending()
            if blk == 3:
                dst = V_d[t * 512:(t + 1) * 512, :].rearrange("(b p) e -> p b e", p=128)
                S.add("pool", lambda e: e.dma_start(out=dst, in_=vst[s][:, :, :]), reads=[b_vst[s]], dma=True)

        pc_list = precast_list(wscr) if wscr is not None else []
        load_x(0)
        if NT > 1:
            load_x(1)
        for f_ in stats_steps(0) + transpose_steps(0):
            f_()
        for t in range(NT):
            if t + 1 < NT:
                asteps.extend(stats_steps(t + 1) + transpose_steps(t + 1))
            for f in range(4):
                conv_part(t, f)
            for ft in range(8):
                qk_part(t, ft)
            for blk in range(4):
                v_part(t, blk)
            run_pending()
            while asteps:
                asteps.pop(0)()
            if t + 2 < NT:
                load_x(t + 2)
            if wscr is not None and t >= 1:
                emit_precast(S, None, n=2, lst=pc_list)
        if wscr is not None:
            emit_precast(S, None, lst=pc_list)
        S.flush()


def ExitStack_():
    from contextlib import ExitStack
    return ExitStack()


def precast_list(wscr):
    w_out, w_gate, w_up, w_down, wo_b, wg_b, wu_b, wd_b = wscr
    lst = []
    for (src_, dst_, rows) in ((w_out, wo_b, D), (w_gate, wg_b, D), (w_up, wu_b, D), (w_down, wd_b, D_FF)):
        step = 256
        for r in range(0, rows, step):
            lst.append((dst_[r:r + step, :], src_[r:r + step, :]))
    return lst


def emit_precast(S, wscr, n=None, lst=None):
    lst = precast_list(wscr) if lst is None else lst
    k = 0
    while lst and (n is None or k < n):
        d_, s_ = lst.pop(0)
        S.add("pool", lambda e, d_=d_, s_=s_: e.dma_start(out=d_, in_=s_), dma=True)
        k += 1


def phase_b(nc, S, qkT_d, V_d, mixT_d, cols, masks, ones_bf, ones_f, wscr=None):
    NCH = _LIMITS.get('b', S_LEN // 2048)
    with ExitStack_() as st:
        def sb(name, shape, dt):
            return st.enter_context(nc.sbuf_tensor(name, shape, dt))

        def ps(name, shape, dt=F32):
            return st.enter_context(nc.psum_tensor(name, shape, dt))

        qj = [sb("qj%d" % i, [128, 2048], BF16) for i in range(2)]
        kj = [sb("kj%d" % i, [128, 2, 2048], BF16) for i in range(2)]
        vnat = sb("vnat", [128, 17, VROW], BF16)
        vd4 = sb("vd4", [128, 20, VROW], BF16)
        vd16 = [sb("vd16_%d" % i, [128, 16, VROW], BF16) for i in range(2)]
        NE = 4
        Et = [sb("E%d" % i, [128, 512], BF16) for i in range(NE)]
        num = sb("num", [128, 2048], F32)
        yT = sb("yT", [64, 8, 2048], F32)
        bcs = sb("bcs", [64, 2048], F32)
        ysqacc = sb("ysqacc", [64, 2048], F32)
        rsa = sb("rsa", [64, 512], F32)
        mao = sb("mao", [64, 8, 512], BF16)

        acc = ps("acc", [128, 2048])
        NST = 3
        ST = [ps("ST%d" % i, [128, 512]) for i in range(NST)]
        SSa = [ps("SSa%d" % i, [128, 512]) for i in range(1)]

        B = S.buf
        b_qj, b_kj = [B(), B()], [B(), B()]
        b_vnat, b_vd4, b_vd16 = B(), B(), [B(), B()]
        b_E = [B() for _ in range(NE)]
        b_num, b_yT = B(), [B() for _ in range(8)]
        b_bcs, b_ysqacc, b_rsa, b_mao = B(), B(), B(), B()
        b_rden = [B(), B()]
        rden_d = nc.dram_tensor("rden_d", [2, 2048], F32, kind="Internal").ap()
        b_acc, b_ST, b_SSa = [B() for _ in range(4)], [B() for _ in range(NST)], [B()]
        pending_epi = []
        steps = []
        urgent = []

        erot = [0]
        srot = [0]
        prot = [0]

        def sl(a, n, step):
            return slice(a, a + (n - 1) * step + 1, step)

        def load_pair(c, j):
            s = prot[0] % 2
            prot[0] += 1
            t0 = c * 2048
            S.add("sp", lambda e: e.dma_start(out=qj[s][:, :], in_=qkT_d[j, :, t0:t0 + 2048]), writes=[b_qj[s]], dma=True)
            if c > 0:
                S.add("sp", lambda e: e.dma_start(out=kj[s][:, :, :],
                                                  in_=qkT_d[4 + j, :, t0 - 2048:t0 + 2048].rearrange("p (c t) -> p c t", c=2)),
                      writes=[b_kj[s]], dma=True)
            else:
                S.add("sp", lambda e: e.dma_start(out=kj[s][:, 1, :], in_=qkT_d[4 + j, :, t0:t0 + 2048]),
                      writes=[b_kj[s]], dma=True)
            return s

        for c in range(NCH):
            t0 = c * 2048
            cp = c % 2
            pp = (c - 1) % 2
            ps_first = load_pair(c, 0)
            if c == 0:
                S.add("sp", lambda e: e.dma_start(out=vnat[:, 1:17, :], in_=V_d[0:2048, :].rearrange("(b p) e -> p b e", p=128)),
                      writes=[b_vnat], dma=True)
                S.add("sp", lambda e: e.dma_start(out=vd4[:, 4:20, :].rearrange("p (t r) e -> p t r e", r=4),
                                                  in_=V_d[0:2048, :].rearrange("(t p r) e -> p t r e", p=128, r=4)),
                      writes=[b_vd4], dma=True)
            else:
                S.add("sp", lambda e, t0=t0: e.dma_start(out=vnat[:, 0:17, :],
                                                         in_=V_d[t0 - 128:t0 + 2048, :].rearrange("(b p) e -> p b e", p=128)),
                      writes=[b_vnat], dma=True)
                S.add("sp", lambda e, t0=t0: e.dma_start(out=vd4[:, 0:20, :].rearrange("p (t r) e -> p t r e", r=4),
                                                         in_=V_d[t0 - 512:t0 + 2048, :].rearrange("(t p r) e -> p t r e", p=128, r=4)),
                      writes=[b_vd4], dma=True)
            S.add("sp", lambda e, t0=t0, cp=cp: e.dma_start(out=vd16[cp][:, :, :],
                                                            in_=V_d[t0:t0 + 2048, :].rearrange("(p r) e -> p r e", r=16)),
                  writes=[b_vd16[cp]], dma=True)

            items = []
            if c > 0:
                items.append((1, 1, 0, sl(1920, 128, 1), vnat, b_vnat, 0, [(0, "U")]))
            for b in range(16):
                qb = [(128 * b, "L")]
                if b < 15:
                    qb.append((128 * (b + 1), "U"))
                items.append((1, 1, 1, sl(128 * b, 128, 1), vnat, b_vnat, b + 1, qb))
            if c > 0:
                for r in range(4):
                    items.append((4, 4, 0, sl(1536 + r, 128, 4), vd4, b_vd4, r, [(r, "U")]))
            for t in range(4):
                for r in range(4):
                    qb = [(512 * t + r, "L")]
                    if t < 3:
                        qb.append((512 * (t + 1) + r, "U"))
                    items.append((4, 4, 1, sl(512 * t + r, 128, 4), vd4, b_vd4, 4 + 4 * t + r, qb))
            if c > 0:
                for r in range(16):
                    items.append((16, 16, 0, sl(r, 128, 16), vd16[pp], b_vd16[pp], r, [(r, "U")]))
            for r in range(16):
                items.append((16, 16, 1, sl(r, 128, 16), vd16[cp], b_vd16[cp], r, [(r, "L")]))

            def cls_of(it):
                kinds = "".join(k for (_q, k) in it[7])
                return kinds
            banks = []
            i = 0
            while i < len(items):
                cl = cls_of(items[i])
                cap = 2 if cl == "LU" else 4
                grp = [items[i]]
                jj = i + 1
                while jj < len(items) and len(grp) < cap and cls_of(items[jj]) == cl:
                    grp.append(items[jj])
                    jj += 1
                banks.append((cl, grp))
                i = jj
            nb = len(banks)

            ps_next = ps_first
            for j in range(4):
                ps_ = ps_next
                if j < 3:
                    ps_next = load_pair(c, j + 1)
                for hd in (2 * j, 2 * j + 1):
                    r0 = 64 * (hd % 2)
                    started = [False] * 4

                    def emit_st(bi_, r0=r0, ps_=ps_):
                        cl, grp = banks[bi_]
                        si = srot[0] % NST
                        srot[0] += 1
                        off = 0
                        for (dil, qstep, kc_, ks, vt, vb, vslot, qb) in grp:
                            npart = len(qb)
                            q0 = qb[0][0]
                            if npart == 1:
                                rhs = qj[ps_][r0:r0 + 64, slice(q0, q0 + 127 * qstep + 1, qstep)]
                                oap = ST[si][:, off:off + 128]
                            elif dil == 1:
                                rhs = qj[ps_][r0:r0 + 64, q0:q0 + 256]
                                oap = ST[si][:, off:off + 256]
                            else:
                                t_, r_ = q0 // 512, q0 % 512
                                rhs = qj[ps_][r0:r0 + 64, :].rearrange("p (t m r) -> p t m r", t=4, r=4)[:, t_:t_ + 2, :, r_]
                                oap = ST[si][:, off:off + 256].rearrange("p (a m) -> p a m", a=2)
                            S.add("pe", lambda e, kc_=kc_, ks=ks, rhs=rhs, oap=oap: e.matmul(
                                oap, lhsT=kj[ps_][r0:r0 + 64, kc_, ks], rhs=rhs, start=True, stop=True),
                                reads=[b_kj[ps_], b_qj[ps_]], writes=[b_ST[si]])
                            off += 128 * npart
                        return si

                    def emit_rest(bi_, si, hd=hd, started=started):
                        cl, grp = banks[bi_]
                        n = sum(len(it[7]) for it in grp) * 128
                        ei = erot[0] % NE
                        erot[0] += 1
                        moff = {"LU": 0, "U": 512, "L": 1024}[cl]
                        S.add("act", lambda e: e.activation(out=Et[ei][:, 0:n], in_=ST[si][:, 0:n], func=AF.Exp),
                              reads=[b_ST[si]], writes=[b_E[ei]])
                        S.add("dve", lambda e: e.tensor_tensor(out=Et[ei][:, 0:n], in0=Et[ei][:, 0:n],
                                                               in1=masks[:, moff:moff + n], op=ALU.mult),
                              reads=[b_E[ei]], writes=[b_E[ei]])
                        off = 0
                        for (dil, qstep, kc_, ks, vt, vb, vslot, qb) in grp:
                            vap = vt[:, vslot, hd * 65:(hd + 1) * 65]
                            segs = []
                            if dil == 16:
                                for (q0, _k) in qb:
                                    for k in range(4):
                                        segs.append((k, 512 * k + q0, off + 32 * k, 32, 16))
                                    off += 128
                            elif dil == 1 and len(qb) == 2 and (qb[0][0] // 512) == (qb[1][0] // 512):
                                segs.append((qb[0][0] // 512, qb[0][0], off, 256, 1))
                                off += 256
                            else:
                                for (q0, _k) in qb:
                                    segs.append((q0 // 512, q0, off, 128, qstep))
                                    off += 128
                            for (bk, oc, eo, nq, ostep) in segs:
                                first = not started[bk]
                                started[bk] = True
                                oslice = slice(oc, oc + (nq - 1) * ostep + 1, ostep)
                                S.add("pe", lambda e, vap=vap, eo=eo, nq=nq, oslice=oslice, first=first: e.matmul(
                                    acc[0:65, oslice], lhsT=vap, rhs=Et[ei][:, eo:eo + nq],
                                    start=first, stop=False, skip_group_check=True),
                                    reads=[vb, b_E[ei]], writes=[b_acc[bk]])

                    inflight = []
                    for bi_ in range(nb):
                        si = emit_st(bi_)
                        inflight.append((bi_, si))
                        if len(inflight) > 2:
                            emit_rest(*inflight.pop(0))
                        if urgent:
                            urgent.pop(0)()
                        if bi_ >= 2:
                            for _ in range(3):
                                if steps:
                                    f_ = steps.pop(0)
                                    if f_ is not None:
                                        f_()
                    while inflight:
                        emit_rest(*inflight.pop(0))

                    while urgent:
                        urgent.pop(0)()
                    while steps:
                        f_ = steps.pop(0)
                        if f_ is not None:
                            f_()
                    def evac(k):
                        S.add("act", lambda e: e.activation(out=num[0:65, 512 * k:512 * (k + 1)],
                                                            in_=acc[0:65, 512 * k:512 * (k + 1)], func=AF.Copy),
                              reads=[b_acc[k]], writes=[b_num])
                    evac(0)
                    for k in (1, 2, 3):
                        urgent.append(lambda k=k, evac=evac: evac(k))
                    rs_ = hd % 2

                    def mk_steps(hd=hd, rs_=rs_):
                        st_ = []
                        for k in range(4):
                            cs = slice(512 * k, 512 * (k + 1))
                            st_.append(lambda cs=cs: S.add("act", lambda e: e.activation(out=num[64:65, cs], in_=num[64:65, cs], func=AF.Ln),
                                                           reads=[b_num], writes=[b_num]))
                        for k in range(4):
                            cs = slice(512 * k, 512 * (k + 1))
                            st_.append(lambda cs=cs: S.add("act", lambda e: e.activation(out=num[64:65, cs], in_=num[64:65, cs], func=AF.Exp,
                                                                                          scale=-1.0),
                                                           reads=[b_num], writes=[b_num]))
                        st_.append(lambda: S.add("sp", lambda e: e.dma_start(out=rden_d[rs_:rs_ + 1, :], in_=num[64:65, :]),
                                                 reads=[b_num], writes=[b_rden[rs_]], dma=True))
                        st_.append(lambda: S.add("sp", lambda e: e.dma_start(out=bcs[0:64, :],
                                                                             in_=rden_d[rs_:rs_ + 1, :].partition_broadcast(64)),
                                                 reads=[b_rden[rs_]], writes=[b_bcs], dma=True))
                        st_ += [None] * 9
                        for k in range(4):
                            cs = slice(512 * k, 512 * (k + 1))
                            st_.append(lambda cs=cs: S.add("dve", lambda e: e.tensor_tensor(out=yT[:, hd, cs], in0=num[0:64, cs],
                                                                                           in1=bcs[0:64, cs], op=ALU.mult),
                                                           reads=[b_num, b_bcs], writes=[b_yT[hd]]))
                        for k in range(4):
                            cs = slice(512 * k, 512 * (k + 1))
                            if hd == 0:
                                st_.append(lambda cs=cs: S.add("act", lambda e: e.activation(out=ysqacc[:, cs], in_=yT[:, hd, cs], func=AF.Square),
                                                               reads=[b_yT[hd]], writes=[b_ysqacc]))
                            else:
                                st_.append(lambda cs=cs: S.add("act", lambda e: e.activation(out=bcs[:, cs], in_=yT[:, hd, cs], func=AF.Square),
                                                               reads=[b_yT[hd]], writes=[b_bcs]))
                        if hd != 0:
                            for k in range(4):
                                cs = slice(512 * k, 512 * (k + 1))
                                st_.append(lambda cs=cs: S.add("dve", lambda e: e.tensor_tensor(out=ysqacc[:, cs], in0=ysqacc[:, cs],
                                                                                               in1=bcs[:, cs], op=ALU.add),
                                                               reads=[b_bcs, b_ysqacc], writes=[b_ysqacc]))
                        return st_
                    steps.extend(mk_steps())

            def mk_epi(t0=t0):
                st_ = []
                for k in range(4):
                    cs = slice(512 * k, 512 * (k + 1))
                    def ssq_step(cs=cs):
                        S.add("pe", lambda e: e.matmul(SSa[0][0:64, :], lhsT=ones_f[0:64, 0:64], rhs=ysqacc[:, cs], start=True, stop=True),
                              reads=[b_ysqacc], writes=[b_SSa[0]])
                        S.add("act", lambda e: e.activation(out=rsa[:, :], in_=SSa[0][0:64, :], func=AF.Ln, scale=1.0 / 512, bias=EPS),
                              reads=[b_SSa[0]], writes=[b_rsa])
                        S.add("act", lambda e: e.activation(out=rsa[:, :], in_=rsa[:, :], func=AF.Exp, scale=-0.5),
                              reads=[b_rsa], writes=[b_rsa])
                    st_.append(ssq_step)
                    st_.append(None)
                    def stt2(h0, cs=cs):
                        for hd in (h0, h0 + 1):
                            S.add("dve", lambda e, hd=hd: e.scalar_tensor_tensor(
                                out=mao[:, hd, :], in0=yT[:, hd, cs], scalar=cols[0:64, C_GAO + hd:C_GAO + hd + 1],
                                in1=rsa[:, :], op0=ALU.mult, op1=ALU.mult),
                                reads=[b_yT[hd], b_rsa], writes=[b_mao])
                    for h0 in (0, 2, 4, 6):
                        st_.append(lambda h0=h0, stt2=stt2: stt2(h0))
                    dst = mixT_d[4:8, :, t0 + 512 * k:t0 + 512 * (k + 1)].rearrange("f (h d) t -> d (f h) t", h=2)
                    st_.append(lambda dst=dst: S.add("pool", lambda e: e.dma_start(out=dst, in_=mao[:, :, :]), reads=[b_mao], dma=True))
                return st_
            steps.extend(mk_epi())
        while urgent:
            urgent.pop(0)()
        while steps:
            f_ = steps.pop(0)
            if f_ is not None:
                f_()
        S.flush()


def phase_c(nc, S, x, out, w_out, w_gate, w_up, w_down, mixT_d, cols, ident, precast=None):
    if precast is not None:
        emit_precast(S, precast)
        S.flush()
    TCB = 2
    TC = TCB * 128
    NT = _LIMITS.get('c', S_LEN // TC)
    with ExitStack_() as st:
        def sb(name, shape, dt):
            return st.enter_context(nc.sbuf_tensor(name, shape, dt))

        def ps(name, shape, dt=F32):
            return st.enter_context(nc.psum_tensor(name, shape, dt))

        Wo = sb("Wo", [128, 8, D], BF16)
        Wg = sb("Wg", [128, 8, D_FF], BF16)
        Wu = sb("Wu", [128, 8, D_FF], BF16)
        Wd = sb("Wd", [128, NFT, D], BF16)
        xbuf = [sb("xc%d" % i, [128, TCB, 1024], F32) for i in range(2)]
        mixT = [sb("mixT%d" % i, [128, 8, TC], BF16) for i in range(2)]
        h2 = sb("h2", [128, TCB, 1024], BF16)
        h2T = [sb("h2T%d" % i, [128, 8, TC], BF16) for i in range(2)]
        actT = sb("actT", [128, NFT, TC], BF16)
        sg = [sb("sg%d" % i, [128, TC], F32) for i in range(2)]
        junk = sb("junkc", [128, 1024], BF16)
        ss = sb("ssc", [128, 2, TCB], F32)
        rstd = sb("rstdc", [128, 2, TCB], F32)

        psT = [ps("psTc%d" % i, [128, 1024], BF16) for i in range(2)]
        NM = 6
        M = [ps("Mc%d" % i, [128, 512]) for i in range(NM)]

        B = S.buf
        b_Wo = [B() for _ in range(8)]
        b_Wg = [B() for _ in range(8)]
        b_Wu = [B() for _ in range(8)]
        b_Wd = [B() for _ in range(NFT)]
        b_x = [[B() for _ in range(TCB)] for _ in range(2)]
        b_mixT = [B(), B()]
        b_h2 = [B() for _ in range(TCB)]
        b_h2T = [[B() for _ in range(8)] for _ in range(2)]
        b_actT = [B() for _ in range(NFT)]
        b_sg = [B(), B()]
        b_junk, b_ss, b_rstd = B(), [B(), B()], [B(), B()]
        b_psT = [B(), B()]
        b_M = [B() for _ in range(NM)]

        wo_v = w_out.rearrange("(k p) n -> p k n", p=128)
        wg_v = w_gate.rearrange("(k p) n -> p k n", p=128)
        wu_v = w_up.rearrange("(k p) n -> p k n", p=128)
        wd_v = w_down.rearrange("(k p) n -> p k n", p=128)
        b_Wg = [B() for _ in range(NFT // 2)]
        b_Wu = [B() for _ in range(NFT // 2)]

        def weight_loads():
            for kc in range(8):
                S.add("sp", lambda e, kc=kc: e.dma_start(out=Wo[:, kc, :], in_=wo_v[:, kc, :]), writes=[b_Wo[kc]], dma=True)
            for cg in range(NFT // 2):
                cs = slice(cg * 256, (cg + 1) * 256)
                S.add("sp", lambda e, cs=cs: e.dma_start(out=Wg[:, :, cs], in_=wg_v[:, :, cs]), writes=[b_Wg[cg]], dma=True)
                S.add("sp", lambda e, cs=cs: e.dma_start(out=Wu[:, :, cs], in_=wu_v[:, :, cs]), writes=[b_Wu[cg]], dma=True)
            for fc in range(NFT):
                S.add("sp", lambda e, fc=fc: e.dma_start(out=Wd[:, fc, :], in_=wd_v[:, fc, :]), writes=[b_Wd[fc]], dma=True)

        mrot = [0]
        grot = [0]

        def next_bank():
            i = mrot[0] % NM
            mrot[0] += 1
            return i

        def loads(t):
            s = t % 2
            a = t * TC
            S.add("sp", lambda e: e.dma_start(out=mixT[s][:, :, :], in_=mixT_d[:, :, a:a + TC].rearrange("k p t -> p k t")),
                  writes=[b_mixT[s]], dma=True)
            for blk in range(TCB):
                src = x[a + blk * 128:a + (blk + 1) * 128, :]
                S.add("sp", lambda e, blk=blk, src=src: e.dma_start(out=xbuf[s][:, blk, :], in_=src),
                      writes=[b_x[s][blk]], dma=True)

        def wout_part(t):
            s = t % 2
            for blk in range(TCB):
                for half in range(2):
                    bi = next_bank()
                    hs = slice(half * 512, (half + 1) * 512)
                    for kc in range(8):
                        S.add("pe", lambda e, kc=kc, blk=blk, hs=hs, bi=bi: e.matmul(
                            M[bi][:, :], lhsT=mixT[s][:, kc, blk * 128:(blk + 1) * 128], rhs=Wo[:, kc, hs],
                            start=(kc == 0), stop=(kc == 7)),
                            reads=[b_mixT[s], b_Wo[kc]], writes=[b_M[bi]])
                    S.add("dve", lambda e, blk=blk, hs=hs, bi=bi: e.tensor_tensor(out=xbuf[s][:, blk, hs], in0=M[bi][:, :],
                                                                                   in1=xbuf[s][:, blk, hs], op=ALU.add),
                          reads=[b_M[bi], b_x[s][blk]], writes=[b_x[s][blk]])
            for blk in range(TCB):
                S.add("act", lambda e, blk=blk: e.activation(out=junk[:, :], in_=xbuf[s][:, blk, :], func=AF.Square,
                                                              accum_out=ss[:, s, blk:blk + 1]),
                      reads=[b_x[s][blk]], writes=[b_junk, b_ss[s]])
            S.add("act", lambda e: e.activation(out=rstd[:, s, :], in_=ss[:, s, :], func=AF.Sqrt, scale=1.0 / D, bias=EPS),
                  reads=[b_ss[s]], writes=[b_rstd[s]])
            S.add("dve", lambda e: e.reciprocal(out=rstd[:, s, :], in_=rstd[:, s, :]), reads=[b_rstd[s]], writes=[b_rstd[s]])
            for blk in range(TCB):
                if blk % 2 == 0:
                    S.add("act", lambda e, blk=blk: e.activation(out=h2[:, blk, :], in_=xbuf[s][:, blk, :], func=AF.Copy,
                                                                  scale=rstd[:, s, blk:blk + 1]),
                          reads=[b_x[s][blk], b_rstd[s]], writes=[b_h2[blk]])
                else:
                    S.add("dve", lambda e, blk=blk: e.tensor_scalar(out=h2[:, blk, :], in0=xbuf[s][:, blk, :],
                                                                     scalar1=rstd[:, s, blk:blk + 1], scalar2=None, op0=ALU.mult),
                          reads=[b_x[s][blk], b_rstd[s]], writes=[b_h2[blk]])

        def transposes(t):
            s = t % 2
            for kp in range(4):
                pb = kp % 2
                for kk in range(2):
                    kc = 2 * kp + kk
                    for blk in range(TCB):
                        S.add("pe", lambda e, kc=kc, blk=blk, pb=pb, kk=kk: e.transpose(
                            out=psT[pb][:, kk * 512 + blk * 128: kk * 512 + (blk + 1) * 128],
                            in_=h2[:, blk, kc * 128:(kc + 1) * 128], identity=ident[:, :]),
                            reads=[b_h2[blk]], writes=[b_psT[pb]])
                kc0, kc1 = 2 * kp, 2 * kp + 1
                S.add("act", lambda e, kc=kc0, pb=pb: e.activation(out=h2T[s][:, kc, :], in_=psT[pb][:, 0:TC],
                                                                   func=AF.Copy, scale=cols[:, C_GFFN + kc:C_GFFN + kc + 1]),
                      reads=[b_psT[pb]], writes=[b_h2T[s][kc0]])
                S.add("act", lambda e, kc=kc1, pb=pb: e.activation(out=h2T[s][:, kc, :], in_=psT[pb][:, 512:512 + TC],
                                                                   func=AF.Copy, scale=cols[:, C_GFFN + kc:C_GFFN + kc + 1]),
                      reads=[b_psT[pb]], writes=[b_h2T[s][kc1]])

        def gateup(t, ft):
            s = t % 2
            bi = next_bank()
            fs = slice(ft * 128, (ft + 1) * 128)
            for kc in range(8):
                S.add("pe", lambda e, kc=kc: e.matmul(M[bi][:, 0:TC], lhsT=Wg[:, kc, fs], rhs=h2T[s][:, kc, :],
                                                       start=(kc == 0), stop=(kc == 7)),
                      reads=[b_Wg[ft // 2], b_h2T[s][kc]], writes=[b_M[bi]])
            for kc in range(8):
                S.add("pe", lambda e, kc=kc: e.matmul(M[bi][:, TC:2 * TC], lhsT=Wu[:, kc, fs], rhs=h2T[s][:, kc, :],
                                                       start=(kc == 0), stop=(kc == 7)),
                      reads=[b_Wu[ft // 2], b_h2T[s][kc]], writes=[b_M[bi]])
            gi = grot[0] % 2
            grot[0] += 1
            S.add("act", lambda e: e.activation(out=sg[gi][:, :], in_=M[bi][:, 0:TC], func=AF.Silu),
                  reads=[b_M[bi]], writes=[b_sg[gi]])
            S.add("dve", lambda e: e.tensor_tensor(out=actT[:, ft, :], in0=M[bi][:, TC:2 * TC], in1=sg[gi][:, :], op=ALU.mult),
                  reads=[b_M[bi], b_sg[gi]], writes=[b_actT[ft]])

        def down(t):
            s = t % 2
            a = t * TC
            for blk in range(TCB):
                for half in range(2):
                    bi = next_bank()
                    hs = slice(half * 512, (half + 1) * 512)
                    for fc in range(NFT):
                        S.add("pe", lambda e, fc=fc, blk=blk, hs=hs, bi=bi: e.matmul(
                            M[bi][:, :], lhsT=actT[:, fc, blk * 128:(blk + 1) * 128], rhs=Wd[:, fc, hs],
                            start=(fc == 0), stop=(fc == NFT - 1)),
                            reads=[b_actT[fc], b_Wd[fc]], writes=[b_M[bi]])
                    S.add("dve", lambda e, blk=blk, hs=hs, bi=bi: e.tensor_tensor(out=xbuf[s][:, blk, hs], in0=M[bi][:, :],
                                                                                   in1=xbuf[s][:, blk, hs], op=ALU.add),
                          reads=[b_M[bi], b_x[s][blk]], writes=[b_x[s][blk]])
                dst = out[a + blk * 128:a + (blk + 1) * 128, :]
                S.add("pool", lambda e, blk=blk, dst=dst: e.dma_start(out=dst, in_=xbuf[s][:, blk, :]),
                      reads=[b_x[s][blk]], dma=True)

        loads(0)
        weight_loads()
        loads(1)
        wout_part(0)
        transposes(0)
        for t in range(NT):
            for ft in range(NFT):
                gateup(t, ft)
                if ft == 7 and t + 1 < NT:
                    wout_part(t + 1)
                if ft == 15 and t + 1 < NT:
                    transposes(t + 1)
            down(t)
            if t + 2 < NT:
                loads(t + 2)
        S.flush(final=True)


_NC_CACHE = {}


def _host_consts(g_mix, conv_w, g_q, g_k, g_conv_out, g_attn_out, g_ffn):
    cols = np.zeros((128, C_N), np.float32)
    cols[:, C_GMIX:C_GMIX + 8] = g_mix.reshape(8, 128).T
    cols[:, C_GFFN:C_GFFN + 8] = g_ffn.reshape(8, 128).T
    cols[:, C_GCO:C_GCO + 4] = g_conv_out.reshape(4, 128).T
    cw = conv_w.reshape(3, 4, 128)
    cols[:, C_CW:C_CW + 12] = np.transpose(cw, (2, 1, 0)).reshape(128, 12)
    cols[:, C_GQ] = np.tile(g_q.reshape(64), 2)
    cols[:, C_GK] = np.tile(g_k.reshape(64), 2)
    gao = g_attn_out.reshape(8, 64).T
    cols[0:64, C_GAO:C_GAO + 8] = gao
    cols[64:128, C_GAO:C_GAO + 8] = gao
    return cols


def _static_consts():
    ident = np.eye(128, dtype=np.float32)
    k = np.arange(128)[:, None]
    q = np.arange(128)[None, :]
    U = (k >= q).astype(np.float32)
    L = (k <= q).astype(np.float32)
    masks = np.concatenate([L, U, L, U, U, U, U, U, L, L, L, L], axis=1)
    return ident, masks


def kernel(x, g_mix, w_in, conv_w, g_q, g_k, g_conv_out, g_attn_out, w_out, g_ffn, w_gate, w_up, w_down,
           _debug=False, _cores=None, _phases="abc"):
    x = np.asarray(x, np.float32)
    f = lambda a: np.ascontiguousarray(np.asarray(a, np.float32))
    cols = _host_consts(f(g_mix), f(conv_w), f(g_q), f(g_k), f(g_conv_out), f(g_attn_out), f(g_ffn))
    ident, masks = _static_consts()
    key = (bool(_debug), _phases)
    if key not in _NC_CACHE:
        _NC_CACHE[key] = build(debug=_debug, phases=_phases)
    nc = _NC_CACHE[key]
    cores = list(range(NCORES)) if _cores is None else _cores
    shared = {"w_in": f(w_in)[0], "w_out": f(w_out)[0], "w_gate": f(w_gate)[0], "w_up": f(w_up)[0],
              "w_down": f(w_down)[0], "cols": cols, "ident": ident, "masks": masks}
    in_maps = []
    for b in cores:
        m = dict(shared)
        m["x"] = np.ascontiguousarray(x[b])
        in_maps.append(m)
    res = run_bass_kernel_spmd(nc, in_maps, core_ids=list(range(len(cores))))
    if _debug:
        return res.results
    return np.stack([r["out"] for r in res.results], axis=0).astype(np.float32)
```

```python
import numpy as np
import concourse.bass as bass
import concourse.mybir as mybir
from concourse.bass_utils import run_bass_kernel_spmd

F32 = mybir.dt.float32
BF16 = mybir.dt.bfloat16
AF = mybir.ActivationFunctionType
ALU = mybir.AluOpType

S_LEN = 8192
D = 1024
D_IN = 3072
D_FF = 2816
NFT = D_FF // 128
EPS = 1e-6
NCORES = 8
_LIMITS = {}
VROW = 8 * 65

C_GMIX, C_GFFN, C_GCO, C_CW, C_GQ, C_GK, C_GAO, C_N = 0, 8, 16, 20, 32, 33, 34, 42


class Buf:
    __slots__ = ("w", "r", "name")

    def __init__(self, name=""):
        self.w = None
        self.r = []
        self.name = name


class Op:
    __slots__ = ("eng", "fn", "deps", "sig", "val", "dma", "sem", "prev")


class Sched:
    ENGS = ("pe", "act", "dve", "pool", "sp")

    def __init__(self, nc, sems, dsems):
        self.nc = nc
        self.sem = sems
        self.dsem = dsems
        self.cnt = {e: 0 for e in sems}
        self.dcnt = {q: [0] * len(l) for q, l in dsems.items()}
        self.dnext = {q: 0 for q in dsems}
        self.waited = {e: {} for e in self.ENGS}
        self.ops = {e: [] for e in self.ENGS}
        self.bufs = []
        self.barrier_tokens = []

    def buf(self, name=""):
        b = Buf(name)
        self.bufs.append(b)
        return b

    def add(self, eng, fn, reads=(), writes=(), dma=False):
        o = Op()
        o.eng, o.fn, o.dma, o.sig, o.val, o.sem, o.prev = eng, fn, dma, False, None, None, 0
        deps = []
        seen = set()

        def adddep(d):
            if d is None or id(d) in seen:
                return
            seen.add(id(d))
            if (not d.dma) and (not dma) and d.eng == eng and eng == "pe":
                return
            deps.append(d)

        for b in reads:
            adddep(b.w)
        for b in writes:
            adddep(b.w)
            for r in b.r:
                adddep(r)
        o.deps = deps
        for d in deps:
            d.sig = True
        for b in reads:
            b.r.append(o)
        for b in writes:
            b.w = o
            b.r = []
        self.ops[eng].append(o)
        return o

    def flush(self, final=False):
        nc = self.nc
        tokens = []
        for e in self.ENGS:
            lastc = None
            for o in self.ops[e]:
                if not o.dma:
                    lastc = o
            if lastc is not None:
                lastc.sig = True
        for e in self.ENGS:
            for o in self.ops[e]:
                if o.dma:
                    k = self.dnext[e]
                    n = len(self.dsem[e])
                    i = k % n
                    o.sem = self.dsem[e][i]
                    o.prev = self.dcnt[e][i]
                    self.dcnt[e][i] += 16
                    o.val = self.dcnt[e][i]
                    self.dnext[e] = k + 1
                elif o.sig:
                    self.cnt[e] += 1
                    o.val = self.cnt[e]
                    o.sem = self.sem[e]
        for e in self.sem:
            if self.cnt[e] > 0:
                tokens.append((self.sem[e], self.cnt[e]))
        for q in self.dsem:
            for i, s in enumerate(self.dsem[q]):
                if self.dcnt[q][i] > 0:
                    tokens.append((s, self.dcnt[q][i]))

        def run(engname, e):
            W = self.waited[engname]
            for o in self.ops[engname]:
                for d in o.deps:
                    if W.get(id(d.sem), 0) < d.val:
                        e.wait_ge(d.sem, d.val)
                        W[id(d.sem)] = d.val
                if o.dma:
                    if o.prev > 0 and W.get(id(o.sem), 0) < o.prev:
                        e.wait_ge(o.sem, o.prev)
                        W[id(o.sem)] = o.prev
                    o.fn(e).then_inc(o.sem, 16)
                else:
                    ins = o.fn(e)
                    if o.sig:
                        ins.then_inc(o.sem, 1)
            for (s, v) in tokens:
                if W.get(id(s), 0) < v:
                    e.wait_ge(s, v)
                    W[id(s)] = v

        with nc.Block() as block:
            @block.tensor
            def _(e):
                run("pe", e)

            @block.scalar
            def _(e):
                run("act", e)

            @block.vector
            def _(e):
                run("dve", e)

            @block.gpsimd
            def _(e):
                run("pool", e)

            @block.sync
            def _(e):
                run("sp", e)

        self.ops = {e: [] for e in self.ENGS}
        for b in self.bufs:
            b.w = None
            b.r = []
        self.bufs = []


def build(debug=False, phases="abc"):
    nc = bass.Bass("TRN2", target_bir_lowering=False)
    x = nc.dram_tensor("x", [S_LEN, D], F32, kind="ExternalInput").ap()
    w_in = nc.dram_tensor("w_in", [D, D_IN], F32, kind="ExternalInput").ap()
    w_out = nc.dram_tensor("w_out", [D, D], F32, kind="ExternalInput").ap()
    w_gate = nc.dram_tensor("w_gate", [D, D_FF], F32, kind="ExternalInput").ap()
    w_up = nc.dram_tensor("w_up", [D, D_FF], F32, kind="ExternalInput").ap()
    w_down = nc.dram_tensor("w_down", [D_FF, D], F32, kind="ExternalInput").ap()
    cols_d = nc.dram_tensor("cols", [128, C_N], F32, kind="ExternalInput").ap()
    ident_d = nc.dram_tensor("ident", [128, 128], F32, kind="ExternalInput").ap()
    mask_d = nc.dram_tensor("masks", [128, 1536], F32, kind="ExternalInput").ap()
    out = nc.dram_tensor("out", [S_LEN, D], F32, kind="ExternalOutput").ap()
    sk = "ExternalOutput" if debug else "Internal"
    qkT_d = nc.dram_tensor("qkT_d", [8, 128, S_LEN], BF16, kind=sk).ap()
    V_d = nc.dram_tensor("V_d", [S_LEN, VROW], BF16, kind=sk).ap()
    mixT_d = nc.dram_tensor("mixT_d", [8, 128, S_LEN], BF16, kind=sk).ap()

    wo_b = nc.dram_tensor("wo_b", [D, D], BF16, kind="Internal").ap()
    wg_b = nc.dram_tensor("wg_b", [D, D_FF], BF16, kind="Internal").ap()
    wu_b = nc.dram_tensor("wu_b", [D, D_FF], BF16, kind="Internal").ap()
    wd_b = nc.dram_tensor("wd_b", [D_FF, D], BF16, kind="Internal").ap()
    wscr = (w_out, w_gate, w_up, w_down, wo_b, wg_b, wu_b, wd_b)

    from contextlib import ExitStack

    with ExitStack() as gstack:
        def sem(name):
            return gstack.enter_context(nc.semaphore(name))

        sems = {e: sem("s_" + e) for e in ("pe", "act", "dve", "pool")}
        dsems = {"sp": [sem("d_sp%d" % i) for i in range(8)],
                 "pool": [sem("d_pl%d" % i) for i in range(8)]}
        S = Sched(nc, sems, dsems)

        def gsb(name, shape, dt):
            return gstack.enter_context(nc.sbuf_tensor(name, shape, dt))

        cols = gsb("cols_sb", [128, C_N], F32)
        gq8 = gsb("gq8", [128, 1], F32)
        ident = gsb("ident_bf", [128, 128], BF16)
        masks = gsb("masks_bf", [128, 1536], BF16)
        ones_bf = gsb("ones_bf", [128, 128], BF16)
        bones_bf = gsb("bones_bf", [128, 128], BF16)
        ones_f = gsb("ones_f", [128, 64], F32)

        with ExitStack() as st:
            stage = st.enter_context(nc.sbuf_tensor("stage", [128, 1536 + 128], F32))
            b_cols, b_stage = S.buf(), S.buf()
            b_c = S.buf()
            S.add("sp", lambda e: e.dma_start(out=cols[:, :], in_=cols_d), writes=[b_cols], dma=True)
            S.add("sp", lambda e: e.dma_start(out=stage[:, 0:1536], in_=mask_d), writes=[b_stage], dma=True)
            S.add("sp", lambda e: e.dma_start(out=stage[:, 1536:1664], in_=ident_d), writes=[b_stage], dma=True)
            S.add("dve", lambda e: e.tensor_copy(out=masks[:, :], in_=stage[:, 0:1536]), reads=[b_stage], writes=[b_c])
            S.add("dve", lambda e: e.tensor_copy(out=ident[:, :], in_=stage[:, 1536:1664]), reads=[b_stage], writes=[b_c])
            S.add("dve", lambda e: e.tensor_scalar(out=gq8[:, :], in0=cols[:, C_GQ:C_GQ + 1], scalar1=0.125,
                                                     scalar2=None, op0=ALU.mult), reads=[b_cols], writes=[b_c])
            S.add("pool", lambda e: e.memset(ones_bf[:, :], 1.0), writes=[b_c])
            S.add("pool", lambda e: e.memset(bones_bf[:, :], 0.0), writes=[b_c])
            S.add("pool", lambda e: e.memset(bones_bf[0:64, 0:64], 1.0), writes=[b_c])
            S.add("pool", lambda e: e.memset(bones_bf[64:128, 64:128], 1.0), writes=[b_c])
            S.add("pool", lambda e: e.memset(ones_f[:, :], 1.0), writes=[b_c])
            S.flush()

        if "a" in phases:
            phase_a(nc, S, x, w_in, qkT_d, V_d, mixT_d, cols, gq8, ident, ones_bf, bones_bf,
                    wscr if "c" in phases else None)
        if "b" in phases:
            phase_b(nc, S, qkT_d, V_d, mixT_d, cols, masks, ones_bf, ones_f, None)
        if "c" in phases:
            phase_c(nc, S, x, out, wo_b, wg_b, wu_b, wd_b, mixT_d, cols, ident,
                    precast=None if "a" in phases else wscr)
    return nc


def phase_a(nc, S, x, w_in, qkT_d, V_d, mixT_d, cols, gq8, ident, ones_bf, bones_bf, wscr=None):
    NT = _LIMITS.get('a', S_LEN // 512)
    with ExitStack_() as st:
        def sb(name, shape, dt):
            return st.enter_context(nc.sbuf_tensor(name, shape, dt))

        def ps(name, shape, dt=F32):
            return st.enter_context(nc.psum_tensor(name, shape, dt))

        Win = sb("Win", [128, 8, D_IN], BF16)
        xbuf = [sb("xbuf%d" % i, [128, 4, 1024], F32) for i in range(2)]
        junk = sb("junk", [128, 1024], BF16)
        ss = sb("ss", [128, 2, 4], F32)
        rstd = sb("rstd", [128, 2, 4], F32)
        h = sb("h", [128, 4, 1024], BF16)
        hT = [sb("hT%d" % i, [128, 8, 512], BF16) for i in range(2)]
        qko = [sb("qko%d" % i, [128, 8, 512], BF16) for i in range(2)]
        vst = [sb("vst%d" % i, [128, 4, VROW], BF16) for i in range(2)]
        mixo = [sb("mixo%d" % i, [128, 4, 512], BF16) for i in range(2)]
        usb = [sb("usb%d" % i, [128, 512], F32) for i in range(4)]
        gbsb = [sb("gbsb%d" % i, [128, 512], F32) for i in range(4)]
        pbuf = [sb("pbuf%d" % i, [128, 514], F32) for i in range(4)]
        cbuf = [sb("cbuf%d" % i, [128, 512], F32) for i in range(4)]
        ybuf = [sb("ybuf%d" % i, [128, 512], F32) for i in range(4)]
        NR = 3
        sqt = [sb("sqt%d" % i, [128, 512], BF16) for i in range(NR)]
        rst = [sb("rst%d" % i, [128, 512], F32) for i in range(NR)]
        rsc = sb("rsc", [128, 512], F32)

        psT = [ps("psT%d" % i, [128, 1024], BF16) for i in range(2)]
        SSqk = ps("SSqk", [128, 512])
        SSc = SSqk
        NM = 5
        M = [ps("M%d" % i, [128, 512]) for i in range(NM)]
        sqc = [sb("sqc%d" % i, [128, 512], BF16) for i in range(4)]

        B = S.buf
        b_Win = [B() for _ in range(8)]
        b_x = [[B() for _ in range(4)] for _ in range(2)]
        b_junk, b_ss, b_rstd = B(), [B(), B()], [B(), B()]
        b_h = [B() for _ in range(4)]
        b_hT = [[B() for _ in range(8)] for _ in range(2)]
        b_qko, b_vst, b_mixo = [B(), B()], [B(), B()], [B(), B()]
        b_usb, b_gbsb, b_p, b_c, b_y = ([B() for _ in range(4)] for _ in range(5))
        b_usb, b_gbsb, b_p, b_c, b_y = list(b_usb), list(b_gbsb), list(b_p), list(b_c), list(b_y)
        b_sq, b_rs = [B() for _ in range(NR)], [B() for _ in range(NR)]
        b_rsc = B()
        b_psT = [B(), B()]
        b_SSqk = B()
        b_SSc = b_SSqk
        b_sqc = [B() for _ in range(4)]
        b_M = [B() for _ in range(NM)]
        b_init = B()

        w_in_v = w_in.rearrange("(k p) n -> p k n", p=128)
        b_Win = [B() for _ in range(24)]
        order = []
        for f in range(4):
            order += [f, 8 + f, 4 + f]
        order += list(range(12, 24))
        for g in order:
            S.add("pool", lambda e, g=g: e.dma_start(out=Win[:, :, g * 128:(g + 1) * 128], in_=w_in_v[:, :, g * 128:(g + 1) * 128]),
                  writes=[b_Win[g]], dma=True)
        for s in range(2):
            S.add("pool", lambda e, s=s: e.memset(vst[s][:, :, :], 1.0), writes=[b_vst[s]])
        for f in range(4):
            S.add("pool", lambda e, f=f: e.memset(pbuf[f][:, 0:2], 0.0), writes=[b_p[f]])

        mrot = [0]
        rrot = [0]

        def next_bank():
            i = mrot[0] % NM
            mrot[0] += 1
            return i

        def load_x(t):
            s = t % 2
            src = x[t * 512:(t + 1) * 512, :].rearrange("(b p) d -> p b d", p=128)
            S.add("sp", lambda e: e.dma_start(out=xbuf[s][:, :, :], in_=src), writes=b_x[s], dma=True)

        def stats_steps(t):
            s = t % 2
            st_ = []
            for blk in range(4):
                st_.append(lambda blk=blk: S.add("act", lambda e: e.activation(out=junk[:, :], in_=xbuf[s][:, blk, :], func=AF.Square,
                                                                                accum_out=ss[:, s, blk:blk + 1]),
                                                 reads=[b_x[s][blk]], writes=[b_junk, b_ss[s]]))
            st_.append(lambda: S.add("act", lambda e: e.activation(out=rstd[:, s, :], in_=ss[:, s, :], func=AF.Ln, scale=1.0 / D, bias=EPS),
                                     reads=[b_ss[s]], writes=[b_rstd[s]]))
            st_.append(lambda: S.add("act", lambda e: e.activation(out=rstd[:, s, :], in_=rstd[:, s, :], func=AF.Exp, scale=-0.5),
                                     reads=[b_rstd[s]], writes=[b_rstd[s]]))
            for blk in range(4):
                if blk % 2 == 0:
                    st_.append(lambda blk=blk: S.add("act", lambda e: e.activation(out=h[:, blk, :], in_=xbuf[s][:, blk, :], func=AF.Copy,
                                                                                    scale=rstd[:, s, blk:blk + 1]),
                                                     reads=[b_x[s][blk], b_rstd[s]], writes=[b_h[blk]]))
                else:
                    st_.append(lambda blk=blk: S.add("dve", lambda e: e.tensor_scalar(out=h[:, blk, :], in0=xbuf[s][:, blk, :],
                                                                                       scalar1=rstd[:, s, blk:blk + 1], scalar2=None,
                                                                                       op0=ALU.mult),
                                                     reads=[b_x[s][blk], b_rstd[s]], writes=[b_h[blk]]))
            return st_

        def transpose_steps(t):
            s = t % 2
            st_ = []
            for kp in range(4):
                def rnd(kp=kp):
                    pb = kp % 2
                    for kk in range(2):
                        kc = 2 * kp + kk
                        for blk in range(4):
                            S.add("pe", lambda e, kc=kc, blk=blk, pb=pb, kk=kk: e.transpose(
                                out=psT[pb][:, kk * 512 + blk * 128: kk * 512 + (blk + 1) * 128],
                                in_=h[:, blk, kc * 128:(kc + 1) * 128], identity=ident[:, :]),
                                reads=[b_h[blk]], writes=[b_psT[pb]])
                    kc0, kc1 = 2 * kp, 2 * kp + 1
                    S.add("act", lambda e, kc=kc0, pb=pb: e.activation(out=hT[s][:, kc, :], in_=psT[pb][:, 0:512],
                                                                       func=AF.Copy, scale=cols[:, C_GMIX + kc:C_GMIX + kc + 1]),
                          reads=[b_psT[pb]], writes=[b_hT[s][kc0]])
                    S.add("act", lambda e, kc=kc1, pb=pb: e.activation(out=hT[s][:, kc, :], in_=psT[pb][:, 512:1024],
                                                                       func=AF.Copy, scale=cols[:, C_GMIX + kc:C_GMIX + kc + 1]),
                          reads=[b_psT[pb]], writes=[b_hT[s][kc1]])
                st_.append(rnd)
            return st_

        asteps = []

        def pop_astep():
            if asteps:
                asteps.pop(0)()

        def group_fm(t, col0):
            s = t % 2
            bi = next_bank()
            for kc in range(8):
                S.add("pe", lambda e, kc=kc: e.matmul(M[bi][:, :], lhsT=Win[:, kc, col0:col0 + 128], rhs=hT[s][:, kc, :],
                                                       start=(kc == 0), stop=(kc == 7)),
                      reads=[b_Win[col0 // 128], b_hT[s][kc]], writes=[b_M[bi]])
            pop_astep()
            return bi

        pending = []

        def run_pending():
            while pending:
                pending.pop(0)()

        def conv_part(t, f):
            s = t % 2
            bu = group_fm(t, f * 128)
            S.add("act", lambda e: e.activation(out=usb[f][:, :], in_=M[bu][:, :], func=AF.Copy),
                  reads=[b_M[bu]], writes=[b_usb[f]])
            run_pending()
            bc = group_fm(t, 1024 + f * 128)
            S.add("dve", lambda e: e.tensor_tensor(out=pbuf[f][:, 2:514], in0=M[bc][:, :], in1=usb[f][:, :], op=ALU.mult),
                  reads=[b_M[bc], b_usb[f]], writes=[b_p[f]])
            bg = group_fm(t, 512 + f * 128)
            S.add("act", lambda e: e.activation(out=gbsb[f][:, :], in_=M[bg][:, :], func=AF.Copy),
                  reads=[b_M[bg]], writes=[b_gbsb[f]])
            cw = C_CW + 3 * f
            S.add("act", lambda e: e.activation(out=cbuf[f][:, :], in_=pbuf[f][:, 0:512], func=AF.Copy,
                                                scale=cols[:, cw:cw + 1]),
                  reads=[b_p[f]], writes=[b_c[f]])
            S.add("dve", lambda e: e.scalar_tensor_tensor(out=cbuf[f][:, :], in0=pbuf[f][:, 1:513], scalar=cols[:, cw + 1:cw + 2],
                                                           in1=cbuf[f][:, :], op0=ALU.mult, op1=ALU.add),
                  reads=[b_p[f], b_c[f]], writes=[b_c[f]])
            S.add("dve", lambda e: e.scalar_tensor_tensor(out=cbuf[f][:, :], in0=pbuf[f][:, 2:514], scalar=cols[:, cw + 2:cw + 3],
                                                           in1=cbuf[f][:, :], op0=ALU.mult, op1=ALU.add),
                  reads=[b_p[f], b_c[f]], writes=[b_c[f]])
            S.add("act", lambda e: e.activation(out=pbuf[f][:, 0:2], in_=pbuf[f][:, 512:514], func=AF.Copy),
                  reads=[b_p[f]], writes=[b_p[f]])
            S.add("dve", lambda e: e.tensor_tensor(out=ybuf[f][:, :], in0=cbuf[f][:, :], in1=gbsb[f][:, :], op=ALU.mult),
                  reads=[b_c[f], b_gbsb[f]], writes=[b_y[f]])
            S.add("act", lambda e: e.activation(out=sqc[f][:, :], in_=ybuf[f][:, :], func=AF.Square),
                  reads=[b_y[f]], writes=[b_sqc[f]])

            def part2():
                if f == 3:
                    for ff in range(4):
                        S.add("pe", lambda e, ff=ff: e.matmul(SSc[:, :], lhsT=ones_bf[:, :], rhs=sqc[ff][:, :],
                                                               start=(ff == 0), stop=(ff == 3)),
                              reads=[b_sqc[ff]], writes=[b_SSc])
                    S.add("act", lambda e: e.activation(out=rsc[:, :], in_=SSc[:, :], func=AF.Ln, scale=1.0 / 512, bias=EPS),
                          reads=[b_SSc], writes=[b_rsc])
                    S.add("act", lambda e: e.activation(out=rsc[:, :], in_=rsc[:, :], func=AF.Exp, scale=-0.5),
                          reads=[b_rsc], writes=[b_rsc])
                    for ff in range(4):
                        S.add("dve", lambda e, ff=ff: e.scalar_tensor_tensor(
                            out=mixo[s][:, ff, :], in0=ybuf[ff][:, :], scalar=cols[:, C_GCO + ff:C_GCO + ff + 1],
                            in1=rsc[:, :], op0=ALU.mult, op1=ALU.mult),
                            reads=[b_y[ff], b_rsc], writes=[b_mixo[s]])
                    dst = mixT_d[0:4, :, t * 512:(t + 1) * 512].rearrange("f p t -> p f t")
                    S.add("pool", lambda e: e.dma_start(out=dst, in_=mixo[s][:, :, :]), reads=[b_mixo[s]], dma=True)
            pending.append(part2)

        def qk_part(t, ft):
            s = t % 2
            bi = group_fm(t, 1536 + ft * 128)
            ri = rrot[0] % NR
            rrot[0] += 1
            S.add("act", lambda e: e.activation(out=sqt[ri][:, :], in_=M[bi][:, :], func=AF.Square),
                  reads=[b_M[bi]], writes=[b_sq[ri]])
            run_pending()
            gcol = gq8[:, 0:1] if ft < 4 else cols[:, C_GK:C_GK + 1]

            def part2():
                S.add("pe", lambda e: e.matmul(SSqk[:, :], lhsT=bones_bf[:, :], rhs=sqt[ri][:, :], start=True, stop=True),
                      reads=[b_sq[ri]], writes=[b_SSqk])
                S.add("act", lambda e: e.activation(out=rst[ri][:, :], in_=SSqk[:, :], func=AF.Ln, scale=1.0 / 64, bias=EPS),
                      reads=[b_SSqk], writes=[b_rs[ri]])
                S.add("act", lambda e: e.activation(out=rst[ri][:, :], in_=rst[ri][:, :], func=AF.Exp, scale=-0.5),
                      reads=[b_rs[ri]], writes=[b_rs[ri]])
                S.add("dve", lambda e: e.scalar_tensor_tensor(out=qko[s][:, ft, :], in0=M[bi][:, :], scalar=gcol,
                                                              in1=rst[ri][:, :], op0=ALU.mult, op1=ALU.mult),
                      reads=[b_M[bi], b_rs[ri]], writes=[b_qko[s]])
                if ft == 7:
                    dst = qkT_d[:, :, t * 512:(t + 1) * 512].rearrange("j p t -> p j t")
                    S.add("pool", lambda e: e.dma_start(out=dst, in_=qko[s][:, :, :]), reads=[b_qko[s]], dma=True)
            pending.append(part2)

        def v_part(t, blk):
            s = t % 2
            bi = next_bank()
            for kc in range(8):
                S.add("pe", lambda e, kc=kc: e.matmul(M[bi][:, :], lhsT=hT[s][:, kc, blk * 128:(blk + 1) * 128],
                                                       rhs=Win[:, kc, 2560:3072], start=(kc == 0), stop=(kc == 7)),
                      reads=[b_Win[20], b_Win[21], b_Win[22], b_Win[23], b_hT[s][kc]], writes=[b_M[bi]])
            pop_astep()
            dstv = vst[s][:, blk, :].rearrange("p (h e) -> p h e", e=65)[:, :, 0:64]
            srcv = M[bi][:, :].rearrange("p (h d) -> p h d", d=64)
            S.add("act", lambda e: e.activation(out=dstv, in_=srcv, func=AF.Copy), reads=[b_M[bi]], writes=[b_vst[s]])
            run_pending()
            if blk == 3:
                dst = V_d[t * 512:(t + 1) * 512, :].rearrange("(b p) e -> p b e", p=128)
                S.add("pool", lambda e: e.dma_start(out=dst, in_=vst[s][:, :, :]), reads=[b_vst[s]], dma=True)

        pc_list = precast_list(wscr) if wscr is not None else []
        load_x(0)
        if NT > 1:
            load_x(1)
        for f_ in stats_steps(0) + transpose_steps(0):
            f_()
        for t in range(NT):
            if t + 1 < NT:
                asteps.extend(stats_steps(t + 1) + transpose_steps(t + 1))
            for f in range(4):
                conv_part(t, f)
            for ft in range(8):
                qk_part(t, ft)
            for blk in range(4):
                v_part(t, blk)
            run_pending()
            while asteps:
                asteps.pop(0)()
            if t + 2 < NT:
                load_x(t + 2)
            if wscr is not None and t >= 1:
                emit_precast(S, None, n=2, lst=pc_list)
        if wscr is not None:
            emit_precast(S, None, lst=pc_list)
        S.flush()


def ExitStack_():
    from contextlib import ExitStack
    return ExitStack()


def precast_list(wscr):
    w_out, w_gate, w_up, w_down, wo_b, wg_b, wu_b, wd_b = wscr
    lst = []
    for (src_, dst_, rows) in ((w_out, wo_b, D), (w_gate, wg_b, D), (w_up, wu_b, D), (w_down, wd_b, D_FF)):
        step = 256
        for r in range(0, rows, step):
            lst.append((dst_[r:r + step, :], src_[r:r + step, :]))
    return lst


def emit_precast(S, wscr, n=None, lst=None):
    lst = precast_list(wscr) if lst is None else lst
    k = 0
    while lst and (n is None or k < n):
        d_, s_ = lst.pop(0)
        S.add("pool", lambda e, d_=d_, s_=s_: e.dma_start(out=d_, in_=s_), dma=True)
        k += 1


def phase_b(nc, S, qkT_d, V_d, mixT_d, cols, masks, ones_bf, ones_f, wscr=None):
    NCH = _LIMITS.get('b', S_LEN // 2048)
    with ExitStack_() as st:
        def sb(name, shape, dt):
            return st.enter_context(nc.sbuf_tensor(name, shape, dt))

        def ps(name, shape, dt=F32):
            return st.enter_context(nc.psum_tensor(name, shape, dt))

        qj = [sb("qz%d" % i, [128, 2, 2048], BF16) for i in range(2)]
        kj = [sb("kj%d" % i, [128, 2, 2048], BF16) for i in range(2)]
        vnat = sb("vnat", [128, 17, VROW], BF16)
        vd4 = sb("vd4", [128, 20, VROW], BF16)
        vd16 = [sb("vd16_%d" % i, [128, 16, VROW], BF16) for i in range(2)]
        NE = 3
        Et = [sb("E%d" % i, [128, 512], BF16) for i in range(NE)]
        num = sb("num", [128, 2048], F32)
        yT = sb("yT", [64, 8, 2048], F32)
        bcs = sb("bcs", [64, 2048], F32)
        ysqacc = sb("ysqacc", [64, 2048], F32)
        rsa = sb("rsa", [64, 512], F32)
        mao = sb("mao", [64, 8, 512], BF16)

        acc = ps("acc", [128, 2048])
        NST = 3
        ST = [ps("ST%d" % i, [128, 512]) for i in range(NST)]
        SSa = [ps("SSa%d" % i, [128, 512]) for i in range(1)]

        B = S.buf
        b_qj, b_kj = [B(), B()], [B(), B()]
        b_vnat, b_vd4, b_vd16 = B(), B(), [B(), B()]
        b_E = [B() for _ in range(NE)]
        b_num, b_yT = B(), [B() for _ in range(8)]
        b_bcs, b_ysqacc, b_rsa, b_mao = B(), B(), B(), B()
        b_rden = [B(), B()]
        rden_d = nc.dram_tensor("rden_d", [2, 2048], F32, kind="Internal").ap()
        b_acc, b_ST, b_SSa = [B() for _ in range(4)], [B() for _ in range(NST)], [B()]
        pending_epi = []
        steps = []
        urgent = []

        erot = [0]
        srot = [0]
        prot = [0]
        for s_ in range(2):
            S.add("pool", lambda e, s_=s_: e.memset(qj[s_][64:128, 0, :], 0.0), writes=[b_qj[s_]])
            S.add("pool", lambda e, s_=s_: e.memset(qj[s_][0:64, 1, :], 0.0), writes=[b_qj[s_]])

        def sl(a, n, step):
            return slice(a, a + (n - 1) * step + 1, step)

        def load_pair(c, j):
            s = prot[0] % 2
            prot[0] += 1
            t0 = c * 2048
            S.add("sp", lambda e: e.dma_start(out=qj[s][0:64, 0, :], in_=qkT_d[j, 0:64, t0:t0 + 2048]), writes=[b_qj[s]], dma=True)
            S.add("sp", lambda e: e.dma_start(out=qj[s][64:128, 1, :], in_=qkT_d[j, 64:128, t0:t0 + 2048]), writes=[b_qj[s]], dma=True)
            if c > 0:
                S.add("sp", lambda e: e.dma_start(out=kj[s][:, :, :],
                                                  in_=qkT_d[4 + j, :, t0 - 2048:t0 + 2048].rearrange("p (c t) -> p c t", c=2)),
                      writes=[b_kj[s]], dma=True)
            else:
                S.add("sp", lambda e: e.dma_start(out=kj[s][:, 1, :], in_=qkT_d[4 + j, :, t0:t0 + 2048]),
                      writes=[b_kj[s]], dma=True)
            return s

        for c in range(NCH):
            t0 = c * 2048
            cp = c % 2
            pp = (c - 1) % 2
            ps_first = load_pair(c, 0)
            if c == 0:
                S.add("sp", lambda e: e.dma_start(out=vnat[:, 1:17, :], in_=V_d[0:2048, :].rearrange("(b p) e -> p b e", p=128)),
                      writes=[b_vnat], dma=True)
                S.add("sp", lambda e: e.dma_start(out=vd4[:, 4:20, :].rearrange("p (t r) e -> p t r e", r=4),
                                                  in_=V_d[0:2048, :].rearrange("(t p r) e -> p t r e", p=128, r=4)),
                      writes=[b_vd4], dma=True)
            else:
                S.add("sp", lambda e, t0=t0: e.dma_start(out=vnat[:, 0:17, :],
                                                         in_=V_d[t0 - 128:t0 + 2048, :].rearrange("(b p) e -> p b e", p=128)),
                      writes=[b_vnat], dma=True)
                S.add("sp", lambda e, t0=t0: e.dma_start(out=vd4[:, 0:20, :].rearrange("p (t r) e -> p t r e", r=4),
                                                         in_=V_d[t0 - 512:t0 + 2048, :].rearrange("(t p r) e -> p t r e", p=128, r=4)),
                      writes=[b_vd4], dma=True)
            S.add("sp", lambda e, t0=t0, cp=cp: e.dma_start(out=vd16[cp][:, :, :],
                                                            in_=V_d[t0:t0 + 2048, :].rearrange("(p r) e -> p r e", r=16)),
                  writes=[b_vd16[cp]], dma=True)

            items = []
            if c > 0:
                items.append((1, 1, 0, sl(1920, 128, 1), vnat, b_vnat, 0, [(0, "U")]))
            for b in range(16):
                qb = [(128 * b, "L")]
                if b < 15:
                    qb.append((128 * (b + 1), "U"))
                items.append((1, 1, 1, sl(128 * b, 128, 1), vnat, b_vnat, b + 1, qb))
            if c > 0:
                for r in range(4):
                    items.append((4, 4, 0, sl(1536 + r, 128, 4), vd4, b_vd4, r, [(r, "U")]))
            for t in range(4):
                for r in range(4):
                    qb = [(512 * t + r, "L")]
                    if t < 3:
                        qb.append((512 * (t + 1) + r, "U"))
                    items.append((4, 4, 1, sl(512 * t + r, 128, 4), vd4, b_vd4, 4 + 4 * t + r, qb))
            if c > 0:
                for r in range(16):
                    items.append((16, 16, 0, sl(r, 128, 16), vd16[pp], b_vd16[pp], r, [(r, "U")]))
            for r in range(16):
                items.append((16, 16, 1, sl(r, 128, 16), vd16[cp], b_vd16[cp], r, [(r, "L")]))

            def cls_of(it):
                kinds = "".join(k for (_q, k) in it[7])
                return kinds
            banks = []
            i = 0
            while i < len(items):
                cl = cls_of(items[i])
                cap = 2 if cl == "LU" else 4
                grp = [items[i]]
                jj = i + 1
                while jj < len(items) and len(grp) < cap and cls_of(items[jj]) == cl:
                    grp.append(items[jj])
                    jj += 1
                banks.append((cl, grp))
                i = jj
            nb = len(banks)

            ps_next = ps_first
            for j in range(4):
                ps_ = ps_next
                if j < 3:
                    ps_next = load_pair(c, j + 1)
                for hd in (2 * j, 2 * j + 1):
                    r0 = 64 * (hd % 2)
                    started = [False] * 4

                    def emit_st(bi_, hsel=hd % 2, ps_=ps_):
                        cl, grp = banks[bi_]
                        si = srot[0] % NST
                        srot[0] += 1
                        off = 0
                        for (dil, qstep, kc_, ks, vt, vb, vslot, qb) in grp:
                            npart = len(qb)
                            q0 = qb[0][0]
                            if npart == 1:
                                rhs = qj[ps_][:, hsel, slice(q0, q0 + 127 * qstep + 1, qstep)]
                                oap = ST[si][:, off:off + 128]
                            elif dil == 1:
                                rhs = qj[ps_][:, hsel, q0:q0 + 256]
                                oap = ST[si][:, off:off + 256]
                            else:
                                t_, r_ = q0 // 512, q0 % 512
                                rhs = qj[ps_][:, hsel, :].rearrange("p (t m r) -> p t m r", t=4, r=4)[:, t_:t_ + 2, :, r_]
                                oap = ST[si][:, off:off + 256].rearrange("p (a m) -> p a m", a=2)
                            S.add("pe", lambda e, kc_=kc_, ks=ks, rhs=rhs, oap=oap: e.matmul(
                                oap, lhsT=kj[ps_][:, kc_, ks], rhs=rhs, start=True, stop=True),
                                reads=[b_kj[ps_], b_qj[ps_]], writes=[b_ST[si]])
                            off += 128 * npart
                        return si

                    def emit_rest(bi_, si, hd=hd, started=started):
                        cl, grp = banks[bi_]
                        n = sum(len(it[7]) for it in grp) * 128
                        ei = erot[0] % NE
                        erot[0] += 1
                        moff = {"LU": 0, "U": 512, "L": 1024}[cl]
                        S.add("act", lambda e: e.activation(out=Et[ei][:, 0:n], in_=ST[si][:, 0:n], func=AF.Exp),
                              reads=[b_ST[si]], writes=[b_E[ei]])
                        S.add("dve", lambda e: e.tensor_tensor(out=Et[ei][:, 0:n], in0=Et[ei][:, 0:n],
                                                               in1=masks[:, moff:moff + n], op=ALU.mult),
                              reads=[b_E[ei]], writes=[b_E[ei]])
                        off = 0
                        for (dil, qstep, kc_, ks, vt, vb, vslot, qb) in grp:
                            vap = vt[:, vslot, hd * 65:(hd + 1) * 65]
                            segs = []
                            if dil == 16:
                                for (q0, _k) in qb:
                                    for k in range(4):
                                        segs.append((k, 512 * k + q0, off + 32 * k, 32, 16))
                                    off += 128
                            elif dil == 1 and len(qb) == 2 and (qb[0][0] // 512) == (qb[1][0] // 512):
                                segs.append((qb[0][0] // 512, qb[0][0], off, 256, 1))
                                off += 256
                            else:
                                for (q0, _k) in qb:
                                    segs.append((q0 // 512, q0, off, 128, qstep))
                                    off += 128
                            for (bk, oc, eo, nq, ostep) in segs:
                                first = not started[bk]
                                started[bk] = True
                                oslice = slice(oc, oc + (nq - 1) * ostep + 1, ostep)
                                S.add("pe", lambda e, vap=vap, eo=eo, nq=nq, oslice=oslice, first=first: e.matmul(
                                    acc[0:65, oslice], lhsT=vap, rhs=Et[ei][:, eo:eo + nq],
                                    start=first, stop=False, skip_group_check=True),
                                    reads=[vb, b_E[ei]], writes=[b_acc[bk]])

                    inflight = []
                    for bi_ in range(nb):
                        si = emit_st(bi_)
                        inflight.append((bi_, si))
                        if len(inflight) > 2:
                            emit_rest(*inflight.pop(0))
                        if urgent:
                            urgent.pop(0)()
                        if bi_ >= 2:
                            for _ in range(3):
                                if steps:
                                    f_ = steps.pop(0)
                                    if f_ is not None:
                                        f_()
                    while inflight:
                        emit_rest(*inflight.pop(0))

                    while urgent:
                        urgent.pop(0)()
                    while steps:
                        f_ = steps.pop(0)
                        if f_ is not None:
                            f_()
                    def evac(k):
                        S.add("act", lambda e: e.activation(out=num[0:65, 512 * k:512 * (k + 1)],
                                                            in_=acc[0:65, 512 * k:512 * (k + 1)], func=AF.Copy),
                              reads=[b_acc[k]], writes=[b_num])
                    evac(0)
                    for k in (1, 2, 3):
                        urgent.append(lambda k=k, evac=evac: evac(k))
                    rs_ = hd % 2

                    def mk_steps(hd=hd, rs_=rs_):
                        st_ = []
                        for k in range(4):
                            cs = slice(512 * k, 512 * (k + 1))
                            st_.append(lambda cs=cs: S.add("act", lambda e: e.activation(out=num[64:65, cs], in_=num[64:65, cs], func=AF.Ln),
                                                           reads=[b_num], writes=[b_num]))
                        for k in range(4):
                            cs = slice(512 * k, 512 * (k + 1))
                            st_.append(lambda cs=cs: S.add("act", lambda e: e.activation(out=num[64:65, cs], in_=num[64:65, cs], func=AF.Exp,
                                                                                          scale=-1.0),
                                                           reads=[b_num], writes=[b_num]))
                        st_.append(lambda: S.add("sp", lambda e: e.dma_start(out=rden_d[rs_:rs_ + 1, :], in_=num[64:65, :]),
                                                 reads=[b_num], writes=[b_rden[rs_]], dma=True))
                        st_.append(lambda: S.add("sp", lambda e: e.dma_start(out=bcs[0:64, :],
                                                                             in_=rden_d[rs_:rs_ + 1, :].partition_broadcast(64)),
                                                 reads=[b_rden[rs_]], writes=[b_bcs], dma=True))
                        st_ += [None] * 9
                        for k in range(4):
                            cs = slice(512 * k, 512 * (k + 1))
                            st_.append(lambda cs=cs: S.add("dve", lambda e: e.tensor_tensor(out=yT[:, hd, cs], in0=num[0:64, cs],
                                                                                           in1=bcs[0:64, cs], op=ALU.mult),
                                                           reads=[b_num, b_bcs], writes=[b_yT[hd]]))
                        for k in range(4):
                            cs = slice(512 * k, 512 * (k + 1))
                            if hd == 0:
                                st_.append(lambda cs=cs: S.add("act", lambda e: e.activation(out=ysqacc[:, cs], in_=yT[:, hd, cs], func=AF.Square),
                                                               reads=[b_yT[hd]], writes=[b_ysqacc]))
                            else:
                                st_.append(lambda cs=cs: S.add("act", lambda e: e.activation(out=bcs[:, cs], in_=yT[:, hd, cs], func=AF.Square),
                                                               reads=[b_yT[hd]], writes=[b_bcs]))
                        if hd != 0:
                            for k in range(4):
                                cs = slice(512 * k, 512 * (k + 1))
                                st_.append(lambda cs=cs: S.add("dve", lambda e: e.tensor_tensor(out=ysqacc[:, cs], in0=ysqacc[:, cs],
                                                                                               in1=bcs[:, cs], op=ALU.add),
                                                               reads=[b_bcs, b_ysqacc], writes=[b_ysqacc]))
                        return st_
                    steps.extend(mk_steps())

            def mk_epi(t0=t0):
                st_ = []
                for k in range(4):
                    cs = slice(512 * k, 512 * (k + 1))
                    def ssq_step(cs=cs):
                        S.add("pe", lambda e: e.matmul(SSa[0][0:64, :], lhsT=ones_f[0:64, 0:64], rhs=ysqacc[:, cs], start=True, stop=True),
                              reads=[b_ysqacc], writes=[b_SSa[0]])
                        S.add("act", lambda e: e.activation(out=rsa[:, :], in_=SSa[0][0:64, :], func=AF.Ln, scale=1.0 / 512, bias=EPS),
                              reads=[b_SSa[0]], writes=[b_rsa])
                        S.add("act", lambda e: e.activation(out=rsa[:, :], in_=rsa[:, :], func=AF.Exp, scale=-0.5),
                              reads=[b_rsa], writes=[b_rsa])
                    st_.append(ssq_step)
                    st_.append(None)
                    def stt2(h0, cs=cs):
                        for hd in (h0, h0 + 1):
                            S.add("dve", lambda e, hd=hd: e.scalar_tensor_tensor(
                                out=mao[:, hd, :], in0=yT[:, hd, cs], scalar=cols[0:64, C_GAO + hd:C_GAO + hd + 1],
                                in1=rsa[:, :], op0=ALU.mult, op1=ALU.mult),
                                reads=[b_yT[hd], b_rsa], writes=[b_mao])
                    for h0 in (0, 2, 4, 6):
                        st_.append(lambda h0=h0, stt2=stt2: stt2(h0))
                    dst = mixT_d[4:8, :, t0 + 512 * k:t0 + 512 * (k + 1)].rearrange("f (h d) t -> d (f h) t", h=2)
                    st_.append(lambda dst=dst: S.add("pool", lambda e: e.dma_start(out=dst, in_=mao[:, :, :]), reads=[b_mao], dma=True))
                return st_
            steps.extend(mk_epi())
        while urgent:
            urgent.pop(0)()
        while steps:
            f_ = steps.pop(0)
            if f_ is not None:
                f_()
        S.flush()


def phase_c(nc, S, x, out, w_out, w_gate, w_up, w_down, mixT_d, cols, ident, precast=None):
    if precast is not None:
        emit_precast(S, precast)
        S.flush()
    TCB = 2
    TC = TCB * 128
    NT = _LIMITS.get('c', S_LEN // TC)
    with ExitStack_() as st:
        def sb(name, shape, dt):
            return st.enter_context(nc.sbuf_tensor(name, shape, dt))

        def ps(name, shape, dt=F32):
            return st.enter_context(nc.psum_tensor(name, shape, dt))

        Wo = sb("Wo", [128, 8, D], BF16)
        Wg = sb("Wg", [128, 8, D_FF], BF16)
        Wu = sb("Wu", [128, 8, D_FF], BF16)
        Wd = sb("Wd", [128, NFT, D], BF16)
        xbuf = [sb("xc%d" % i, [128, TCB, 1024], F32) for i in range(2)]
        mixT = [sb("mixT%d" % i, [128, 8, TC], BF16) for i in range(2)]
        h2 = sb("h2", [128, TCB, 1024], BF16)
        h2T = [sb("h2T%d" % i, [128, 8, TC], BF16) for i in range(2)]
        actT = sb("actT", [128, NFT, TC], BF16)
        sg = [sb("sg%d" % i, [128, TC], F32) for i in range(2)]
        junk = sb("junkc", [128, 1024], BF16)
        ss = sb("ssc", [128, 2, TCB], F32)
        rstd = sb("rstdc", [128, 2, TCB], F32)

        psT = [ps("psTc%d" % i, [128, 1024], BF16) for i in range(2)]
        NM = 6
        M = [ps("Mc%d" % i, [128, 512]) for i in range(NM)]

        B = S.buf
        b_Wo = [B() for _ in range(8)]
        b_Wg = [B() for _ in range(8)]
        b_Wu = [B() for _ in range(8)]
        b_Wd = [B() for _ in range(NFT)]
        b_x = [[B() for _ in range(TCB)] for _ in range(2)]
        b_mixT = [B(), B()]
        b_h2 = [B() for _ in range(TCB)]
        b_h2T = [[B() for _ in range(8)] for _ in range(2)]
        b_actT = [B() for _ in range(NFT)]
        b_sg = [B(), B()]
        b_junk, b_ss, b_rstd = B(), [B(), B()], [B(), B()]
        b_psT = [B(), B()]
        b_M = [B() for _ in range(NM)]

        wo_v = w_out.rearrange("(k p) n -> p k n", p=128)
        wg_v = w_gate.rearrange("(k p) n -> p k n", p=128)
        wu_v = w_up.rearrange("(k p) n -> p k n", p=128)
        wd_v = w_down.rearrange("(k p) n -> p k n", p=128)
        b_Wg = [B() for _ in range(NFT // 2)]
        b_Wu = [B() for _ in range(NFT // 2)]

        def weight_loads():
            for kc in range(8):
                S.add("sp", lambda e, kc=kc: e.dma_start(out=Wo[:, kc, :], in_=wo_v[:, kc, :]), writes=[b_Wo[kc]], dma=True)
            for cg in range(NFT // 2):
                cs = slice(cg * 256, (cg + 1) * 256)
                S.add("sp", lambda e, cs=cs: e.dma_start(out=Wg[:, :, cs], in_=wg_v[:, :, cs]), writes=[b_Wg[cg]], dma=True)
                S.add("sp", lambda e, cs=cs: e.dma_start(out=Wu[:, :, cs], in_=wu_v[:, :, cs]), writes=[b_Wu[cg]], dma=True)
            for fc in range(NFT):
                S.add("sp", lambda e, fc=fc: e.dma_start(out=Wd[:, fc, :], in_=wd_v[:, fc, :]), writes=[b_Wd[fc]], dma=True)

        mrot = [0]
        grot = [0]

        def next_bank():
            i = mrot[0] % NM
            mrot[0] += 1
            return i

        def loads(t):
            s = t % 2
            a = t * TC
            S.add("sp", lambda e: e.dma_start(out=mixT[s][:, :, :], in_=mixT_d[:, :, a:a + TC].rearrange("k p t -> p k t")),
                  writes=[b_mixT[s]], dma=True)
            for blk in range(TCB):
                src = x[a + blk * 128:a + (blk + 1) * 128, :]
                S.add("sp", lambda e, blk=blk, src=src: e.dma_start(out=xbuf[s][:, blk, :], in_=src),
                      writes=[b_x[s][blk]], dma=True)

        def wout_part(t):
            s = t % 2
            for blk in range(TCB):
                for half in range(2):
                    bi = next_bank()
                    hs = slice(half * 512, (half + 1) * 512)
                    for kc in range(8):
                        S.add("pe", lambda e, kc=kc, blk=blk, hs=hs, bi=bi: e.matmul(
                            M[bi][:, :], lhsT=mixT[s][:, kc, blk * 128:(blk + 1) * 128], rhs=Wo[:, kc, hs],
                            start=(kc == 0), stop=(kc == 7)),
                            reads=[b_mixT[s], b_Wo[kc]], writes=[b_M[bi]])
                    S.add("dve", lambda e, blk=blk, hs=hs, bi=bi: e.tensor_tensor(out=xbuf[s][:, blk, hs], in0=M[bi][:, :],
                                                                                   in1=xbuf[s][:, blk, hs], op=ALU.add),
                          reads=[b_M[bi], b_x[s][blk]], writes=[b_x[s][blk]])
            for blk in range(TCB):
                S.add("act", lambda e, blk=blk: e.activation(out=junk[:, :], in_=xbuf[s][:, blk, :], func=AF.Square,
                                                              accum_out=ss[:, s, blk:blk + 1]),
                      reads=[b_x[s][blk]], writes=[b_junk, b_ss[s]])
            S.add("act", lambda e: e.activation(out=rstd[:, s, :], in_=ss[:, s, :], func=AF.Sqrt, scale=1.0 / D, bias=EPS),
                  reads=[b_ss[s]], writes=[b_rstd[s]])
            S.add("dve", lambda e: e.reciprocal(out=rstd[:, s, :], in_=rstd[:, s, :]), reads=[b_rstd[s]], writes=[b_rstd[s]])
            for blk in range(TCB):
                if blk % 2 == 0:
                    S.add("act", lambda e, blk=blk: e.activation(out=h2[:, blk, :], in_=xbuf[s][:, blk, :], func=AF.Copy,
                                                                  scale=rstd[:, s, blk:blk + 1]),
                          reads=[b_x[s][blk], b_rstd[s]], writes=[b_h2[blk]])
                else:
                    S.add("dve", lambda e, blk=blk: e.tensor_scalar(out=h2[:, blk, :], in0=xbuf[s][:, blk, :],
                                                                     scalar1=rstd[:, s, blk:blk + 1], scalar2=None, op0=ALU.mult),
                          reads=[b_x[s][blk], b_rstd[s]], writes=[b_h2[blk]])

        def transposes(t):
            s = t % 2
            for kp in range(4):
                pb = kp % 2
                for kk in range(2):
                    kc = 2 * kp + kk
                    for blk in range(TCB):
                        S.add("pe", lambda e, kc=kc, blk=blk, pb=pb, kk=kk: e.transpose(
                            out=psT[pb][:, kk * 512 + blk * 128: kk * 512 + (blk + 1) * 128],
                            in_=h2[:, blk, kc * 128:(kc + 1) * 128], identity=ident[:, :]),
                            reads=[b_h2[blk]], writes=[b_psT[pb]])
                kc0, kc1 = 2 * kp, 2 * kp + 1
                S.add("act", lambda e, kc=kc0, pb=pb: e.activation(out=h2T[s][:, kc, :], in_=psT[pb][:, 0:TC],
                                                                   func=AF.Copy, scale=cols[:, C_GFFN + kc:C_GFFN + kc + 1]),
                      reads=[b_psT[pb]], writes=[b_h2T[s][kc0]])
                S.add("act", lambda e, kc=kc1, pb=pb: e.activation(out=h2T[s][:, kc, :], in_=psT[pb][:, 512:512 + TC],
                                                                   func=AF.Copy, scale=cols[:, C_GFFN + kc:C_GFFN + kc + 1]),
                      reads=[b_psT[pb]], writes=[b_h2T[s][kc1]])

        def gateup(t, ft):
            s = t % 2
            bi = next_bank()
            fs = slice(ft * 128, (ft + 1) * 128)
            for kc in range(8):
                S.add("pe", lambda e, kc=kc: e.matmul(M[bi][:, 0:TC], lhsT=Wg[:, kc, fs], rhs=h2T[s][:, kc, :],
                                                       start=(kc == 0), stop=(kc == 7)),
                      reads=[b_Wg[ft // 2], b_h2T[s][kc]], writes=[b_M[bi]])
            for kc in range(8):
                S.add("pe", lambda e, kc=kc: e.matmul(M[bi][:, TC:2 * TC], lhsT=Wu[:, kc, fs], rhs=h2T[s][:, kc, :],
                                                       start=(kc == 0), stop=(kc == 7)),
                      reads=[b_Wu[ft // 2], b_h2T[s][kc]], writes=[b_M[bi]])
            gi = grot[0] % 2
            grot[0] += 1
            S.add("act", lambda e: e.activation(out=sg[gi][:, :], in_=M[bi][:, 0:TC], func=AF.Silu),
                  reads=[b_M[bi]], writes=[b_sg[gi]])
            S.add("dve", lambda e: e.tensor_tensor(out=actT[:, ft, :], in0=M[bi][:, TC:2 * TC], in1=sg[gi][:, :], op=ALU.mult),
                  reads=[b_M[bi], b_sg[gi]], writes=[b_actT[ft]])

        def down(t):
            s = t % 2
            a = t * TC
            for blk in range(TCB):
                for half in range(2):
                    bi = next_bank()
                    hs = slice(half * 512, (half + 1) * 512)
                    for fc in range(NFT):
                        S.add("pe", lambda e, fc=fc, blk=blk, hs=hs, bi=bi: e.matmul(
                            M[bi][:, :], lhsT=actT[:, fc, blk * 128:(blk + 1) * 128], rhs=Wd[:, fc, hs],
                            start=(fc == 0), stop=(fc == NFT - 1)),
                            reads=[b_actT[fc], b_Wd[fc]], writes=[b_M[bi]])
                    S.add("dve", lambda e, blk=blk, hs=hs, bi=bi: e.tensor_tensor(out=xbuf[s][:, blk, hs], in0=M[bi][:, :],
                                                                                   in1=xbuf[s][:, blk, hs], op=ALU.add),
                          reads=[b_M[bi], b_x[s][blk]], writes=[b_x[s][blk]])
                dst = out[a + blk * 128:a + (blk + 1) * 128, :]
                S.add("pool", lambda e, blk=blk, dst=dst: e.dma_start(out=dst, in_=xbuf[s][:, blk, :]),
                      reads=[b_x[s][blk]], dma=True)

        loads(0)
        weight_loads()
        loads(1)
        wout_part(0)
        transposes(0)
        for t in range(NT):
            for ft in range(NFT):
                gateup(t, ft)
                if ft == 7 and t + 1 < NT:
                    wout_part(t + 1)
                if ft == 15 and t + 1 < NT:
                    transposes(t + 1)
            down(t)
            if t + 2 < NT:
                loads(t + 2)
        S.flush(final=True)


_NC_CACHE = {}


def _host_consts(g_mix, conv_w, g_q, g_k, g_conv_out, g_attn_out, g_ffn):
    cols = np.zeros((128, C_N), np.float32)
    cols[:, C_GMIX:C_GMIX + 8] = g_mix.reshape(8, 128).T
    cols[:, C_GFFN:C_GFFN + 8] = g_ffn.reshape(8, 128).T
    cols[:, C_GCO:C_GCO + 4] = g_conv_out.reshape(4, 128).T
    cw = conv_w.reshape(3, 4, 128)
    cols[:, C_CW:C_CW + 12] = np.transpose(cw, (2, 1, 0)).reshape(128, 12)
    cols[:, C_GQ] = np.tile(g_q.reshape(64), 2)
    cols[:, C_GK] = np.tile(g_k.reshape(64), 2)
    gao = g_attn_out.reshape(8, 64).T
    cols[0:64, C_GAO:C_GAO + 8] = gao
    cols[64:128, C_GAO:C_GAO + 8] = gao
    return cols


def _static_consts():
    ident = np.eye(128, dtype=np.float32)
    k = np.arange(128)[:, None]
    q = np.arange(128)[None, :]
    U = (k >= q).astype(np.float32)
    L = (k <= q).astype(np.float32)
    masks = np.concatenate([L, U, L, U, U, U, U, U, L, L, L, L], axis=1)
    return ident, masks


def kernel(x, g_mix, w_in, conv_w, g_q, g_k, g_conv_out, g_attn_out, w_out, g_ffn, w_gate, w_up, w_down,
           _debug=False, _cores=None, _phases="abc"):
    x = np.asarray(x, np.float32)
    f = lambda a: np.ascontiguousarray(np.asarray(a, np.float32))
    cols = _host_consts(f(g_mix), f(conv_w), f(g_q), f(g_k), f(g_conv_out), f(g_attn_out), f(g_ffn))
    ident, masks = _static_consts()
    key = (bool(_debug), _phases)
    if key not in _NC_CACHE:
        _NC_CACHE[key] = build(debug=_debug, phases=_phases)
    nc = _NC_CACHE[key]
    cores = list(range(NCORES)) if _cores is None else _cores
    shared = {"w_in": f(w_in)[0], "w_out": f(w_out)[0], "w_gate": f(w_gate)[0], "w_up": f(w_up)[0],
              "w_down": f(w_down)[0], "cols": cols, "ident": ident, "masks": masks}
    in_maps = []
    for b in cores:
        m = dict(shared)
        m["x"] = np.ascontiguousarray(x[b])
        in_maps.append(m)
    res = run_bass_kernel_spmd(nc, in_maps, core_ids=list(range(len(cores))))
    if _debug:
        return res.results
    return np.stack([r["out"] for r in res.results], axis=0).astype(np.float32)
```

```python
import numpy as np
import concourse.bass as bass
import concourse.mybir as mybir
from concourse.bass_utils import run_bass_kernel_spmd

F32 = mybir.dt.float32
BF16 = mybir.dt.bfloat16
AF = mybir.ActivationFunctionType
ALU = mybir.AluOpType

S_LEN = 8192
D = 1024
D_IN = 3072
D_FF = 2816
NFT = D_FF // 128
EPS = 1e-6
NCORES = 8
_LIMITS = {}
VROW = 8 * 65

C_GMIX, C_GFFN, C_GCO, C_CW, C_GQ, C_GK, C_GAO, C_N = 0, 8, 16, 20, 32, 33, 34, 42


class Buf:
    __slots__ = ("w", "r", "name")

    def __init__(self, name=""):
        self.w = None
        self.r = []
        self.name = name


class Op:
    __slots__ = ("eng", "fn", "deps", "sig", "val", "dma", "sem", "prev")


class Sched:
    ENGS = ("pe", "act", "dve", "pool", "sp")

    def __init__(self, nc, sems, dsems):
        self.nc = nc
        self.sem = sems
        self.dsem = dsems
        self.cnt = {e: 0 for e in sems}
        self.dcnt = {q: [0] * len(l) for q, l in dsems.items()}
        self.dnext = {q: 0 for q in dsems}
        self.waited = {e: {} for e in self.ENGS}
        self.ops = {e: [] for e in self.ENGS}
        self.bufs = []
        self.barrier_tokens = []

    def buf(self, name=""):
        b = Buf(name)
        self.bufs.append(b)
        return b

    def add(self, eng, fn, reads=(), writes=(), dma=False):
        o = Op()
        o.eng, o.fn, o.dma, o.sig, o.val, o.sem, o.prev = eng, fn, dma, False, None, None, 0
        deps = []
        seen = set()

        def adddep(d):
            if d is None or id(d) in seen:
                return
            seen.add(id(d))
            if (not d.dma) and (not dma) and d.eng == eng and eng == "pe":
                return
            deps.append(d)

        for b in reads:
            adddep(b.w)
        for b in writes:
            adddep(b.w)
            for r in b.r:
                adddep(r)
        o.deps = deps
        for d in deps:
            d.sig = True
        for b in reads:
            b.r.append(o)
        for b in writes:
            b.w = o
            b.r = []
        self.ops[eng].append(o)
        return o

    def flush(self, final=False):
        nc = self.nc
        tokens = []
        for e in self.ENGS:
            lastc = None
            for o in self.ops[e]:
                if not o.dma:
                    lastc = o
            if lastc is not None:
                lastc.sig = True
        for e in self.ENGS:
            for o in self.ops[e]:
                if o.dma:
                    k = self.dnext[e]
                    n = len(self.dsem[e])
                    i = k % n
                    o.sem = self.dsem[e][i]
                    o.prev = self.dcnt[e][i]
                    self.dcnt[e][i] += 16
                    o.val = self.dcnt[e][i]
                    self.dnext[e] = k + 1
                elif o.sig:
                    self.cnt[e] += 1
                    o.val = self.cnt[e]
                    o.sem = self.sem[e]
        for e in self.sem:
            if self.cnt[e] > 0:
                tokens.append((self.sem[e], self.cnt[e]))
        for q in self.dsem:
            for i, s in enumerate(self.dsem[q]):
                if self.dcnt[q][i] > 0:
                    tokens.append((s, self.dcnt[q][i]))

        def run(engname, e):
            W = self.waited[engname]
            for o in self.ops[engname]:
                for d in o.deps:
                    if W.get(id(d.sem), 0) < d.val:
                        e.wait_ge(d.sem, d.val)
                        W[id(d.sem)] = d.val
                if o.dma:
                    if o.prev > 0 and W.get(id(o.sem), 0) < o.prev:
                        e.wait_ge(o.sem, o.prev)
                        W[id(o.sem)] = o.prev
                    o.fn(e).then_inc(o.sem, 16)
                else:
                    ins = o.fn(e)
                    if o.sig:
                        ins.then_inc(o.sem, 1)
            for (s, v) in tokens:
                if W.get(id(s), 0) < v:
                    e.wait_ge(s, v)
                    W[id(s)] = v

        with nc.Block() as block:
            @block.tensor
            def _(e):
                run("pe", e)

            @block.scalar
            def _(e):
                run("act", e)

            @block.vector
            def _(e):
                run("dve", e)

            @block.gpsimd
            def _(e):
                run("pool", e)

            @block.sync
            def _(e):
                run("sp", e)

        self.ops = {e: [] for e in self.ENGS}
        for b in self.bufs:
            b.w = None
            b.r = []
        self.bufs = []


def build(debug=False, phases="abc"):
    nc = bass.Bass("TRN2", target_bir_lowering=False)
    x = nc.dram_tensor("x", [S_LEN, D], F32, kind="ExternalInput").ap()
    w_in = nc.dram_tensor("w_in", [D, D_IN], F32, kind="ExternalInput").ap()
    w_out = nc.dram_tensor("w_out", [D, D], F32, kind="ExternalInput").ap()
    w_gate = nc.dram_tensor("w_gate", [D, D_FF], F32, kind="ExternalInput").ap()
    w_up = nc.dram_tensor("w_up", [D, D_FF], F32, kind="ExternalInput").ap()
    w_down = nc.dram_tensor("w_down", [D_FF, D], F32, kind="ExternalInput").ap()
    cols_d = nc.dram_tensor("cols", [128, C_N], F32, kind="ExternalInput").ap()
    ident_d = nc.dram_tensor("ident", [128, 128], F32, kind="ExternalInput").ap()
    mask_d = nc.dram_tensor("masks", [128, 1536], F32, kind="ExternalInput").ap()
    out = nc.dram_tensor("out", [S_LEN, D], F32, kind="ExternalOutput").ap()
    sk = "ExternalOutput" if debug else "Internal"
    qkT_d = nc.dram_tensor("qkT_d", [8, 128, S_LEN], BF16, kind=sk).ap()
    V_d = nc.dram_tensor("V_d", [S_LEN, VROW], BF16, kind=sk).ap()
    mixT_d = nc.dram_tensor("mixT_d", [8, 128, S_LEN], BF16, kind=sk).ap()

    wo_b = nc.dram_tensor("wo_b", [D, D], BF16, kind="Internal").ap()
    wg_b = nc.dram_tensor("wg_b", [D, D_FF], BF16, kind="Internal").ap()
    wu_b = nc.dram_tensor("wu_b", [D, D_FF], BF16, kind="Internal").ap()
    wd_b = nc.dram_tensor("wd_b", [D_FF, D], BF16, kind="Internal").ap()
    wscr = (w_out, w_gate, w_up, w_down, wo_b, wg_b, wu_b, wd_b)

    from contextlib import ExitStack

    with ExitStack() as gstack:
        def sem(name):
            return gstack.enter_context(nc.semaphore(name))

        sems = {e: sem("s_" + e) for e in ("pe", "act", "dve", "pool")}
        dsems = {"sp": [sem("d_sp%d" % i) for i in range(8)],
                 "pool": [sem("d_pl%d" % i) for i in range(8)]}
        S = Sched(nc, sems, dsems)

        def gsb(name, shape, dt):
            return gstack.enter_context(nc.sbuf_tensor(name, shape, dt))

        cols = gsb("cols_sb", [128, C_N], F32)
        gq8 = gsb("gq8", [128, 1], F32)
        ident = gsb("ident_bf", [128, 128], BF16)
        masks = gsb("masks_bf", [128, 1536], BF16)
        ones_bf = gsb("ones_bf", [128, 128], BF16)
        bones_bf = gsb("bones_bf", [128, 128], BF16)
        ones_f = gsb("ones_f", [128, 64], F32)

        with ExitStack() as st:
            stage = st.enter_context(nc.sbuf_tensor("stage", [128, 1536 + 128], F32))
            b_cols, b_stage = S.buf(), S.buf()
            b_c = S.buf()
            S.add("sp", lambda e: e.dma_start(out=cols[:, :], in_=cols_d), writes=[b_cols], dma=True)
            S.add("sp", lambda e: e.dma_start(out=stage[:, 0:1536], in_=mask_d), writes=[b_stage], dma=True)
            S.add("sp", lambda e: e.dma_start(out=stage[:, 1536:1664], in_=ident_d), writes=[b_stage], dma=True)
            S.add("dve", lambda e: e.tensor_copy(out=masks[:, :], in_=stage[:, 0:1536]), reads=[b_stage], writes=[b_c])
            S.add("dve", lambda e: e.tensor_copy(out=ident[:, :], in_=stage[:, 1536:1664]), reads=[b_stage], writes=[b_c])
            S.add("dve", lambda e: e.tensor_scalar(out=gq8[:, :], in0=cols[:, C_GQ:C_GQ + 1], scalar1=0.125,
                                                     scalar2=None, op0=ALU.mult), reads=[b_cols], writes=[b_c])
            S.add("pool", lambda e: e.memset(ones_bf[:, :], 1.0), writes=[b_c])
            S.add("pool", lambda e: e.memset(bones_bf[:, :], 0.0), writes=[b_c])
            S.add("pool", lambda e: e.memset(bones_bf[0:64, 0:64], 1.0), writes=[b_c])
            S.add("pool", lambda e: e.memset(bones_bf[64:128, 64:128], 1.0), writes=[b_c])
            S.add("pool", lambda e: e.memset(ones_f[:, :], 1.0), writes=[b_c])
            S.flush()

        if "a" in phases:
            phase_a(nc, S, x, w_in, qkT_d, V_d, mixT_d, cols, gq8, ident, ones_bf, bones_bf,
                    wscr if "c" in phases else None)
        if "b" in phases:
            phase_b(nc, S, qkT_d, V_d, mixT_d, cols, masks, ones_bf, ones_f, None, ident)
        if "c" in phases:
            phase_c(nc, S, x, out, wo_b, wg_b, wu_b, wd_b, mixT_d, cols, ident,
                    precast=None if "a" in phases else wscr)
    return nc


def phase_a(nc, S, x, w_in, qkT_d, V_d, mixT_d, cols, gq8, ident, ones_bf, bones_bf, wscr=None):
    NT = _LIMITS.get('a', S_LEN // 512)
    with ExitStack_() as st:
        def sb(name, shape, dt):
            return st.enter_context(nc.sbuf_tensor(name, shape, dt))

        def ps(name, shape, dt=F32):
            return st.enter_context(nc.psum_tensor(name, shape, dt))

        Win = sb("Win", [128, 8, D_IN], BF16)
        xbuf = [sb("xbuf%d" % i, [128, 4, 1024], F32) for i in range(2)]
        junk = sb("junk", [128, 1024], BF16)
        ss = sb("ss", [128, 2, 4], F32)
        rstd = sb("rstd", [128, 2, 4], F32)
        h = sb("h", [128, 4, 1024], BF16)
        hT = [sb("hT%d" % i, [128, 8, 512], BF16) for i in range(2)]
        qko = [sb("qko%d" % i, [128, 8, 512], BF16) for i in range(2)]
        vst = [sb("vst%d" % i, [128, 4, VROW], BF16) for i in range(2)]
        mixo = [sb("mixo%d" % i, [128, 4, 512], BF16) for i in range(2)]
        usb = [sb("usb%d" % i, [128, 512], F32) for i in range(4)]
        gbsb = [sb("gbsb%d" % i, [128, 512], F32) for i in range(4)]
        pbuf = [sb("pbuf%d" % i, [128, 514], F32) for i in range(4)]
        cbuf = [sb("cbuf%d" % i, [128, 512], F32) for i in range(4)]
        ybuf = [sb("ybuf%d" % i, [128, 512], F32) for i in range(4)]
        NR = 3
        sqt = [sb("sqt%d" % i, [128, 512], BF16) for i in range(NR)]
        rst = [sb("rst%d" % i, [128, 512], F32) for i in range(NR)]
        rsc = sb("rsc", [128, 512], F32)

        psT = [ps("psT%d" % i, [128, 1024], BF16) for i in range(2)]
        SSqk = ps("SSqk", [128, 512])
        SSc = SSqk
        NM = 5
        M = [ps("M%d" % i, [128, 512]) for i in range(NM)]
        sqc = [sb("sqc%d" % i, [128, 512], BF16) for i in range(4)]

        B = S.buf
        b_Win = [B() for _ in range(8)]
        b_x = [[B() for _ in range(4)] for _ in range(2)]
        b_junk, b_ss, b_rstd = B(), [B(), B()], [B(), B()]
        b_h = [B() for _ in range(4)]
        b_hT = [[B() for _ in range(8)] for _ in range(2)]
        b_qko, b_vst, b_mixo = [B(), B()], [B(), B()], [B(), B()]
        b_usb, b_gbsb, b_p, b_c, b_y = ([B() for _ in range(4)] for _ in range(5))
        b_usb, b_gbsb, b_p, b_c, b_y = list(b_usb), list(b_gbsb), list(b_p), list(b_c), list(b_y)
        b_sq, b_rs = [B() for _ in range(NR)], [B() for _ in range(NR)]
        b_rsc = B()
        b_psT = [B(), B()]
        b_SSqk = B()
        b_SSc = b_SSqk
        b_sqc = [B() for _ in range(4)]
        b_M = [B() for _ in range(NM)]
        b_init = B()

        w_in_v = w_in.rearrange("(k p) n -> p k n", p=128)
        b_Win = [B() for _ in range(24)]
        order = []
        for f in range(4):
            order += [f, 8 + f, 4 + f]
        order += list(range(12, 24))
        for g in order:
            S.add("pool", lambda e, g=g: e.dma_start(out=Win[:, :, g * 128:(g + 1) * 128], in_=w_in_v[:, :, g * 128:(g + 1) * 128]),
                  writes=[b_Win[g]], dma=True)
        for s in range(2):
            S.add("pool", lambda e, s=s: e.memset(vst[s][:, :, :], 1.0), writes=[b_vst[s]])
        for f in range(4):
            S.add("pool", lambda e, f=f: e.memset(pbuf[f][:, 0:2], 0.0), writes=[b_p[f]])

        mrot = [0]
        rrot = [0]

        def next_bank():
            i = mrot[0] % NM
            mrot[0] += 1
            return i

        def load_x(t):
            s = t % 2
            src = x[t * 512:(t + 1) * 512, :].rearrange("(b p) d -> p b d", p=128)
            S.add("sp", lambda e: e.dma_start(out=xbuf[s][:, :, :], in_=src), writes=b_x[s], dma=True)

        def stats_steps(t):
            s = t % 2
            st_ = []
            for blk in range(4):
                st_.append(lambda blk=blk: S.add("act", lambda e: e.activation(out=junk[:, :], in_=xbuf[s][:, blk, :], func=AF.Square,
                                                                                accum_out=ss[:, s, blk:blk + 1]),
                                                 reads=[b_x[s][blk]], writes=[b_junk, b_ss[s]]))
            st_.append(lambda: S.add("act", lambda e: e.activation(out=rstd[:, s, :], in_=ss[:, s, :], func=AF.Ln, scale=1.0 / D, bias=EPS),
                                     reads=[b_ss[s]], writes=[b_rstd[s]]))
            st_.append(lambda: S.add("act", lambda e: e.activation(out=rstd[:, s, :], in_=rstd[:, s, :], func=AF.Exp, scale=-0.5),
                                     reads=[b_rstd[s]], writes=[b_rstd[s]]))
            for blk in range(4):
                if blk % 2 == 0:
                    st_.append(lambda blk=blk: S.add("act", lambda e: e.activation(out=h[:, blk, :], in_=xbuf[s][:, blk, :], func=AF.Copy,
                                                                                    scale=rstd[:, s, blk:blk + 1]),
                                                     reads=[b_x[s][blk], b_rstd[s]], writes=[b_h[blk]]))
                else:
                    st_.append(lambda blk=blk: S.add("dve", lambda e: e.tensor_scalar(out=h[:, blk, :], in0=xbuf[s][:, blk, :],
                                                                                       scalar1=rstd[:, s, blk:blk + 1], scalar2=None,
                                                                                       op0=ALU.mult),
                                                     reads=[b_x[s][blk], b_rstd[s]], writes=[b_h[blk]]))
            return st_

        def transpose_steps(t):
            s = t % 2
            st_ = []
            for kp in range(4):
                def rnd(kp=kp):
                    pb = kp % 2
                    for kk in range(2):
                        kc = 2 * kp + kk
                        for blk in range(4):
                            S.add("pe", lambda e, kc=kc, blk=blk, pb=pb, kk=kk: e.transpose(
                                out=psT[pb][:, kk * 512 + blk * 128: kk * 512 + (blk + 1) * 128],
                                in_=h[:, blk, kc * 128:(kc + 1) * 128], identity=ident[:, :]),
                                reads=[b_h[blk]], writes=[b_psT[pb]])
                    kc0, kc1 = 2 * kp, 2 * kp + 1
                    S.add("act", lambda e, kc=kc0, pb=pb: e.activation(out=hT[s][:, kc, :], in_=psT[pb][:, 0:512],
                                                                       func=AF.Copy, scale=cols[:, C_GMIX + kc:C_GMIX + kc + 1]),
                          reads=[b_psT[pb]], writes=[b_hT[s][kc0]])
                    S.add("act", lambda e, kc=kc1, pb=pb: e.activation(out=hT[s][:, kc, :], in_=psT[pb][:, 512:1024],
                                                                       func=AF.Copy, scale=cols[:, C_GMIX + kc:C_GMIX + kc + 1]),
                          reads=[b_psT[pb]], writes=[b_hT[s][kc1]])
                st_.append(rnd)
            return st_

        asteps = []

        def pop_astep():
            if asteps:
                asteps.pop(0)()

        def group_fm(t, col0):
            s = t % 2
            bi = next_bank()
            for kc in range(8):
                S.add("pe", lambda e, kc=kc: e.matmul(M[bi][:, :], lhsT=Win[:, kc, col0:col0 + 128], rhs=hT[s][:, kc, :],
                                                       start=(kc == 0), stop=(kc == 7)),
                      reads=[b_Win[col0 // 128], b_hT[s][kc]], writes=[b_M[bi]])
            pop_astep()
            return bi

        pending = []

        def run_pending():
            while pending:
                pending.pop(0)()

        def conv_part(t, f):
            s = t % 2
            bu = group_fm(t, f * 128)
            S.add("act", lambda e: e.activation(out=usb[f][:, :], in_=M[bu][:, :], func=AF.Copy),
                  reads=[b_M[bu]], writes=[b_usb[f]])
            run_pending()
            bc = group_fm(t, 1024 + f * 128)
            S.add("dve", lambda e: e.tensor_tensor(out=pbuf[f][:, 2:514], in0=M[bc][:, :], in1=usb[f][:, :], op=ALU.mult),
                  reads=[b_M[bc], b_usb[f]], writes=[b_p[f]])
            bg = group_fm(t, 512 + f * 128)
            S.add("act", lambda e: e.activation(out=gbsb[f][:, :], in_=M[bg][:, :], func=AF.Copy),
                  reads=[b_M[bg]], writes=[b_gbsb[f]])
            cw = C_CW + 3 * f
            S.add("act", lambda e: e.activation(out=cbuf[f][:, :], in_=pbuf[f][:, 0:512], func=AF.Copy,
                                                scale=cols[:, cw:cw + 1]),
                  reads=[b_p[f]], writes=[b_c[f]])
            S.add("dve", lambda e: e.scalar_tensor_tensor(out=cbuf[f][:, :], in0=pbuf[f][:, 1:513], scalar=cols[:, cw + 1:cw + 2],
                                                           in1=cbuf[f][:, :], op0=ALU.mult, op1=ALU.add),
                  reads=[b_p[f], b_c[f]], writes=[b_c[f]])
            S.add("dve", lambda e: e.scalar_tensor_tensor(out=cbuf[f][:, :], in0=pbuf[f][:, 2:514], scalar=cols[:, cw + 2:cw + 3],
                                                           in1=cbuf[f][:, :], op0=ALU.mult, op1=ALU.add),
                  reads=[b_p[f], b_c[f]], writes=[b_c[f]])
            S.add("act", lambda e: e.activation(out=pbuf[f][:, 0:2], in_=pbuf[f][:, 512:514], func=AF.Copy),
                  reads=[b_p[f]], writes=[b_p[f]])
            S.add("dve", lambda e: e.tensor_tensor(out=ybuf[f][:, :], in0=cbuf[f][:, :], in1=gbsb[f][:, :], op=ALU.mult),
                  reads=[b_c[f], b_gbsb[f]], writes=[b_y[f]])
            S.add("act", lambda e: e.activation(out=sqc[f][:, :], in_=ybuf[f][:, :], func=AF.Square),
                  reads=[b_y[f]], writes=[b_sqc[f]])

            def part2():
                if f == 3:
                    for ff in range(4):
                        S.add("pe", lambda e, ff=ff: e.matmul(SSc[:, :], lhsT=ones_bf[:, :], rhs=sqc[ff][:, :],
                                                               start=(ff == 0), stop=(ff == 3)),
                              reads=[b_sqc[ff]], writes=[b_SSc])
                    S.add("act", lambda e: e.activation(out=rsc[:, :], in_=SSc[:, :], func=AF.Ln, scale=1.0 / 512, bias=EPS),
                          reads=[b_SSc], writes=[b_rsc])
                    S.add("act", lambda e: e.activation(out=rsc[:, :], in_=rsc[:, :], func=AF.Exp, scale=-0.5),
                          reads=[b_rsc], writes=[b_rsc])
                    for ff in range(4):
                        S.add("dve", lambda e, ff=ff: e.scalar_tensor_tensor(
                            out=mixo[s][:, ff, :], in0=ybuf[ff][:, :], scalar=cols[:, C_GCO + ff:C_GCO + ff + 1],
                            in1=rsc[:, :], op0=ALU.mult, op1=ALU.mult),
                            reads=[b_y[ff], b_rsc], writes=[b_mixo[s]])
                    dst = mixT_d[0:4, :, t * 512:(t + 1) * 512].rearrange("f p t -> p f t")
                    S.add("pool", lambda e: e.dma_start(out=dst, in_=mixo[s][:, :, :]), reads=[b_mixo[s]], dma=True)
            pending.append(part2)

        def qk_part(t, ft):
            s = t % 2
            bi = group_fm(t, 1536 + ft * 128)
            ri = rrot[0] % NR
            rrot[0] += 1
            S.add("act", lambda e: e.activation(out=sqt[ri][:, :], in_=M[bi][:, :], func=AF.Square),
                  reads=[b_M[bi]], writes=[b_sq[ri]])
            run_pending()
            gcol = gq8[:, 0:1] if ft < 4 else cols[:, C_GK:C_GK + 1]

            def part2():
                S.add("pe", lambda e: e.matmul(SSqk[:, :], lhsT=bones_bf[:, :], rhs=sqt[ri][:, :], start=True, stop=True),
                      reads=[b_sq[ri]], writes=[b_SSqk])
                S.add("act", lambda e: e.activation(out=rst[ri][:, :], in_=SSqk[:, :], func=AF.Ln, scale=1.0 / 64, bias=EPS),
                      reads=[b_SSqk], writes=[b_rs[ri]])
                S.add("act", lambda e: e.activation(out=rst[ri][:, :], in_=rst[ri][:, :], func=AF.Exp, scale=-0.5),
                      reads=[b_rs[ri]], writes=[b_rs[ri]])
                S.add("dve", lambda e: e.scalar_tensor_tensor(out=qko[s][:, ft, :], in0=M[bi][:, :], scalar=gcol,
                                                              in1=rst[ri][:, :], op0=ALU.mult, op1=ALU.mult),
                      reads=[b_M[bi], b_rs[ri]], writes=[b_qko[s]])
                if ft == 7:
                    dst = qkT_d[:, :, t * 512:(t + 1) * 512].rearrange("j p t -> p j t")
                    S.add("pool", lambda e: e.dma_start(out=dst, in_=qko[s][:, :, :]), reads=[b_qko[s]], dma=True)
            pending.append(part2)

        def v_part(t, blk):
            s = t % 2
            bi = next_bank()
            for kc in range(8):
                S.add("pe", lambda e, kc=kc: e.matmul(M[bi][:, :], lhsT=hT[s][:, kc, blk * 128:(blk + 1) * 128],
                                                       rhs=Win[:, kc, 2560:3072], start=(kc == 0), stop=(kc == 7)),
                      reads=[b_Win[20], b_Win[21], b_Win[22], b_Win[23], b_hT[s][kc]], writes=[b_M[bi]])
            pop_astep()
            dstv = vst[s][:, blk, :].rearrange("p (h e) -> p h e", e=65)[:, :, 0:64]
            srcv = M[bi][:, :].rearrange("p (h d) -> p h d", d=64)
            S.add("act", lambda e: e.activation(out=dstv, in_=srcv, func=AF.Copy), reads=[b_M[bi]], writes=[b_vst[s]])
            run_pending()
            if blk == 3:
                dst = V_d[t * 512:(t + 1) * 512, :].rearrange("(b p) e -> p b e", p=128)
                S.add("pool", lambda e: e.dma_start(out=dst, in_=vst[s][:, :, :]), reads=[b_vst[s]], dma=True)

        pc_list = precast_list(wscr) if wscr is not None else []
        load_x(0)
        if NT > 1:
            load_x(1)
        for f_ in stats_steps(0) + transpose_steps(0):
            f_()
        for t in range(NT):
            if t + 1 < NT:
                asteps.extend(stats_steps(t + 1) + transpose_steps(t + 1))
            for f in range(4):
                conv_part(t, f)
            for ft in range(8):
                qk_part(t, ft)
            for blk in range(4):
                v_part(t, blk)
            run_pending()
            while asteps:
                asteps.pop(0)()
            if t + 2 < NT:
                load_x(t + 2)
            if wscr is not None and t >= 1:
                emit_precast(S, None, n=2, lst=pc_list)
        if wscr is not None:
            emit_precast(S, None, lst=pc_list)
        S.flush()


def ExitStack_():
    from contextlib import ExitStack
    return ExitStack()


def precast_list(wscr):
    w_out, w_gate, w_up, w_down, wo_b, wg_b, wu_b, wd_b = wscr
    lst = []
    for (src_, dst_, rows) in ((w_out, wo_b, D), (w_gate, wg_b, D), (w_up, wu_b, D), (w_down, wd_b, D_FF)):
        step = 256
        for r in range(0, rows, step):
            lst.append((dst_[r:r + step, :], src_[r:r + step, :]))
    return lst


def emit_precast(S, wscr, n=None, lst=None):
    lst = precast_list(wscr) if lst is None else lst
    k = 0
    while lst and (n is None or k < n):
        d_, s_ = lst.pop(0)
        S.add("pool", lambda e, d_=d_, s_=s_: e.dma_start(out=d_, in_=s_), dma=True)
        k += 1


def phase_b(nc, S, qkT_d, V_d, mixT_d, cols, masks, ones_bf, ones_f, wscr=None, ident=None):
    NCH = _LIMITS.get('b', S_LEN // 2048)
    with ExitStack_() as st:
        def sb(name, shape, dt):
            return st.enter_context(nc.sbuf_tensor(name, shape, dt))

        def ps(name, shape, dt=F32):
            return st.enter_context(nc.psum_tensor(name, shape, dt))

        qj = [sb("qz%d" % i, [128, 2, 2048], BF16) for i in range(2)]
        kj = [sb("kj%d" % i, [128, 2, 2048], BF16) for i in range(2)]
        vnat = sb("vnat", [128, 17, VROW], BF16)
        vd4 = sb("vd4", [128, 20, VROW], BF16)
        vd16 = [sb("vd16_%d" % i, [128, 16, VROW], BF16) for i in range(2)]
        NE = 3
        Et = [sb("E%d" % i, [128, 512], BF16) for i in range(NE)]
        num = sb("num", [128, 2048], F32)
        yT = sb("yT", [64, 8, 2048], F32)
        bcs = sb("bcs", [64, 2048], F32)
        ysqacc = sb("ysqacc", [64, 2048], F32)
        rsa = sb("rsa", [64, 512], F32)
        mao = sb("mao", [64, 8, 512], BF16)

        acc = ps("acc", [128, 2048])
        NST = 3
        ST = [ps("ST%d" % i, [128, 512]) for i in range(NST)]
        SSa = [ps("SSa%d" % i, [128, 512]) for i in range(1)]

        B = S.buf
        b_qj, b_kj = [B(), B()], [B(), B()]
        b_vnat, b_vd4, b_vd16 = B(), B(), [B(), B()]
        b_E = [B() for _ in range(NE)]
        b_num, b_yT = B(), [B() for _ in range(8)]
        b_bcs, b_ysqacc, b_rsa, b_mao = B(), B(), B(), B()
        b_rden = [B(), B()]
        rden_d = nc.dram_tensor("rden_d", [2, 2048], F32, kind="Internal").ap()
        b_acc, b_ST, b_SSa = [B() for _ in range(4)], [B() for _ in range(NST)], [B()]
        pending_epi = []
        steps = []
        urgent = []

        erot = [0]
        srot = [0]
        prot = [0]
        for s_ in range(2):
            S.add("pool", lambda e, s_=s_: e.memset(qj[s_][64:128, 0, :], 0.0), writes=[b_qj[s_]])
            S.add("pool", lambda e, s_=s_: e.memset(qj[s_][0:64, 1, :], 0.0), writes=[b_qj[s_]])

        def sl(a, n, step):
            return slice(a, a + (n - 1) * step + 1, step)

        def load_pair(c, j):
            s = prot[0] % 2
            prot[0] += 1
            t0 = c * 2048
            S.add("sp", lambda e: e.dma_start(out=qj[s][0:64, 0, :], in_=qkT_d[j, 0:64, t0:t0 + 2048]), writes=[b_qj[s]], dma=True)
            S.add("sp", lambda e: e.dma_start(out=qj[s][64:128, 1, :], in_=qkT_d[j, 64:128, t0:t0 + 2048]), writes=[b_qj[s]], dma=True)
            if c > 0:
                S.add("sp", lambda e: e.dma_start(out=kj[s][:, :, :],
                                                  in_=qkT_d[4 + j, :, t0 - 2048:t0 + 2048].rearrange("p (c t) -> p c t", c=2)),
                      writes=[b_kj[s]], dma=True)
            else:
                S.add("sp", lambda e: e.dma_start(out=kj[s][:, 1, :], in_=qkT_d[4 + j, :, t0:t0 + 2048]),
                      writes=[b_kj[s]], dma=True)
            return s

        for c in range(NCH):
            t0 = c * 2048
            cp = c % 2
            pp = (c - 1) % 2
            ps_first = load_pair(c, 0)
            if c == 0:
                S.add("sp", lambda e: e.dma_start(out=vnat[:, 1:17, :], in_=V_d[0:2048, :].rearrange("(b p) e -> p b e", p=128)),
                      writes=[b_vnat], dma=True)
                S.add("sp", lambda e: e.dma_start(out=vd4[:, 4:20, :].rearrange("p (t r) e -> p t r e", r=4),
                                                  in_=V_d[0:2048, :].rearrange("(t p r) e -> p t r e", p=128, r=4)),
                      writes=[b_vd4], dma=True)
            else:
                S.add("sp", lambda e, t0=t0: e.dma_start(out=vnat[:, 0:17, :],
                                                         in_=V_d[t0 - 128:t0 + 2048, :].rearrange("(b p) e -> p b e", p=128)),
                      writes=[b_vnat], dma=True)
                S.add("sp", lambda e, t0=t0: e.dma_start(out=vd4[:, 0:20, :].rearrange("p (t r) e -> p t r e", r=4),
                                                         in_=V_d[t0 - 512:t0 + 2048, :].rearrange("(t p r) e -> p t r e", p=128, r=4)),
                      writes=[b_vd4], dma=True)
            S.add("sp", lambda e, t0=t0, cp=cp: e.dma_start(out=vd16[cp][:, :, :],
                                                            in_=V_d[t0:t0 + 2048, :].rearrange("(p r) e -> p r e", r=16)),
                  writes=[b_vd16[cp]], dma=True)

            items = []
            if c > 0:
                items.append((1, 1, 0, sl(1920, 128, 1), vnat, b_vnat, 0, [(0, "U")]))
            for b in range(16):
                qb = [(128 * b, "L")]
                if b < 15:
                    qb.append((128 * (b + 1), "U"))
                items.append((1, 1, 1, sl(128 * b, 128, 1), vnat, b_vnat, b + 1, qb))
            if c > 0:
                for r in range(4):
                    items.append((4, 4, 0, sl(1536 + r, 128, 4), vd4, b_vd4, r, [(r, "U")]))
            for t in range(4):
                for r in range(4):
                    qb = [(512 * t + r, "L")]
                    if t < 3:
                        qb.append((512 * (t + 1) + r, "U"))
                    items.append((4, 4, 1, sl(512 * t + r, 128, 4), vd4, b_vd4, 4 + 4 * t + r, qb))
            if c > 0:
                for r in range(16):
                    items.append((16, 16, 0, sl(r, 128, 16), vd16[pp], b_vd16[pp], r, [(r, "U")]))
            for r in range(16):
                items.append((16, 16, 1, sl(r, 128, 16), vd16[cp], b_vd16[cp], r, [(r, "L")]))

            def cls_of(it):
                kinds = "".join(k for (_q, k) in it[7])
                return kinds
            banks = []
            i = 0
            while i < len(items):
                cl = cls_of(items[i])
                cap = 2 if cl == "LU" else 4
                grp = [items[i]]
                jj = i + 1
                while jj < len(items) and len(grp) < cap and cls_of(items[jj]) == cl:
                    grp.append(items[jj])
                    jj += 1
                banks.append((cl, grp))
                i = jj
            nb = len(banks)

            def make_head(hd, ps_):
                started = [False] * 4

                def emit_st(bi_, hsel=hd % 2, ps_=ps_):
                    cl, grp = banks[bi_]
                    si = srot[0] % NST
                    srot[0] += 1
                    off = 0
                    for (dil, qstep, kc_, ks, vt, vb, vslot, qb) in grp:
                        npart = len(qb)
                        q0 = qb[0][0]
                        if npart == 1:
                            rhs = qj[ps_][:, hsel, slice(q0, q0 + 127 * qstep + 1, qstep)]
                            oap = ST[si][:, off:off + 128]
                        elif dil == 1:
                            rhs = qj[ps_][:, hsel, q0:q0 + 256]
                            oap = ST[si][:, off:off + 256]
                        else:
                            t_, r_ = q0 // 512, q0 % 512
                            rhs = qj[ps_][:, hsel, :].rearrange("p (t m r) -> p t m r", t=4, r=4)[:, t_:t_ + 2, :, r_]
                            oap = ST[si][:, off:off + 256].rearrange("p (a m) -> p a m", a=2)
                        S.add("pe", lambda e, kc_=kc_, ks=ks, rhs=rhs, oap=oap, first=(off == 0): e.matmul(
                            oap, lhsT=kj[ps_][:, kc_, ks], rhs=rhs, start=first, stop=False, skip_group_check=True),
                            reads=[b_kj[ps_], b_qj[ps_]], writes=[b_ST[si]])
                        off += 128 * npart
                    moff = {"LU": 0, "U": 512, "L": 1024}[cl]
                    S.add("pe", lambda e, n=off, moff=moff: e.matmul(ST[si][:, 0:n], lhsT=ident[:, :], rhs=masks[:, moff:moff + n],
                                                                      start=False, stop=True, skip_group_check=True),
                          writes=[b_ST[si]])
                    return si

                def emit_rest(bi_, si, hd=hd, started=started):
                    cl, grp = banks[bi_]
                    n = sum(len(it[7]) for it in grp) * 128
                    ei = erot[0] % NE
                    erot[0] += 1
                    S.add("act", lambda e: e.activation(out=Et[ei][:, 0:n], in_=ST[si][:, 0:n], func=AF.Exp),
                          reads=[b_ST[si]], writes=[b_E[ei]])
                    off = 0
                    for (dil, qstep, kc_, ks, vt, vb, vslot, qb) in grp:
                        vap = vt[:, vslot, hd * 65:(hd + 1) * 65]
                        segs = []
                        if dil == 16:
                            for (q0, _k) in qb:
                                for k in range(4):
                                    segs.append((k, 512 * k + q0, off + 32 * k, 32, 16))
                                off += 128
                        elif dil == 1 and len(qb) == 2 and (qb[0][0] // 512) == (qb[1][0] // 512):
                            segs.append((qb[0][0] // 512, qb[0][0], off, 256, 1))
                            off += 256
                        else:
                            for (q0, _k) in qb:
                                segs.append((q0 // 512, q0, off, 128, qstep))
                                off += 128
                        for (bk, oc, eo, nq, ostep) in segs:
                            first = not started[bk]
                            started[bk] = True
                            oslice = slice(oc, oc + (nq - 1) * ostep + 1, ostep)
                            S.add("pe", lambda e, vap=vap, eo=eo, nq=nq, oslice=oslice, first=first: e.matmul(
                                acc[0:65, oslice], lhsT=vap, rhs=Et[ei][:, eo:eo + nq],
                                start=first, stop=False, skip_group_check=True),
                                reads=[vb, b_E[ei]], writes=[b_acc[bk]])


                def head_end():
                    while urgent:
                        urgent.pop(0)()
                    while steps:
                        f_ = steps.pop(0)
                        if f_ is not None:
                            f_()
                    def evac(k):
                        S.add("act", lambda e: e.activation(out=num[0:65, 512 * k:512 * (k + 1)],
                                                            in_=acc[0:65, 512 * k:512 * (k + 1)], func=AF.Copy),
                              reads=[b_acc[k]], writes=[b_num])
                    evac(0)
                    for k in (1, 2, 3):
                        urgent.append(lambda k=k, evac=evac: evac(k))
                    rs_ = hd % 2

                    def mk_steps(hd=hd, rs_=rs_):
                        st_ = []
                        for k in range(4):
                            cs = slice(512 * k, 512 * (k + 1))
                            st_.append(lambda cs=cs: S.add("act", lambda e: e.activation(out=num[64:65, cs], in_=num[64:65, cs], func=AF.Ln),
                                                           reads=[b_num], writes=[b_num]))
                        for k in range(4):
                            cs = slice(512 * k, 512 * (k + 1))
                            st_.append(lambda cs=cs: S.add("act", lambda e: e.activation(out=num[64:65, cs], in_=num[64:65, cs], func=AF.Exp,
                                                                                          scale=-1.0),
                                                           reads=[b_num], writes=[b_num]))
                        st_.append(lambda: S.add("sp", lambda e: e.dma_start(out=rden_d[rs_:rs_ + 1, :], in_=num[64:65, :]),
                                                 reads=[b_num], writes=[b_rden[rs_]], dma=True))
                        st_.append(lambda: S.add("sp", lambda e: e.dma_start(out=bcs[0:64, :],
                                                                             in_=rden_d[rs_:rs_ + 1, :].partition_broadcast(64)),
                                                 reads=[b_rden[rs_]], writes=[b_bcs], dma=True))
                        st_ += [None] * 9
                        for k in range(4):
                            cs = slice(512 * k, 512 * (k + 1))
                            st_.append(lambda cs=cs: S.add("dve", lambda e: e.tensor_tensor(out=yT[:, hd, cs], in0=num[0:64, cs],
                                                                                           in1=bcs[0:64, cs], op=ALU.mult),
                                                           reads=[b_num, b_bcs], writes=[b_yT[hd]]))
                        for k in range(4):
                            cs = slice(512 * k, 512 * (k + 1))
                            if hd == 0:
                                st_.append(lambda cs=cs: S.add("act", lambda e: e.activation(out=ysqacc[:, cs], in_=yT[:, hd, cs], func=AF.Square),
                                                               reads=[b_yT[hd]], writes=[b_ysqacc]))
                            else:
                                st_.append(lambda cs=cs: S.add("act", lambda e: e.activation(out=bcs[:, cs], in_=yT[:, hd, cs], func=AF.Square),
                                                               reads=[b_yT[hd]], writes=[b_bcs]))
                        if hd != 0:
                            for k in range(4):
                                cs = slice(512 * k, 512 * (k + 1))
                                st_.append(lambda cs=cs: S.add("dve", lambda e: e.tensor_tensor(out=ysqacc[:, cs], in0=ysqacc[:, cs],
                                                                                               in1=bcs[:, cs], op=ALU.add),
                                                               reads=[b_bcs, b_ysqacc], writes=[b_ysqacc]))
                        return st_
                    steps.extend(mk_steps())


                return emit_st, emit_rest, head_end

            slots = {0: ps_first}
            ctx = {}
            inflight = []
            for hd in range(8):
                j = hd // 2
                for bi_ in range(nb):
                    if bi_ == 0:
                        if hd % 2 == 0 and j < 3:
                            slots[j + 1] = load_pair(c, j + 1)
                        ctx[hd] = make_head(hd, slots[j])
                    si = ctx[hd][0](bi_)
                    inflight.append((hd, bi_, si))
                    if len(inflight) > 2:
                        h0, b0_, s0 = inflight.pop(0)
                        ctx[h0][1](b0_, s0)
                        if b0_ == nb - 1:
                            ctx[h0][2]()
                    if urgent:
                        urgent.pop(0)()
                    if bi_ >= 2:
                        for _ in range(3):
                            if steps:
                                f_ = steps.pop(0)
                                if f_ is not None:
                                    f_()
            while inflight:
                h0, b0_, s0 = inflight.pop(0)
                ctx[h0][1](b0_, s0)
                if b0_ == nb - 1:
                    ctx[h0][2]()

            def mk_epi(t0=t0):
                st_ = []
                for k in range(4):
                    cs = slice(512 * k, 512 * (k + 1))
                    def ssq_step(cs=cs):
                        S.add("pe", lambda e: e.matmul(SSa[0][0:64, :], lhsT=ones_f[0:64, 0:64], rhs=ysqacc[:, cs], start=True, stop=True),
                              reads=[b_ysqacc], writes=[b_SSa[0]])
                        S.add("act", lambda e: e.activation(out=rsa[:, :], in_=SSa[0][0:64, :], func=AF.Ln, scale=1.0 / 512, bias=EPS),
                              reads=[b_SSa[0]], writes=[b_rsa])
                        S.add("act", lambda e: e.activation(out=rsa[:, :], in_=rsa[:, :], func=AF.Exp, scale=-0.5),
                              reads=[b_rsa], writes=[b_rsa])
                    st_.append(ssq_step)
                    st_.append(None)
                    def stt2(h0, cs=cs):
                        for hd in (h0, h0 + 1):
                            S.add("dve", lambda e, hd=hd: e.scalar_tensor_tensor(
                                out=mao[:, hd, :], in0=yT[:, hd, cs], scalar=cols[0:64, C_GAO + hd:C_GAO + hd + 1],
                                in1=rsa[:, :], op0=ALU.mult, op1=ALU.mult),
                                reads=[b_yT[hd], b_rsa], writes=[b_mao])
                    for h0 in (0, 2, 4, 6):
                        st_.append(lambda h0=h0, stt2=stt2: stt2(h0))
                    dst = mixT_d[4:8, :, t0 + 512 * k:t0 + 512 * (k + 1)].rearrange("f (h d) t -> d (f h) t", h=2)
                    st_.append(lambda dst=dst: S.add("pool", lambda e: e.dma_start(out=dst, in_=mao[:, :, :]), reads=[b_mao], dma=True))
                return st_
            steps.extend(mk_epi())
        while urgent:
            urgent.pop(0)()
        while steps:
            f_ = steps.pop(0)
            if f_ is not None:
                f_()
        S.flush()


def phase_c(nc, S, x, out, w_out, w_gate, w_up, w_down, mixT_d, cols, ident, precast=None):
    if precast is not None:
        emit_precast(S, precast)
        S.flush()
    TCB = 2
    TC = TCB * 128
    NT = _LIMITS.get('c', S_LEN // TC)
    with ExitStack_() as st:
        def sb(name, shape, dt):
            return st.enter_context(nc.sbuf_tensor(name, shape, dt))

        def ps(name, shape, dt=F32):
            return st.enter_context(nc.psum_tensor(name, shape, dt))

        Wo = sb("Wo", [128, 8, D], BF16)
        Wg = sb("Wg", [128, 8, D_FF], BF16)
        Wu = sb("Wu", [128, 8, D_FF], BF16)
        Wd = sb("Wd", [128, NFT, D], BF16)
        xbuf = [sb("xc%d" % i, [128, TCB, 1024], F32) for i in range(2)]
        mixT = [sb("mixT%d" % i, [128, 8, TC], BF16) for i in range(2)]
        h2 = sb("h2", [128, TCB, 1024], BF16)
        h2T = [sb("h2T%d" % i, [128, 8, TC], BF16) for i in range(2)]
        actT = sb("actT", [128, NFT, TC], BF16)
        sg = [sb("sg%d" % i, [128, TC], F32) for i in range(2)]
        junk = sb("junkc", [128, 1024], BF16)
        ss = sb("ssc", [128, 2, TCB], F32)
        rstd = sb("rstdc", [128, 2, TCB], F32)

        psT = [ps("psTc%d" % i, [128, 1024], BF16) for i in range(2)]
        NM = 6
        M = [ps("Mc%d" % i, [128, 512]) for i in range(NM)]

        B = S.buf
        b_Wo = [B() for _ in range(8)]
        b_Wg = [B() for _ in range(8)]
        b_Wu = [B() for _ in range(8)]
        b_Wd = [B() for _ in range(NFT)]
        b_x = [[B() for _ in range(TCB)] for _ in range(2)]
        b_mixT = [B(), B()]
        b_h2 = [B() for _ in range(TCB)]
        b_h2T = [[B() for _ in range(8)] for _ in range(2)]
        b_actT = [B() for _ in range(NFT)]
        b_sg = [B(), B()]
        b_junk, b_ss, b_rstd = B(), [B(), B()], [B(), B()]
        b_psT = [B(), B()]
        b_M = [B() for _ in range(NM)]

        wo_v = w_out.rearrange("(k p) n -> p k n", p=128)
        wg_v = w_gate.rearrange("(k p) n -> p k n", p=128)
        wu_v = w_up.rearrange("(k p) n -> p k n", p=128)
        wd_v = w_down.rearrange("(k p) n -> p k n", p=128)
        b_Wg = [B() for _ in range(NFT // 2)]
        b_Wu = [B() for _ in range(NFT // 2)]

        def weight_loads():
            for kc in range(8):
                S.add("sp", lambda e, kc=kc: e.dma_start(out=Wo[:, kc, :], in_=wo_v[:, kc, :]), writes=[b_Wo[kc]], dma=True)
            for cg in range(NFT // 2):
                cs = slice(cg * 256, (cg + 1) * 256)
                S.add("sp", lambda e, cs=cs: e.dma_start(out=Wg[:, :, cs], in_=wg_v[:, :, cs]), writes=[b_Wg[cg]], dma=True)
                S.add("sp", lambda e, cs=cs: e.dma_start(out=Wu[:, :, cs], in_=wu_v[:, :, cs]), writes=[b_Wu[cg]], dma=True)
            for fc in range(NFT):
                S.add("sp", lambda e, fc=fc: e.dma_start(out=Wd[:, fc, :], in_=wd_v[:, fc, :]), writes=[b_Wd[fc]], dma=True)

        mrot = [0]
        grot = [0]

        def next_bank():
            i = mrot[0] % NM
            mrot[0] += 1
            return i

        def loads(t):
            s = t % 2
            a = t * TC
            S.add("sp", lambda e: e.dma_start(out=mixT[s][:, :, :], in_=mixT_d[:, :, a:a + TC].rearrange("k p t -> p k t")),
                  writes=[b_mixT[s]], dma=True)
            for blk in range(TCB):
                src = x[a + blk * 128:a + (blk + 1) * 128, :]
                S.add("sp", lambda e, blk=blk, src=src: e.dma_start(out=xbuf[s][:, blk, :], in_=src),
                      writes=[b_x[s][blk]], dma=True)

        def wout_part(t):
            s = t % 2
            for blk in range(TCB):
                for half in range(2):
                    bi = next_bank()
                    hs = slice(half * 512, (half + 1) * 512)
                    for kc in range(8):
                        S.add("pe", lambda e, kc=kc, blk=blk, hs=hs, bi=bi: e.matmul(
                            M[bi][:, :], lhsT=mixT[s][:, kc, blk * 128:(blk + 1) * 128], rhs=Wo[:, kc, hs],
                            start=(kc == 0), stop=(kc == 7)),
                            reads=[b_mixT[s], b_Wo[kc]], writes=[b_M[bi]])
                    S.add("dve", lambda e, blk=blk, hs=hs, bi=bi: e.tensor_tensor(out=xbuf[s][:, blk, hs], in0=M[bi][:, :],
                                                                                   in1=xbuf[s][:, blk, hs], op=ALU.add),
                          reads=[b_M[bi], b_x[s][blk]], writes=[b_x[s][blk]])
            for blk in range(TCB):
                S.add("act", lambda e, blk=blk: e.activation(out=junk[:, :], in_=xbuf[s][:, blk, :], func=AF.Square,
                                                              accum_out=ss[:, s, blk:blk + 1]),
                      reads=[b_x[s][blk]], writes=[b_junk, b_ss[s]])
            S.add("act", lambda e: e.activation(out=rstd[:, s, :], in_=ss[:, s, :], func=AF.Sqrt, scale=1.0 / D, bias=EPS),
                  reads=[b_ss[s]], writes=[b_rstd[s]])
            S.add("dve", lambda e: e.reciprocal(out=rstd[:, s, :], in_=rstd[:, s, :]), reads=[b_rstd[s]], writes=[b_rstd[s]])
            for blk in range(TCB):
                if blk % 2 == 0:
                    S.add("act", lambda e, blk=blk: e.activation(out=h2[:, blk, :], in_=xbuf[s][:, blk, :], func=AF.Copy,
                                                                  scale=rstd[:, s, blk:blk + 1]),
                          reads=[b_x[s][blk], b_rstd[s]], writes=[b_h2[blk]])
                else:
                    S.add("dve", lambda e, blk=blk: e.tensor_scalar(out=h2[:, blk, :], in0=xbuf[s][:, blk, :],
                                                                     scalar1=rstd[:, s, blk:blk + 1], scalar2=None, op0=ALU.mult),
                          reads=[b_x[s][blk], b_rstd[s]], writes=[b_h2[blk]])

        def transposes(t):
            s = t % 2
            for kp in range(4):
                pb = kp % 2
                for kk in range(2):
                    kc = 2 * kp + kk
                    for blk in range(TCB):
                        S.add("pe", lambda e, kc=kc, blk=blk, pb=pb, kk=kk: e.transpose(
                            out=psT[pb][:, kk * 512 + blk * 128: kk * 512 + (blk + 1) * 128],
                            in_=h2[:, blk, kc * 128:(kc + 1) * 128], identity=ident[:, :]),
                            reads=[b_h2[blk]], writes=[b_psT[pb]])
                kc0, kc1 = 2 * kp, 2 * kp + 1
                S.add("act", lambda e, kc=kc0, pb=pb: e.activation(out=h2T[s][:, kc, :], in_=psT[pb][:, 0:TC],
                                                                   func=AF.Copy, scale=cols[:, C_GFFN + kc:C_GFFN + kc + 1]),
                      reads=[b_psT[pb]], writes=[b_h2T[s][kc0]])
                S.add("act", lambda e, kc=kc1, pb=pb: e.activation(out=h2T[s][:, kc, :], in_=psT[pb][:, 512:512 + TC],
                                                                   func=AF.Copy, scale=cols[:, C_GFFN + kc:C_GFFN + kc + 1]),
                      reads=[b_psT[pb]], writes=[b_h2T[s][kc1]])

        def gateup(t, ft):
            s = t % 2
            bi = next_bank()
            fs = slice(ft * 128, (ft + 1) * 128)
            for kc in range(8):
                S.add("pe", lambda e, kc=kc: e.matmul(M[bi][:, 0:TC], lhsT=Wg[:, kc, fs], rhs=h2T[s][:, kc, :],
                                                       start=(kc == 0), stop=(kc == 7)),
                      reads=[b_Wg[ft // 2], b_h2T[s][kc]], writes=[b_M[bi]])
            for kc in range(8):
                S.add("pe", lambda e, kc=kc: e.matmul(M[bi][:, TC:2 * TC], lhsT=Wu[:, kc, fs], rhs=h2T[s][:, kc, :],
                                                       start=(kc == 0), stop=(kc == 7)),
                      reads=[b_Wu[ft // 2], b_h2T[s][kc]], writes=[b_M[bi]])
            gi = grot[0] % 2
            grot[0] += 1
            S.add("act", lambda e: e.activation(out=sg[gi][:, :], in_=M[bi][:, 0:TC], func=AF.Silu),
                  reads=[b_M[bi]], writes=[b_sg[gi]])
            S.add("dve", lambda e: e.tensor_tensor(out=actT[:, ft, :], in0=M[bi][:, TC:2 * TC], in1=sg[gi][:, :], op=ALU.mult),
                  reads=[b_M[bi], b_sg[gi]], writes=[b_actT[ft]])

        def down(t):
            s = t % 2
            a = t * TC
            for blk in range(TCB):
                for half in range(2):
                    bi = next_bank()
                    hs = slice(half * 512, (half + 1) * 512)
                    for fc in range(NFT):
                        S.add("pe", lambda e, fc=fc, blk=blk, hs=hs, bi=bi: e.matmul(
                            M[bi][:, :], lhsT=actT[:, fc, blk * 128:(blk + 1) * 128], rhs=Wd[:, fc, hs],
                            start=(fc == 0), stop=(fc == NFT - 1)),
                            reads=[b_actT[fc], b_Wd[fc]], writes=[b_M[bi]])
                    S.add("dve", lambda e, blk=blk, hs=hs, bi=bi: e.tensor_tensor(out=xbuf[s][:, blk, hs], in0=M[bi][:, :],
                                                                                   in1=xbuf[s][:, blk, hs], op=ALU.add),
                          reads=[b_M[bi], b_x[s][blk]], writes=[b_x[s][blk]])
                dst = out[a + blk * 128:a + (blk + 1) * 128, :]
                S.add("pool", lambda e, blk=blk, dst=dst: e.dma_start(out=dst, in_=xbuf[s][:, blk, :]),
                      reads=[b_x[s][blk]], dma=True)

        loads(0)
        weight_loads()
        loads(1)
        wout_part(0)
        transposes(0)
        for t in range(NT):
            for ft in range(NFT):
                gateup(t, ft)
                if ft == 7 and t + 1 < NT:
                    wout_part(t + 1)
                if ft == 15 and t + 1 < NT:
                    transposes(t + 1)
            down(t)
            if t + 2 < NT:
                loads(t + 2)
        S.flush(final=True)


_NC_CACHE = {}


def _host_consts(g_mix, conv_w, g_q, g_k, g_conv_out, g_attn_out, g_ffn):
    cols = np.zeros((128, C_N), np.float32)
    cols[:, C_GMIX:C_GMIX + 8] = g_mix.reshape(8, 128).T
    cols[:, C_GFFN:C_GFFN + 8] = g_ffn.reshape(8, 128).T
    cols[:, C_GCO:C_GCO + 4] = g_conv_out.reshape(4, 128).T
    cw = conv_w.reshape(3, 4, 128)
    cols[:, C_CW:C_CW + 12] = np.transpose(cw, (2, 1, 0)).reshape(128, 12)
    cols[:, C_GQ] = np.tile(g_q.reshape(64), 2)
    cols[:, C_GK] = np.tile(g_k.reshape(64), 2)
    gao = g_attn_out.reshape(8, 64).T
    cols[0:64, C_GAO:C_GAO + 8] = gao
    cols[64:128, C_GAO:C_GAO + 8] = gao
    return cols


def _static_consts():
    ident = np.eye(128, dtype=np.float32)
    k = np.arange(128)[:, None]
    q = np.arange(128)[None, :]
    U = (k >= q).astype(np.float32)
    L = (k <= q).astype(np.float32)
    masks = (np.concatenate([L, U, L, U, U, U, U, U, L, L, L, L], axis=1) - 1.0) * 30000.0
    return ident, masks


def kernel(x, g_mix, w_in, conv_w, g_q, g_k, g_conv_out, g_attn_out, w_out, g_ffn, w_gate, w_up, w_down,
           _debug=False, _cores=None, _phases="abc"):
    x = np.asarray(x, np.float32)
    f = lambda a: np.ascontiguousarray(np.asarray(a, np.float32))
    cols = _host_consts(f(g_mix), f(conv_w), f(g_q), f(g_k), f(g_conv_out), f(g_attn_out), f(g_ffn))
    ident, masks = _static_consts()
    key = (bool(_debug), _phases)
    if key not in _NC_CACHE:
        _NC_CACHE[key] = build(debug=_debug, phases=_phases)
    nc = _NC_CACHE[key]
    cores = list(range(NCORES)) if _cores is None else _cores
    shared = {"w_in": f(w_in)[0], "w_out": f(w_out)[0], "w_gate": f(w_gate)[0], "w_up": f(w_up)[0],
              "w_down": f(w_down)[0], "cols": cols, "ident": ident, "masks": masks}
    in_maps = []
    for b in cores:
        m = dict(shared)
        m["x"] = np.ascontiguousarray(x[b])
        in_maps.append(m)
    res = run_bass_kernel_spmd(nc, in_maps, core_ids=list(range(len(cores))))
    if _debug:
        return res.results
    return np.stack([r["out"] for r in res.results], axis=0).astype(np.float32)
```

```python
import numpy as np
import concourse.bass as bass
import concourse.mybir as mybir
from concourse.bass_utils import run_bass_kernel_spmd

F32 = mybir.dt.float32
BF16 = mybir.dt.bfloat16
AF = mybir.ActivationFunctionType
ALU = mybir.AluOpType

S_LEN = 8192
D = 1024
D_IN = 3072
D_FF = 2816
NFT = D_FF // 128
EPS = 1e-6
NCORES = 8
_LIMITS = {}
VROW = 8 * 65

C_GMIX, C_GFFN, C_GCO, C_CW, C_GQ, C_GK, C_GAO, C_N = 0, 8, 16, 20, 32, 33, 34, 42


class Buf:
    __slots__ = ("w", "r", "name")

    def __init__(self, name=""):
        self.w = None
        self.r = []
        self.name = name


class Op:
    __slots__ = ("eng", "fn", "deps", "sig", "val", "dma", "sem", "prev")


class Sched:
    ENGS = ("pe", "act", "dve", "pool", "sp")

    def __init__(self, nc, sems, dsems):
        self.nc = nc
        self.sem = sems
        self.dsem = dsems
        self.cnt = {e: 0 for e in sems}
        self.dcnt = {q: [0] * len(l) for q, l in dsems.items()}
        self.dnext = {q: 0 for q in dsems}
        self.waited = {e: {} for e in self.ENGS}
        self.ops = {e: [] for e in self.ENGS}
        self.bufs = []
        self.barrier_tokens = []

    def buf(self, name=""):
        b = Buf(name)
        self.bufs.append(b)
        return b

    def add(self, eng, fn, reads=(), writes=(), dma=False):
        o = Op()
        o.eng, o.fn, o.dma, o.sig, o.val, o.sem, o.prev = eng, fn, dma, False, None, None, 0
        deps = []
        seen = set()

        def adddep(d):
            if d is None or id(d) in seen:
                return
            seen.add(id(d))
            if (not d.dma) and (not dma) and d.eng == eng and eng == "pe":
                return
            deps.append(d)

        for b in reads:
            adddep(b.w)
        for b in writes:
            adddep(b.w)
            for r in b.r:
                adddep(r)
        o.deps = deps
        for d in deps:
            d.sig = True
        for b in reads:
            b.r.append(o)
        for b in writes:
            b.w = o
            b.r = []
        self.ops[eng].append(o)
        return o

    def flush(self, final=False):
        nc = self.nc
        tokens = []
        for e in self.ENGS:
            lastc = None
            for o in self.ops[e]:
                if not o.dma:
                    lastc = o
            if lastc is not None:
                lastc.sig = True
        for e in self.ENGS:
            for o in self.ops[e]:
                if o.dma:
                    k = self.dnext[e]
                    n = len(self.dsem[e])
                    i = k % n
                    o.sem = self.dsem[e][i]
                    o.prev = self.dcnt[e][i]
                    self.dcnt[e][i] += 16
                    o.val = self.dcnt[e][i]
                    self.dnext[e] = k + 1
                elif o.sig:
                    self.cnt[e] += 1
                    o.val = self.cnt[e]
                    o.sem = self.sem[e]
        for e in self.sem:
            if self.cnt[e] > 0:
                tokens.append((self.sem[e], self.cnt[e]))
        for q in self.dsem:
            for i, s in enumerate(self.dsem[q]):
                if self.dcnt[q][i] > 0:
                    tokens.append((s, self.dcnt[q][i]))

        def run(engname, e):
            W = self.waited[engname]
            for o in self.ops[engname]:
                for d in o.deps:
                    if W.get(id(d.sem), 0) < d.val:
                        e.wait_ge(d.sem, d.val)
                        W[id(d.sem)] = d.val
                if o.dma:
                    if o.prev > 0 and W.get(id(o.sem), 0) < o.prev:
                        e.wait_ge(o.sem, o.prev)
                        W[id(o.sem)] = o.prev
                    o.fn(e).then_inc(o.sem, 16)
                else:
                    ins = o.fn(e)
                    if o.sig:
                        ins.then_inc(o.sem, 1)
            for (s, v) in tokens:
                if W.get(id(s), 0) < v:
                    e.wait_ge(s, v)
                    W[id(s)] = v

        with nc.Block() as block:
            @block.tensor
            def _(e):
                run("pe", e)

            @block.scalar
            def _(e):
                run("act", e)

            @block.vector
            def _(e):
                run("dve", e)

            @block.gpsimd
            def _(e):
                run("pool", e)

            @block.sync
            def _(e):
                run("sp", e)

        self.ops = {e: [] for e in self.ENGS}
        for b in self.bufs:
            b.w = None
            b.r = []
        self.bufs = []


def build(debug=False, phases="abc"):
    nc = bass.Bass("TRN2", target_bir_lowering=False)
    x = nc.dram_tensor("x", [S_LEN, D], F32, kind="ExternalInput").ap()
    w_in = nc.dram_tensor("w_in", [D, D_IN], F32, kind="ExternalInput").ap()
    w_out = nc.dram_tensor("w_out", [D, D], F32, kind="ExternalInput").ap()
    w_gate = nc.dram_tensor("w_gate", [D, D_FF], F32, kind="ExternalInput").ap()
    w_up = nc.dram_tensor("w_up", [D, D_FF], F32, kind="ExternalInput").ap()
    w_down = nc.dram_tensor("w_down", [D_FF, D], F32, kind="ExternalInput").ap()
    cols_d = nc.dram_tensor("cols", [128, C_N], F32, kind="ExternalInput").ap()
    ident_d = nc.dram_tensor("ident", [128, 128], F32, kind="ExternalInput").ap()
    mask_d = nc.dram_tensor("masks", [128, 1536], F32, kind="ExternalInput").ap()
    out = nc.dram_tensor("out", [S_LEN, D], F32, kind="ExternalOutput").ap()
    sk = "ExternalOutput" if debug else "Internal"
    qkT_d = nc.dram_tensor("qkT_d", [8, 128, S_LEN], BF16, kind=sk).ap()
    V_d = nc.dram_tensor("V_d", [S_LEN, VROW], BF16, kind=sk).ap()
    mixT_d = nc.dram_tensor("mixT_d", [8, 128, S_LEN], BF16, kind=sk).ap()

    wo_b = nc.dram_tensor("wo_b", [D, D], BF16, kind="Internal").ap()
    wg_b = nc.dram_tensor("wg_b", [D, D_FF], BF16, kind="Internal").ap()
    wu_b = nc.dram_tensor("wu_b", [D, D_FF], BF16, kind="Internal").ap()
    wd_b = nc.dram_tensor("wd_b", [D_FF, D], BF16, kind="Internal").ap()
    wscr = (w_out, w_gate, w_up, w_down, wo_b, wg_b, wu_b, wd_b)

    from contextlib import ExitStack

    with ExitStack() as gstack:
        def sem(name):
            return gstack.enter_context(nc.semaphore(name))

        sems = {e: sem("s_" + e) for e in ("pe", "act", "dve", "pool")}
        dsems = {"sp": [sem("d_sp%d" % i) for i in range(8)],
                 "pool": [sem("d_pl%d" % i) for i in range(8)]}
        S = Sched(nc, sems, dsems)

        def gsb(name, shape, dt):
            return gstack.enter_context(nc.sbuf_tensor(name, shape, dt))

        cols = gsb("cols_sb", [128, C_N], F32)
        gq8 = gsb("gq8", [128, 1], F32)
        ident = gsb("ident_bf", [128, 128], BF16)
        masks = gsb("masks_bf", [128, 1536], BF16)
        ones_bf = gsb("ones_bf", [128, 128], BF16)
        bones_bf = gsb("bones_bf", [128, 128], BF16)
        ones_f = gsb("ones_f", [128, 64], F32)

        with ExitStack() as st:
            stage = st.enter_context(nc.sbuf_tensor("stage", [128, 1536 + 128], F32))
            b_cols, b_stage = S.buf(), S.buf()
            b_c = S.buf()
            S.add("sp", lambda e: e.dma_start(out=cols[:, :], in_=cols_d), writes=[b_cols], dma=True)
            S.add("sp", lambda e: e.dma_start(out=stage[:, 0:1536], in_=mask_d), writes=[b_stage], dma=True)
            S.add("sp", lambda e: e.dma_start(out=stage[:, 1536:1664], in_=ident_d), writes=[b_stage], dma=True)
            S.add("dve", lambda e: e.tensor_copy(out=masks[:, :], in_=stage[:, 0:1536]), reads=[b_stage], writes=[b_c])
            S.add("dve", lambda e: e.tensor_copy(out=ident[:, :], in_=stage[:, 1536:1664]), reads=[b_stage], writes=[b_c])
            S.add("dve", lambda e: e.tensor_scalar(out=gq8[:, :], in0=cols[:, C_GQ:C_GQ + 1], scalar1=0.125,
                                                     scalar2=None, op0=ALU.mult), reads=[b_cols], writes=[b_c])
            S.add("pool", lambda e: e.memset(ones_bf[:, :], 1.0), writes=[b_c])
            S.add("pool", lambda e: e.memset(bones_bf[:, :], 0.0), writes=[b_c])
            S.add("pool", lambda e: e.memset(bones_bf[0:64, 0:64], 1.0), writes=[b_c])
            S.add("pool", lambda e: e.memset(bones_bf[64:128, 64:128], 1.0), writes=[b_c])
            S.add("pool", lambda e: e.memset(ones_f[:, :], 1.0), writes=[b_c])
            S.flush()

        if "a" in phases:
            phase_a(nc, S, x, w_in, qkT_d, V_d, mixT_d, cols, gq8, ident, ones_bf, bones_bf,
                    wscr if "c" in phases else None)
        if "b" in phases:
            phase_b(nc, S, qkT_d, V_d, mixT_d, cols, masks, ones_bf, ones_f, None, ident)
        if "c" in phases:
            phase_c(nc, S, x, out, wo_b, wg_b, wu_b, wd_b, mixT_d, cols, ident,
                    precast=None if "a" in phases else wscr)
    return nc


def phase_a(nc, S, x, w_in, qkT_d, V_d, mixT_d, cols, gq8, ident, ones_bf, bones_bf, wscr=None):
    NT = _LIMITS.get('a', S_LEN // 512)
    with ExitStack_() as st:
        def sb(name, shape, dt):
            return st.enter_context(nc.sbuf_tensor(name, shape, dt))

        def ps(name, shape, dt=F32):
            return st.enter_context(nc.psum_tensor(name, shape, dt))

        Win = sb("Win", [128, 8, D_IN], BF16)
        xbuf = [sb("xbuf%d" % i, [128, 4, 1024], F32) for i in range(2)]
        junk = sb("junk", [128, 1024], BF16)
        ss = sb("ss", [128, 2, 4], F32)
        rstd = sb("rstd", [128, 2, 4], F32)
        h = sb("h", [128, 4, 1024], BF16)
        hT = [sb("hT%d" % i, [128, 8, 512], BF16) for i in range(2)]
        qko = [sb("qko%d" % i, [128, 8, 512], BF16) for i in range(2)]
        vst = [sb("vst%d" % i, [128, 4, VROW], BF16) for i in range(2)]
        mixo = [sb("mixo%d" % i, [128, 4, 512], BF16) for i in range(2)]
        usb = [sb("usb%d" % i, [128, 512], F32) for i in range(4)]
        gbsb = [sb("gbsb%d" % i, [128, 512], F32) for i in range(4)]
        pbuf = [sb("pbuf%d" % i, [128, 514], F32) for i in range(4)]
        cbuf = [sb("cbuf%d" % i, [128, 512], F32) for i in range(4)]
        ybuf = [sb("ybuf%d" % i, [128, 512], F32) for i in range(4)]
        NR = 3
        sqt = [sb("sqt%d" % i, [128, 512], BF16) for i in range(NR)]
        rst = [sb("rst%d" % i, [128, 512], F32) for i in range(NR)]
        rsc = sb("rsc", [128, 512], F32)

        psT = [ps("psT%d" % i, [128, 1024], BF16) for i in range(2)]
        SSqk = ps("SSqk", [128, 512])
        SSc = SSqk
        NM = 5
        M = [ps("M%d" % i, [128, 512]) for i in range(NM)]
        sqc = [sb("sqc%d" % i, [128, 512], BF16) for i in range(4)]

        B = S.buf
        b_Win = [B() for _ in range(8)]
        b_x = [[B() for _ in range(4)] for _ in range(2)]
        b_junk, b_ss, b_rstd = B(), [B(), B()], [B(), B()]
        b_h = [B() for _ in range(4)]
        b_hT = [[B() for _ in range(8)] for _ in range(2)]
        b_qko, b_vst, b_mixo = [B(), B()], [B(), B()], [B(), B()]
        b_usb, b_gbsb, b_p, b_c, b_y = ([B() for _ in range(4)] for _ in range(5))
        b_usb, b_gbsb, b_p, b_c, b_y = list(b_usb), list(b_gbsb), list(b_p), list(b_c), list(b_y)
        b_sq, b_rs = [B() for _ in range(NR)], [B() for _ in range(NR)]
        b_rsc = B()
        b_psT = [B(), B()]
        b_SSqk = B()
        b_SSc = b_SSqk
        b_sqc = [B() for _ in range(4)]
        b_M = [B() for _ in range(NM)]
        b_init = B()

        w_in_v = w_in.rearrange("(k p) n -> p k n", p=128)
        b_Win = [B() for _ in range(24)]
        order = []
        for f in range(4):
            order += [f, 8 + f, 4 + f]
        order += list(range(12, 24))
        for g in order:
            S.add("pool", lambda e, g=g: e.dma_start(out=Win[:, :, g * 128:(g + 1) * 128], in_=w_in_v[:, :, g * 128:(g + 1) * 128]),
                  writes=[b_Win[g]], dma=True)
        for s in range(2):
            S.add("pool", lambda e, s=s: e.memset(vst[s][:, :, :], 1.0), writes=[b_vst[s]])
        for f in range(4):
            S.add("pool", lambda e, f=f: e.memset(pbuf[f][:, 0:2], 0.0), writes=[b_p[f]])

        mrot = [0]
        rrot = [0]

        def next_bank():
            i = mrot[0] % NM
            mrot[0] += 1
            return i

        def load_x(t):
            s = t % 2
            src = x[t * 512:(t + 1) * 512, :].rearrange("(b p) d -> p b d", p=128)
            S.add("sp", lambda e: e.dma_start(out=xbuf[s][:, :, :], in_=src), writes=b_x[s], dma=True)

        def stats_steps(t):
            s = t % 2
            st_ = []
            for blk in range(4):
                st_.append(lambda blk=blk: S.add("act", lambda e: e.activation(out=junk[:, :], in_=xbuf[s][:, blk, :], func=AF.Square,
                                                                                accum_out=ss[:, s, blk:blk + 1]),
                                                 reads=[b_x[s][blk]], writes=[b_junk, b_ss[s]]))
            st_.append(lambda: S.add("act", lambda e: e.activation(out=rstd[:, s, :], in_=ss[:, s, :], func=AF.Ln, scale=1.0 / D, bias=EPS),
                                     reads=[b_ss[s]], writes=[b_rstd[s]]))
            st_.append(lambda: S.add("act", lambda e: e.activation(out=rstd[:, s, :], in_=rstd[:, s, :], func=AF.Exp, scale=-0.5),
                                     reads=[b_rstd[s]], writes=[b_rstd[s]]))
            for blk in range(4):
                if blk % 2 == 0:
                    st_.append(lambda blk=blk: S.add("act", lambda e: e.activation(out=h[:, blk, :], in_=xbuf[s][:, blk, :], func=AF.Copy,
                                                                                    scale=rstd[:, s, blk:blk + 1]),
                                                     reads=[b_x[s][blk], b_rstd[s]], writes=[b_h[blk]]))
                else:
                    st_.append(lambda blk=blk: S.add("dve", lambda e: e.tensor_scalar(out=h[:, blk, :], in0=xbuf[s][:, blk, :],
                                                                                       scalar1=rstd[:, s, blk:blk + 1], scalar2=None,
                                                                                       op0=ALU.mult),
                                                     reads=[b_x[s][blk], b_rstd[s]], writes=[b_h[blk]]))
            return st_

        def transpose_steps(t):
            s = t % 2
            st_ = []
            for kp in range(4):
                def rnd(kp=kp):
                    pb = kp % 2
                    for kk in range(2):
                        kc = 2 * kp + kk
                        for blk in range(4):
                            S.add("pe", lambda e, kc=kc, blk=blk, pb=pb, kk=kk: e.transpose(
                                out=psT[pb][:, kk * 512 + blk * 128: kk * 512 + (blk + 1) * 128],
                                in_=h[:, blk, kc * 128:(kc + 1) * 128], identity=ident[:, :]),
                                reads=[b_h[blk]], writes=[b_psT[pb]])
                    kc0, kc1 = 2 * kp, 2 * kp + 1
                    for (kc, lo) in ((kc0, 0), (kc1, 512)):
                        if pb == 0:
                            S.add("act", lambda e, kc=kc, lo=lo: e.activation(out=hT[s][:, kc, :], in_=psT[0][:, lo:lo + 512],
                                                                              func=AF.Copy, scale=cols[:, C_GMIX + kc:C_GMIX + kc + 1]),
                                  reads=[b_psT[0]], writes=[b_hT[s][kc]])
                        else:
                            S.add("dve", lambda e, kc=kc, lo=lo: e.tensor_scalar(out=hT[s][:, kc, :], in0=psT[1][:, lo:lo + 512],
                                                                                 scalar1=cols[:, C_GMIX + kc:C_GMIX + kc + 1],
                                                                                 scalar2=None, op0=ALU.mult),
                                  reads=[b_psT[1]], writes=[b_hT[s][kc]])
                st_.append(rnd)
            return st_

        asteps = []

        def pop_astep():
            if asteps:
                asteps.pop(0)()

        def group_fm(t, col0):
            s = t % 2
            bi = next_bank()
            for kc in range(8):
                S.add("pe", lambda e, kc=kc: e.matmul(M[bi][:, :], lhsT=Win[:, kc, col0:col0 + 128], rhs=hT[s][:, kc, :],
                                                       start=(kc == 0), stop=(kc == 7)),
                      reads=[b_Win[col0 // 128], b_hT[s][kc]], writes=[b_M[bi]])
            pop_astep()
            return bi

        pending = []

        def run_pending():
            while pending:
                pending.pop(0)()

        def conv_part(t, f):
            s = t % 2
            bu = group_fm(t, f * 128)
            S.add("act", lambda e: e.activation(out=usb[f][:, :], in_=M[bu][:, :], func=AF.Copy),
                  reads=[b_M[bu]], writes=[b_usb[f]])
            run_pending()
            bc = group_fm(t, 1024 + f * 128)
            S.add("dve", lambda e: e.tensor_tensor(out=pbuf[f][:, 2:514], in0=M[bc][:, :], in1=usb[f][:, :], op=ALU.mult),
                  reads=[b_M[bc], b_usb[f]], writes=[b_p[f]])
            bg = group_fm(t, 512 + f * 128)
            S.add("dve", lambda e: e.tensor_copy(out=gbsb[f][:, :], in_=M[bg][:, :]),
                  reads=[b_M[bg]], writes=[b_gbsb[f]])
            cw = C_CW + 3 * f
            S.add("act", lambda e: e.activation(out=cbuf[f][:, :], in_=pbuf[f][:, 0:512], func=AF.Copy,
                                                scale=cols[:, cw:cw + 1]),
                  reads=[b_p[f]], writes=[b_c[f]])
            S.add("dve", lambda e: e.scalar_tensor_tensor(out=cbuf[f][:, :], in0=pbuf[f][:, 1:513], scalar=cols[:, cw + 1:cw + 2],
                                                           in1=cbuf[f][:, :], op0=ALU.mult, op1=ALU.add),
                  reads=[b_p[f], b_c[f]], writes=[b_c[f]])
            S.add("dve", lambda e: e.scalar_tensor_tensor(out=cbuf[f][:, :], in0=pbuf[f][:, 2:514], scalar=cols[:, cw + 2:cw + 3],
                                                           in1=cbuf[f][:, :], op0=ALU.mult, op1=ALU.add),
                  reads=[b_p[f], b_c[f]], writes=[b_c[f]])
            S.add("act", lambda e: e.activation(out=pbuf[f][:, 0:2], in_=pbuf[f][:, 512:514], func=AF.Copy),
                  reads=[b_p[f]], writes=[b_p[f]])
            S.add("dve", lambda e: e.tensor_tensor(out=ybuf[f][:, :], in0=cbuf[f][:, :], in1=gbsb[f][:, :], op=ALU.mult),
                  reads=[b_c[f], b_gbsb[f]], writes=[b_y[f]])
            S.add("act", lambda e: e.activation(out=sqc[f][:, :], in_=ybuf[f][:, :], func=AF.Square),
                  reads=[b_y[f]], writes=[b_sqc[f]])

            def part2():
                if f == 3:
                    for ff in range(4):
                        S.add("pe", lambda e, ff=ff: e.matmul(SSc[:, :], lhsT=ones_bf[:, :], rhs=sqc[ff][:, :],
                                                               start=(ff == 0), stop=(ff == 3)),
                              reads=[b_sqc[ff]], writes=[b_SSc])
                    S.add("act", lambda e: e.activation(out=rsc[:, :], in_=SSc[:, :], func=AF.Ln, scale=1.0 / 512, bias=EPS),
                          reads=[b_SSc], writes=[b_rsc])
                    S.add("act", lambda e: e.activation(out=rsc[:, :], in_=rsc[:, :], func=AF.Exp, scale=-0.5),
                          reads=[b_rsc], writes=[b_rsc])
                    for ff in range(4):
                        S.add("dve", lambda e, ff=ff: e.scalar_tensor_tensor(
                            out=mixo[s][:, ff, :], in0=ybuf[ff][:, :], scalar=cols[:, C_GCO + ff:C_GCO + ff + 1],
                            in1=rsc[:, :], op0=ALU.mult, op1=ALU.mult),
                            reads=[b_y[ff], b_rsc], writes=[b_mixo[s]])
                    dst = mixT_d[0:4, :, t * 512:(t + 1) * 512].rearrange("f p t -> p f t")
                    S.add("pool", lambda e: e.dma_start(out=dst, in_=mixo[s][:, :, :]), reads=[b_mixo[s]], dma=True)
            pending.append(part2)

        def qk_part(t, ft):
            s = t % 2
            bi = group_fm(t, 1536 + ft * 128)
            ri = rrot[0] % NR
            rrot[0] += 1
            S.add("act", lambda e: e.activation(out=sqt[ri][:, :], in_=M[bi][:, :], func=AF.Square),
                  reads=[b_M[bi]], writes=[b_sq[ri]])
            run_pending()
            gcol = gq8[:, 0:1] if ft < 4 else cols[:, C_GK:C_GK + 1]

            def part2():
                S.add("pe", lambda e: e.matmul(SSqk[:, :], lhsT=bones_bf[:, :], rhs=sqt[ri][:, :], start=True, stop=True),
                      reads=[b_sq[ri]], writes=[b_SSqk])
                S.add("act", lambda e: e.activation(out=rst[ri][:, :], in_=SSqk[:, :], func=AF.Ln, scale=1.0 / 64, bias=EPS),
                      reads=[b_SSqk], writes=[b_rs[ri]])
                S.add("act", lambda e: e.activation(out=rst[ri][:, :], in_=rst[ri][:, :], func=AF.Exp, scale=-0.5),
                      reads=[b_rs[ri]], writes=[b_rs[ri]])
                S.add("dve", lambda e: e.scalar_tensor_tensor(out=qko[s][:, ft, :], in0=M[bi][:, :], scalar=gcol,
                                                              in1=rst[ri][:, :], op0=ALU.mult, op1=ALU.mult),
                      reads=[b_M[bi], b_rs[ri]], writes=[b_qko[s]])
                if ft == 7:
                    dst = qkT_d[:, :, t * 512:(t + 1) * 512].rearrange("j p t -> p j t")
                    S.add("pool", lambda e: e.dma_start(out=dst, in_=qko[s][:, :, :]), reads=[b_qko[s]], dma=True)
            pending.append(part2)

        def v_part(t, blk):
            s = t % 2
            bi = next_bank()
            for kc in range(8):
                S.add("pe", lambda e, kc=kc: e.matmul(M[bi][:, :], lhsT=hT[s][:, kc, blk * 128:(blk + 1) * 128],
                                                       rhs=Win[:, kc, 2560:3072], start=(kc == 0), stop=(kc == 7)),
                      reads=[b_Win[20], b_Win[21], b_Win[22], b_Win[23], b_hT[s][kc]], writes=[b_M[bi]])
            pop_astep()
            dstv = vst[s][:, blk, :].rearrange("p (h e) -> p h e", e=65)[:, :, 0:64]
            srcv = M[bi][:, :].rearrange("p (h d) -> p h d", d=64)
            S.add("act", lambda e: e.activation(out=dstv, in_=srcv, func=AF.Copy), reads=[b_M[bi]], writes=[b_vst[s]])
            run_pending()
            if blk == 3:
                dst = V_d[t * 512:(t + 1) * 512, :].rearrange("(b p) e -> p b e", p=128)
                S.add("pool", lambda e: e.dma_start(out=dst, in_=vst[s][:, :, :]), reads=[b_vst[s]], dma=True)

        pc_list = precast_list(wscr) if wscr is not None else []
        load_x(0)
        if NT > 1:
            load_x(1)
        for f_ in stats_steps(0) + transpose_steps(0):
            f_()
        for t in range(NT):
            if t + 1 < NT:
                asteps.extend(stats_steps(t + 1) + transpose_steps(t + 1))
            for f in range(4):
                conv_part(t, f)
            for ft in range(8):
                qk_part(t, ft)
            for blk in range(4):
                v_part(t, blk)
            run_pending()
            while asteps:
                asteps.pop(0)()
            if t + 2 < NT:
                load_x(t + 2)
            if wscr is not None and t >= 1:
                emit_precast(S, None, n=2, lst=pc_list)
        if wscr is not None:
            emit_precast(S, None, lst=pc_list)
        S.flush()


def ExitStack_():
    from contextlib import ExitStack
    return ExitStack()


def precast_list(wscr):
    w_out, w_gate, w_up, w_down, wo_b, wg_b, wu_b, wd_b = wscr
    lst = []
    for (src_, dst_, rows) in ((w_out, wo_b, D), (w_gate, wg_b, D), (w_up, wu_b, D), (w_down, wd_b, D_FF)):
        step = 256
        for r in range(0, rows, step):
            lst.append((dst_[r:r + step, :], src_[r:r + step, :]))
    return lst


def emit_precast(S, wscr, n=None, lst=None):
    lst = precast_list(wscr) if lst is None else lst
    k = 0
    while lst and (n is None or k < n):
        d_, s_ = lst.pop(0)
        S.add("pool", lambda e, d_=d_, s_=s_: e.dma_start(out=d_, in_=s_), dma=True)
        k += 1


def phase_b(nc, S, qkT_d, V_d, mixT_d, cols, masks, ones_bf, ones_f, wscr=None, ident=None):
    NCH = _LIMITS.get('b', S_LEN // 2048)
    with ExitStack_() as st:
        def sb(name, shape, dt):
            return st.enter_context(nc.sbuf_tensor(name, shape, dt))

        def ps(name, shape, dt=F32):
            return st.enter_context(nc.psum_tensor(name, shape, dt))

        qj = [sb("qz%d" % i, [128, 2, 2048], BF16) for i in range(2)]
        kj = [sb("kj%d" % i, [128, 2, 2048], BF16) for i in range(2)]
        vnat = sb("vnat", [128, 17, VROW], BF16)
        vd4 = sb("vd4", [128, 20, VROW], BF16)
        vd16 = [sb("vd16_%d" % i, [128, 16, VROW], BF16) for i in range(2)]
        NE = 3
        Et = [sb("E%d" % i, [128, 512], BF16) for i in range(NE)]
        num = sb("num", [128, 2048], F32)
        yT = sb("yT", [64, 8, 2048], F32)
        bcs = sb("bcs", [64, 2048], F32)
        ysqacc = sb("ysqacc", [64, 2048], F32)
        rsa = sb("rsa", [64, 512], F32)
        mao = sb("mao", [64, 8, 512], BF16)

        acc = ps("acc", [128, 2048])
        NST = 3
        ST = [ps("ST%d" % i, [128, 512]) for i in range(NST)]
        SSa = [ps("SSa%d" % i, [128, 512]) for i in range(1)]

        B = S.buf
        b_qj, b_kj = [B(), B()], [B(), B()]
        b_vnat, b_vd4, b_vd16 = B(), B(), [B(), B()]
        b_E = [B() for _ in range(NE)]
        b_num, b_yT = B(), [B() for _ in range(8)]
        b_bcs, b_ysqacc, b_rsa, b_mao = B(), B(), B(), B()
        b_rden = [B(), B()]
        rden_d = nc.dram_tensor("rden_d", [2, 2048], F32, kind="Internal").ap()
        b_acc, b_ST, b_SSa = [B() for _ in range(4)], [B() for _ in range(NST)], [B()]
        pending_epi = []
        steps = []
        urgent = []

        erot = [0]
        srot = [0]
        prot = [0]
        for s_ in range(2):
            S.add("pool", lambda e, s_=s_: e.memset(qj[s_][64:128, 0, :], 0.0), writes=[b_qj[s_]])
            S.add("pool", lambda e, s_=s_: e.memset(qj[s_][0:64, 1, :], 0.0), writes=[b_qj[s_]])

        def sl(a, n, step):
            return slice(a, a + (n - 1) * step + 1, step)

        def load_pair(c, j):
            s = prot[0] % 2
            prot[0] += 1
            t0 = c * 2048
            S.add("sp", lambda e: e.dma_start(out=qj[s][0:64, 0, :], in_=qkT_d[j, 0:64, t0:t0 + 2048]), writes=[b_qj[s]], dma=True)
            S.add("sp", lambda e: e.dma_start(out=qj[s][64:128, 1, :], in_=qkT_d[j, 64:128, t0:t0 + 2048]), writes=[b_qj[s]], dma=True)
            if c > 0:
                S.add("sp", lambda e: e.dma_start(out=kj[s][:, :, :],
                                                  in_=qkT_d[4 + j, :, t0 - 2048:t0 + 2048].rearrange("p (c t) -> p c t", c=2)),
                      writes=[b_kj[s]], dma=True)
            else:
                S.add("sp", lambda e: e.dma_start(out=kj[s][:, 1, :], in_=qkT_d[4 + j, :, t0:t0 + 2048]),
                      writes=[b_kj[s]], dma=True)
            return s

        for c in range(NCH):
            t0 = c * 2048
            cp = c % 2
            pp = (c - 1) % 2
            ps_first = load_pair(c, 0)
            if c == 0:
                S.add("sp", lambda e: e.dma_start(out=vnat[:, 1:17, :], in_=V_d[0:2048, :].rearrange("(b p) e -> p b e", p=128)),
                      writes=[b_vnat], dma=True)
                S.add("sp", lambda e: e.dma_start(out=vd4[:, 4:20, :].rearrange("p (t r) e -> p t r e", r=4),
                                                  in_=V_d[0:2048, :].rearrange("(t p r) e -> p t r e", p=128, r=4)),
                      writes=[b_vd4], dma=True)
            else:
                S.add("sp", lambda e, t0=t0: e.dma_start(out=vnat[:, 0:17, :],
                                                         in_=V_d[t0 - 128:t0 + 2048, :].rearrange("(b p) e -> p b e", p=128)),
                      writes=[b_vnat], dma=True)
                S.add("sp", lambda e, t0=t0: e.dma_start(out=vd4[:, 0:20, :].rearrange("p (t r) e -> p t r e", r=4),
                                                         in_=V_d[t0 - 512:t0 + 2048, :].rearrange("(t p r) e -> p t r e", p=128, r=4)),
                      writes=[b_vd4], dma=True)
            S.add("sp", lambda e, t0=t0, cp=cp: e.dma_start(out=vd16[cp][:, :, :],
                                                            in_=V_d[t0:t0 + 2048, :].rearrange("(p r) e -> p r e", r=16)),
                  writes=[b_vd16[cp]], dma=True)

            items = []
            if c > 0:
                items.append((1, 1, 0, sl(1920, 128, 1), vnat, b_vnat, 0, [(0, "U")]))
            for b in range(16):
                qb = [(128 * b, "L")]
                if b < 15:
                    qb.append((128 * (b + 1), "U"))
                items.append((1, 1, 1, sl(128 * b, 128, 1), vnat, b_vnat, b + 1, qb))
            if c > 0:
                for r in range(4):
                    items.append((4, 4, 0, sl(1536 + r, 128, 4), vd4, b_vd4, r, [(r, "U")]))
            for t in range(4):
                for r in range(4):
                    qb = [(512 * t + r, "L")]
                    if t < 3:
                        qb.append((512 * (t + 1) + r, "U"))
                    items.append((4, 4, 1, sl(512 * t + r, 128, 4), vd4, b_vd4, 4 + 4 * t + r, qb))
            if c > 0:
                for r in range(16):
                    items.append((16, 16, 0, sl(r, 128, 16), vd16[pp], b_vd16[pp], r, [(r, "U")]))
            for r in range(16):
                items.append((16, 16, 1, sl(r, 128, 16), vd16[cp], b_vd16[cp], r, [(r, "L")]))

            def cls_of(it):
                kinds = "".join(k for (_q, k) in it[7])
                return kinds
            banks = []
            i = 0
            while i < len(items):
                cl = cls_of(items[i])
                cap = 2 if cl == "LU" else 4
                grp = [items[i]]
                jj = i + 1
                while jj < len(items) and len(grp) < cap and cls_of(items[jj]) == cl:
                    grp.append(items[jj])
                    jj += 1
                banks.append((cl, grp))
                i = jj
            nb = len(banks)

            def make_head(hd, ps_):
                started = [False] * 4

                def emit_st(bi_, hsel=hd % 2, ps_=ps_):
                    cl, grp = banks[bi_]
                    si = srot[0] % NST
                    srot[0] += 1
                    off = 0
                    for (dil, qstep, kc_, ks, vt, vb, vslot, qb) in grp:
                        npart = len(qb)
                        q0 = qb[0][0]
                        if npart == 1:
                            rhs = qj[ps_][:, hsel, slice(q0, q0 + 127 * qstep + 1, qstep)]
                            oap = ST[si][:, off:off + 128]
                        elif dil == 1:
                            rhs = qj[ps_][:, hsel, q0:q0 + 256]
                            oap = ST[si][:, off:off + 256]
                        else:
                            t_, r_ = q0 // 512, q0 % 512
                            rhs = qj[ps_][:, hsel, :].rearrange("p (t m r) -> p t m r", t=4, r=4)[:, t_:t_ + 2, :, r_]
                            oap = ST[si][:, off:off + 256].rearrange("p (a m) -> p a m", a=2)
                        S.add("pe", lambda e, kc_=kc_, ks=ks, rhs=rhs, oap=oap, first=(off == 0): e.matmul(
                            oap, lhsT=kj[ps_][:, kc_, ks], rhs=rhs, start=first, stop=False, skip_group_check=True),
                            reads=[b_kj[ps_], b_qj[ps_]], writes=[b_ST[si]])
                        off += 128 * npart
                    moff = {"LU": 0, "U": 512, "L": 1024}[cl]
                    S.add("pe", lambda e, n=off, moff=moff: e.matmul(ST[si][:, 0:n], lhsT=ident[:, :], rhs=masks[:, moff:moff + n],
                                                                      start=False, stop=True, skip_group_check=True),
                          writes=[b_ST[si]])
                    return si

                def emit_rest(bi_, si, hd=hd, started=started):
                    cl, grp = banks[bi_]
                    n = sum(len(it[7]) for it in grp) * 128
                    ei = erot[0] % NE
                    erot[0] += 1
                    S.add("act", lambda e: e.activation(out=Et[ei][:, 0:n], in_=ST[si][:, 0:n], func=AF.Exp),
                          reads=[b_ST[si]], writes=[b_E[ei]])
                    off = 0
                    for (dil, qstep, kc_, ks, vt, vb, vslot, qb) in grp:
                        vap = vt[:, vslot, hd * 65:(hd + 1) * 65]
                        segs = []
                        if dil == 16:
                            for (q0, _k) in qb:
                                for k in range(4):
                                    segs.append((k, 512 * k + q0, off + 32 * k, 32, 16))
                                off += 128
                        elif dil == 1 and len(qb) == 2 and (qb[0][0] // 512) == (qb[1][0] // 512):
                            segs.append((qb[0][0] // 512, qb[0][0], off, 256, 1))
                            off += 256
                        else:
                            for (q0, _k) in qb:
                                segs.append((q0 // 512, q0, off, 128, qstep))
                                off += 128
                        for (bk, oc, eo, nq, ostep) in segs:
                            first = not started[bk]
                            started[bk] = True
                            oslice = slice(oc, oc + (nq - 1) * ostep + 1, ostep)
                            S.add("pe", lambda e, vap=vap, eo=eo, nq=nq, oslice=oslice, first=first: e.matmul(
                                acc[0:65, oslice], lhsT=vap, rhs=Et[ei][:, eo:eo + nq],
                                start=first, stop=False, skip_group_check=True),
                                reads=[vb, b_E[ei]], writes=[b_acc[bk]])


                def head_end():
                    while urgent:
                        urgent.pop(0)()
                    while steps:
                        f_ = steps.pop(0)
                        if f_ is not None:
                            f_()
                    def evac(k):
                        S.add("act", lambda e: e.activation(out=num[0:65, 512 * k:512 * (k + 1)],
                                                            in_=acc[0:65, 512 * k:512 * (k + 1)], func=AF.Copy),
                              reads=[b_acc[k]], writes=[b_num])
                    evac(0)
                    for k in (1, 2, 3):
                        urgent.append(lambda k=k, evac=evac: evac(k))
                    rs_ = hd % 2

                    def mk_steps(hd=hd, rs_=rs_):
                        st_ = []
                        for k in range(4):
                            cs = slice(512 * k, 512 * (k + 1))
                            st_.append(lambda cs=cs: S.add("act", lambda e: e.activation(out=num[64:65, cs], in_=num[64:65, cs], func=AF.Ln),
                                                           reads=[b_num], writes=[b_num]))
                        for k in range(4):
                            cs = slice(512 * k, 512 * (k + 1))
                            st_.append(lambda cs=cs: S.add("act", lambda e: e.activation(out=num[64:65, cs], in_=num[64:65, cs], func=AF.Exp,
                                                                                          scale=-1.0),
                                                           reads=[b_num], writes=[b_num]))
                        st_.append(lambda: S.add("sp", lambda e: e.dma_start(out=rden_d[rs_:rs_ + 1, :], in_=num[64:65, :]),
                                                 reads=[b_num], writes=[b_rden[rs_]], dma=True))
                        st_.append(lambda: S.add("sp", lambda e: e.dma_start(out=bcs[0:64, :],
                                                                             in_=rden_d[rs_:rs_ + 1, :].partition_broadcast(64)),
                                                 reads=[b_rden[rs_]], writes=[b_bcs], dma=True))
                        st_ += [None] * 9
                        for k in range(4):
                            cs = slice(512 * k, 512 * (k + 1))
                            st_.append(lambda cs=cs: S.add("dve", lambda e: e.tensor_tensor(out=yT[:, hd, cs], in0=num[0:64, cs],
                                                                                           in1=bcs[0:64, cs], op=ALU.mult),
                                                           reads=[b_num, b_bcs], writes=[b_yT[hd]]))
                        for k in range(4):
                            cs = slice(512 * k, 512 * (k + 1))
                            if hd == 0:
                                st_.append(lambda cs=cs: S.add("act", lambda e: e.activation(out=ysqacc[:, cs], in_=yT[:, hd, cs], func=AF.Square),
                                                               reads=[b_yT[hd]], writes=[b_ysqacc]))
                            else:
                                st_.append(lambda cs=cs: S.add("act", lambda e: e.activation(out=bcs[:, cs], in_=yT[:, hd, cs], func=AF.Square),
                                                               reads=[b_yT[hd]], writes=[b_bcs]))
                        if hd != 0:
                            for k in range(4):
                                cs = slice(512 * k, 512 * (k + 1))
                                st_.append(lambda cs=cs: S.add("dve", lambda e: e.tensor_tensor(out=ysqacc[:, cs], in0=ysqacc[:, cs],
                                                                                               in1=bcs[:, cs], op=ALU.add),
                                                               reads=[b_bcs, b_ysqacc], writes=[b_ysqacc]))
                        return st_
                    steps.extend(mk_steps())


                return emit_st, emit_rest, head_end

            slots = {0: ps_first}
            ctx = {}
            inflight = []
            for hd in range(8):
                j = hd // 2
                for bi_ in range(nb):
                    if bi_ == 0:
                        if hd % 2 == 0 and j < 3:
                            slots[j + 1] = load_pair(c, j + 1)
                        ctx[hd] = make_head(hd, slots[j])
                    si = ctx[hd][0](bi_)
                    inflight.append((hd, bi_, si))
                    if len(inflight) > 2:
                        h0, b0_, s0 = inflight.pop(0)
                        ctx[h0][1](b0_, s0)
                        if b0_ == nb - 1:
                            ctx[h0][2]()
                    if urgent:
                        urgent.pop(0)()
                    if bi_ >= 2:
                        for _ in range(3):
                            if steps:
                                f_ = steps.pop(0)
                                if f_ is not None:
                                    f_()
            while inflight:
                h0, b0_, s0 = inflight.pop(0)
                ctx[h0][1](b0_, s0)
                if b0_ == nb - 1:
                    ctx[h0][2]()

            def mk_epi(t0=t0):
                st_ = []
                for k in range(4):
                    cs = slice(512 * k, 512 * (k + 1))
                    def ssq_step(cs=cs):
                        S.add("pe", lambda e: e.matmul(SSa[0][0:64, :], lhsT=ones_f[0:64, 0:64], rhs=ysqacc[:, cs], start=True, stop=True),
                              reads=[b_ysqacc], writes=[b_SSa[0]])
                        S.add("act", lambda e: e.activation(out=rsa[:, :], in_=SSa[0][0:64, :], func=AF.Ln, scale=1.0 / 512, bias=EPS),
                              reads=[b_SSa[0]], writes=[b_rsa])
                        S.add("act", lambda e: e.activation(out=rsa[:, :], in_=rsa[:, :], func=AF.Exp, scale=-0.5),
                              reads=[b_rsa], writes=[b_rsa])
                    st_.append(ssq_step)
                    st_.append(None)
                    def stt2(h0, cs=cs):
                        for hd in (h0, h0 + 1):
                            S.add("dve", lambda e, hd=hd: e.scalar_tensor_tensor(
                                out=mao[:, hd, :], in0=yT[:, hd, cs], scalar=cols[0:64, C_GAO + hd:C_GAO + hd + 1],
                                in1=rsa[:, :], op0=ALU.mult, op1=ALU.mult),
                                reads=[b_yT[hd], b_rsa], writes=[b_mao])
                    for h0 in (0, 2, 4, 6):
                        st_.append(lambda h0=h0, stt2=stt2: stt2(h0))
                    dst = mixT_d[4:8, :, t0 + 512 * k:t0 + 512 * (k + 1)].rearrange("f (h d) t -> d (f h) t", h=2)
                    st_.append(lambda dst=dst: S.add("pool", lambda e: e.dma_start(out=dst, in_=mao[:, :, :]), reads=[b_mao], dma=True))
                return st_
            steps.extend(mk_epi())
        while urgent:
            urgent.pop(0)()
        while steps:
            f_ = steps.pop(0)
            if f_ is not None:
                f_()
        S.flush()


def phase_c(nc, S, x, out, w_out, w_gate, w_up, w_down, mixT_d, cols, ident, precast=None):
    if precast is not None:
        emit_precast(S, precast)
        S.flush()
    TCB = 2
    TC = TCB * 128
    NT = _LIMITS.get('c', S_LEN // TC)
    with ExitStack_() as st:
        def sb(name, shape, dt):
            return st.enter_context(nc.sbuf_tensor(name, shape, dt))

        def ps(name, shape, dt=F32):
            return st.enter_context(nc.psum_tensor(name, shape, dt))

        Wo = sb("Wo", [128, 8, D], BF16)
        Wg = sb("Wg", [128, 8, D_FF], BF16)
        Wu = sb("Wu", [128, 8, D_FF], BF16)
        Wd = sb("Wd", [128, NFT, D], BF16)
        xbuf = [sb("xc%d" % i, [128, TCB, 1024], F32) for i in range(2)]
        mixT = [sb("mixT%d" % i, [128, 8, TC], BF16) for i in range(2)]
        h2 = sb("h2", [128, TCB, 1024], BF16)
        h2T = [sb("h2T%d" % i, [128, 8, TC], BF16) for i in range(2)]
        actT = sb("actT", [128, NFT, TC], BF16)
        sg = [sb("sg%d" % i, [128, TC], F32) for i in range(2)]
        junk = sb("junkc", [128, 1024], BF16)
        ss = sb("ssc", [128, 2, TCB], F32)
        rstd = sb("rstdc", [128, 2, TCB], F32)

        psT = [ps("psTc%d" % i, [128, 1024], BF16) for i in range(2)]
        NM = 6
        M = [ps("Mc%d" % i, [128, 512]) for i in range(NM)]

        B = S.buf
        b_Wo = [B() for _ in range(8)]
        b_Wg = [B() for _ in range(8)]
        b_Wu = [B() for _ in range(8)]
        b_Wd = [B() for _ in range(NFT)]
        b_x = [[B() for _ in range(TCB)] for _ in range(2)]
        b_mixT = [B(), B()]
        b_h2 = [B() for _ in range(TCB)]
        b_h2T = [[B() for _ in range(8)] for _ in range(2)]
        b_actT = [B() for _ in range(NFT)]
        b_sg = [B(), B()]
        b_junk, b_ss, b_rstd = B(), [B(), B()], [B(), B()]
        b_psT = [B(), B()]
        b_M = [B() for _ in range(NM)]

        wo_v = w_out.rearrange("(k p) n -> p k n", p=128)
        wg_v = w_gate.rearrange("(k p) n -> p k n", p=128)
        wu_v = w_up.rearrange("(k p) n -> p k n", p=128)
        wd_v = w_down.rearrange("(k p) n -> p k n", p=128)
        b_Wg = [B() for _ in range(NFT // 2)]
        b_Wu = [B() for _ in range(NFT // 2)]

        def weight_loads():
            for kc in range(8):
                S.add("sp", lambda e, kc=kc: e.dma_start(out=Wo[:, kc, :], in_=wo_v[:, kc, :]), writes=[b_Wo[kc]], dma=True)
            for cg in range(NFT // 2):
                cs = slice(cg * 256, (cg + 1) * 256)
                S.add("sp", lambda e, cs=cs: e.dma_start(out=Wg[:, :, cs], in_=wg_v[:, :, cs]), writes=[b_Wg[cg]], dma=True)
                S.add("sp", lambda e, cs=cs: e.dma_start(out=Wu[:, :, cs], in_=wu_v[:, :, cs]), writes=[b_Wu[cg]], dma=True)
            for fc in range(NFT):
                S.add("sp", lambda e, fc=fc: e.dma_start(out=Wd[:, fc, :], in_=wd_v[:, fc, :]), writes=[b_Wd[fc]], dma=True)

        mrot = [0]
        grot = [0]

        def next_bank():
            i = mrot[0] % NM
            mrot[0] += 1
            return i

        def loads(t):
            s = t % 2
            a = t * TC
            S.add("sp", lambda e: e.dma_start(out=mixT[s][:, :, :], in_=mixT_d[:, :, a:a + TC].rearrange("k p t -> p k t")),
                  writes=[b_mixT[s]], dma=True)
            for blk in range(TCB):
                src = x[a + blk * 128:a + (blk + 1) * 128, :]
                S.add("sp", lambda e, blk=blk, src=src: e.dma_start(out=xbuf[s][:, blk, :], in_=src),
                      writes=[b_x[s][blk]], dma=True)

        def wout_part(t):
            s = t % 2
            for blk in range(TCB):
                for half in range(2):
                    bi = next_bank()
                    hs = slice(half * 512, (half + 1) * 512)
                    for kc in range(8):
                        S.add("pe", lambda e, kc=kc, blk=blk, hs=hs, bi=bi: e.matmul(
                            M[bi][:, :], lhsT=mixT[s][:, kc, blk * 128:(blk + 1) * 128], rhs=Wo[:, kc, hs],
                            start=(kc == 0), stop=(kc == 7)),
                            reads=[b_mixT[s], b_Wo[kc]], writes=[b_M[bi]])
                    S.add("dve", lambda e, blk=blk, hs=hs, bi=bi: e.tensor_tensor(out=xbuf[s][:, blk, hs], in0=M[bi][:, :],
                                                                                   in1=xbuf[s][:, blk, hs], op=ALU.add),
                          reads=[b_M[bi], b_x[s][blk]], writes=[b_x[s][blk]])
            for blk in range(TCB):
                S.add("act", lambda e, blk=blk: e.activation(out=junk[:, :], in_=xbuf[s][:, blk, :], func=AF.Square,
                                                              accum_out=ss[:, s, blk:blk + 1]),
                      reads=[b_x[s][blk]], writes=[b_junk, b_ss[s]])
            S.add("act", lambda e: e.activation(out=rstd[:, s, :], in_=ss[:, s, :], func=AF.Sqrt, scale=1.0 / D, bias=EPS),
                  reads=[b_ss[s]], writes=[b_rstd[s]])
            S.add("dve", lambda e: e.reciprocal(out=rstd[:, s, :], in_=rstd[:, s, :]), reads=[b_rstd[s]], writes=[b_rstd[s]])
            for blk in range(TCB):
                if blk % 2 == 0:
                    S.add("act", lambda e, blk=blk: e.activation(out=h2[:, blk, :], in_=xbuf[s][:, blk, :], func=AF.Copy,
                                                                  scale=rstd[:, s, blk:blk + 1]),
                          reads=[b_x[s][blk], b_rstd[s]], writes=[b_h2[blk]])
                else:
                    S.add("dve", lambda e, blk=blk: e.tensor_scalar(out=h2[:, blk, :], in0=xbuf[s][:, blk, :],
                                                                     scalar1=rstd[:, s, blk:blk + 1], scalar2=None, op0=ALU.mult),
                          reads=[b_x[s][blk], b_rstd[s]], writes=[b_h2[blk]])

        def transposes(t):
            s = t % 2
            for kp in range(4):
                pb = kp % 2
                for kk in range(2):
                    kc = 2 * kp + kk
                    for blk in range(TCB):
                        S.add("pe", lambda e, kc=kc, blk=blk, pb=pb, kk=kk: e.transpose(
                            out=psT[pb][:, kk * 512 + blk * 128: kk * 512 + (blk + 1) * 128],
                            in_=h2[:, blk, kc * 128:(kc + 1) * 128], identity=ident[:, :]),
                            reads=[b_h2[blk]], writes=[b_psT[pb]])
                kc0, kc1 = 2 * kp, 2 * kp + 1
                S.add("act", lambda e, kc=kc0, pb=pb: e.activation(out=h2T[s][:, kc, :], in_=psT[pb][:, 0:TC],
                                                                   func=AF.Copy, scale=cols[:, C_GFFN + kc:C_GFFN + kc + 1]),
                      reads=[b_psT[pb]], writes=[b_h2T[s][kc0]])
                S.add("act", lambda e, kc=kc1, pb=pb: e.activation(out=h2T[s][:, kc, :], in_=psT[pb][:, 512:512 + TC],
                                                                   func=AF.Copy, scale=cols[:, C_GFFN + kc:C_GFFN + kc + 1]),
                      reads=[b_psT[pb]], writes=[b_h2T[s][kc1]])

        def gateup(t, ft):
            s = t % 2
            bi = next_bank()
            fs = slice(ft * 128, (ft + 1) * 128)
            for kc in range(8):
                S.add("pe", lambda e, kc=kc: e.matmul(M[bi][:, 0:TC], lhsT=Wg[:, kc, fs], rhs=h2T[s][:, kc, :],
                                                       start=(kc == 0), stop=(kc == 7)),
                      reads=[b_Wg[ft // 2], b_h2T[s][kc]], writes=[b_M[bi]])
            for kc in range(8):
                S.add("pe", lambda e, kc=kc: e.matmul(M[bi][:, TC:2 * TC], lhsT=Wu[:, kc, fs], rhs=h2T[s][:, kc, :],
                                                       start=(kc == 0), stop=(kc == 7)),
                      reads=[b_Wu[ft // 2], b_h2T[s][kc]], writes=[b_M[bi]])
            gi = grot[0] % 2
            grot[0] += 1
            S.add("act", lambda e: e.activation(out=sg[gi][:, :], in_=M[bi][:, 0:TC], func=AF.Silu),
                  reads=[b_M[bi]], writes=[b_sg[gi]])
            S.add("dve", lambda e: e.tensor_tensor(out=actT[:, ft, :], in0=M[bi][:, TC:2 * TC], in1=sg[gi][:, :], op=ALU.mult),
                  reads=[b_M[bi], b_sg[gi]], writes=[b_actT[ft]])

        def down(t):
            s = t % 2
            a = t * TC
            for blk in range(TCB):
                for half in range(2):
                    bi = next_bank()
                    hs = slice(half * 512, (half + 1) * 512)
                    for fc in range(NFT):
                        S.add("pe", lambda e, fc=fc, blk=blk, hs=hs, bi=bi: e.matmul(
                            M[bi][:, :], lhsT=actT[:, fc, blk * 128:(blk + 1) * 128], rhs=Wd[:, fc, hs],
                            start=(fc == 0), stop=(fc == NFT - 1)),
                            reads=[b_actT[fc], b_Wd[fc]], writes=[b_M[bi]])
                    S.add("dve", lambda e, blk=blk, hs=hs, bi=bi: e.tensor_tensor(out=xbuf[s][:, blk, hs], in0=M[bi][:, :],
                                                                                   in1=xbuf[s][:, blk, hs], op=ALU.add),
                          reads=[b_M[bi], b_x[s][blk]], writes=[b_x[s][blk]])
                dst = out[a + blk * 128:a + (blk + 1) * 128, :]
                S.add("pool", lambda e, blk=blk, dst=dst: e.dma_start(out=dst, in_=xbuf[s][:, blk, :]),
                      reads=[b_x[s][blk]], dma=True)

        loads(0)
        weight_loads()
        loads(1)
        wout_part(0)
        transposes(0)
        for t in range(NT):
            for ft in range(NFT):
                gateup(t, ft)
                if ft == 7 and t + 1 < NT:
                    wout_part(t + 1)
                if ft == 15 and t + 1 < NT:
                    transposes(t + 1)
            down(t)
            if t + 2 < NT:
                loads(t + 2)
        S.flush(final=True)


_NC_CACHE = {}


def _host_consts(g_mix, conv_w, g_q, g_k, g_conv_out, g_attn_out, g_ffn):
    cols = np.zeros((128, C_N), np.float32)
    cols[:, C_GMIX:C_GMIX + 8] = g_mix.reshape(8, 128).T
    cols[:, C_GFFN:C_GFFN + 8] = g_ffn.reshape(8, 128).T
    cols[:, C_GCO:C_GCO + 4] = g_conv_out.reshape(4, 128).T
    cw = conv_w.reshape(3, 4, 128)
    cols[:, C_CW:C_CW + 12] = np.transpose(cw, (2, 1, 0)).reshape(128, 12)
    cols[:, C_GQ] = np.tile(g_q.reshape(64), 2)
    cols[:, C_GK] = np.tile(g_k.reshape(64), 2)
    gao = g_attn_out.reshape(8, 64).T
    cols[0:64, C_GAO:C_GAO + 8] = gao
    cols[64:128, C_GAO:C_GAO + 8] = gao
    return cols


def _static_consts():
    ident = np.eye(128, dtype=np.float32)
    k = np.arange(128)[:, None]
    q = np.arange(128)[None, :]
    U = (k >= q).astype(np.float32)
    L = (k <= q).astype(np.float32)
    masks = (np.concatenate([L, U, L, U, U, U, U, U, L, L, L, L], axis=1) - 1.0) * 30000.0
    return ident, masks


def kernel(x, g_mix, w_in, conv_w, g_q, g_k, g_conv_out, g_attn_out, w_out, g_ffn, w_gate, w_up, w_down,
           _debug=False, _cores=None, _phases="abc"):
    x = np.asarray(x, np.float32)
    f = lambda a: np.ascontiguousarray(np.asarray(a, np.float32))
    cols = _host_consts(f(g_mix), f(conv_w), f(g_q), f(g_k), f(g_conv_out), f(g_attn_out), f(g_ffn))
    ident, masks = _static_consts()
    key = (bool(_debug), _phases)
    if key not in _NC_CACHE:
        _NC_CACHE[key] = build(debug=_debug, phases=_phases)
    nc = _NC_CACHE[key]
    cores = list(range(NCORES)) if _cores is None else _cores
    shared = {"w_in": f(w_in)[0], "w_out": f(w_out)[0], "w_gate": f(w_gate)[0], "w_up": f(w_up)[0],
              "w_down": f(w_down)[0], "cols": cols, "ident": ident, "masks": masks}
    in_maps = []
    for b in cores:
        m = dict(shared)
        m["x"] = np.ascontiguousarray(x[b])
        in_maps.append(m)
    res = run_bass_kernel_spmd(nc, in_maps, core_ids=list(range(len(cores))))
    if _debug:
        return res.results
    return np.stack([r["out"] for r in res.results], axis=0).astype(np.float32)
```

```python
import numpy as np
import concourse.bass as bass
import concourse.mybir as mybir
from concourse.bass_utils import run_bass_kernel_spmd

F32 = mybir.dt.float32
BF16 = mybir.dt.bfloat16
AF = mybir.ActivationFunctionType
ALU = mybir.AluOpType

S_LEN = 8192
D = 1024
D_IN = 3072
D_FF = 2816
NFT = D_FF // 128
EPS = 1e-6
NCORES = 8
_LIMITS = {}
VROW = 8 * 65

C_GMIX, C_GFFN, C_GCO, C_CW, C_GQ, C_GK, C_GAO, C_N = 0, 8, 16, 20, 32, 33, 34, 42


class Buf:
    __slots__ = ("w", "r", "name")

    def __init__(self, name=""):
        self.w = None
        self.r = []
        self.name = name


class Op:
    __slots__ = ("eng", "fn", "deps", "sig", "val", "dma", "sem", "prev")


class Sched:
    ENGS = ("pe", "act", "dve", "pool", "sp")

    def __init__(self, nc, sems, dsems):
        self.nc = nc
        self.sem = sems
        self.dsem = dsems
        self.cnt = {e: 0 for e in sems}
        self.dcnt = {q: [0] * len(l) for q, l in dsems.items()}
        self.dnext = {q: 0 for q in dsems}
        self.waited = {e: {} for e in self.ENGS}
        self.ops = {e: [] for e in self.ENGS}
        self.bufs = []
        self.barrier_tokens = []

    def buf(self, name=""):
        b = Buf(name)
        self.bufs.append(b)
        return b

    def add(self, eng, fn, reads=(), writes=(), dma=False):
        o = Op()
        o.eng, o.fn, o.dma, o.sig, o.val, o.sem, o.prev = eng, fn, dma, False, None, None, 0
        deps = []
        seen = set()

        def adddep(d):
            if d is None or id(d) in seen:
                return
            seen.add(id(d))
            if (not d.dma) and (not dma) and d.eng == eng and eng == "pe":
                return
            deps.append(d)

        for b in reads:
            adddep(b.w)
        for b in writes:
            adddep(b.w)
            for r in b.r:
                adddep(r)
        o.deps = deps
        for d in deps:
            d.sig = True
        for b in reads:
            b.r.append(o)
        for b in writes:
            b.w = o
            b.r = []
        self.ops[eng].append(o)
        return o

    def flush(self, final=False):
        nc = self.nc
        tokens = []
        for e in self.ENGS:
            lastc = None
            for o in self.ops[e]:
                if not o.dma:
                    lastc = o
            if lastc is not None:
                lastc.sig = True
        for e in self.ENGS:
            for o in self.ops[e]:
                if o.dma:
                    k = self.dnext[e]
                    n = len(self.dsem[e])
                    i = k % n
                    o.sem = self.dsem[e][i]
                    o.prev = self.dcnt[e][i]
                    self.dcnt[e][i] += 16
                    o.val = self.dcnt[e][i]
                    self.dnext[e] = k + 1
                elif o.sig:
                    self.cnt[e] += 1
                    o.val = self.cnt[e]
                    o.sem = self.sem[e]
        for e in self.sem:
            if self.cnt[e] > 0:
                tokens.append((self.sem[e], self.cnt[e]))
        for q in self.dsem:
            for i, s in enumerate(self.dsem[q]):
                if self.dcnt[q][i] > 0:
                    tokens.append((s, self.dcnt[q][i]))

        def run(engname, e):
            W = self.waited[engname]
            for o in self.ops[engname]:
                for d in o.deps:
                    if W.get(id(d.sem), 0) < d.val:
                        e.wait_ge(d.sem, d.val)
                        W[id(d.sem)] = d.val
                if o.dma:
                    if o.prev > 0 and W.get(id(o.sem), 0) < o.prev:
                        e.wait_ge(o.sem, o.prev)
                        W[id(o.sem)] = o.prev
                    o.fn(e).then_inc(o.sem, 16)
                else:
                    ins = o.fn(e)
                    if o.sig:
                        ins.then_inc(o.sem, 1)
            for (s, v) in tokens:
                if W.get(id(s), 0) < v:
                    e.wait_ge(s, v)
                    W[id(s)] = v

        with nc.Block() as block:
            @block.tensor
            def _(e):
                run("pe", e)

            @block.scalar
            def _(e):
                run("act", e)

            @block.vector
            def _(e):
                run("dve", e)

            @block.gpsimd
            def _(e):
                run("pool", e)

            @block.sync
            def _(e):
                run("sp", e)

        self.ops = {e: [] for e in self.ENGS}
        for b in self.bufs:
            b.w = None
            b.r = []
        self.bufs = []


def build(debug=False, phases="abc"):
    nc = bass.Bass("TRN2", target_bir_lowering=False)
    x = nc.dram_tensor("x", [S_LEN, D], F32, kind="ExternalInput").ap()
    w_in = nc.dram_tensor("w_in", [D, D_IN], F32, kind="ExternalInput").ap()
    w_out = nc.dram_tensor("w_out", [D, D], F32, kind="ExternalInput").ap()
    w_gate = nc.dram_tensor("w_gate", [D, D_FF], F32, kind="ExternalInput").ap()
    w_up = nc.dram_tensor("w_up", [D, D_FF], F32, kind="ExternalInput").ap()
    w_down = nc.dram_tensor("w_down", [D_FF, D], F32, kind="ExternalInput").ap()
    cols_d = nc.dram_tensor("cols", [128, C_N], F32, kind="ExternalInput").ap()
    ident_d = nc.dram_tensor("ident", [128, 128], F32, kind="ExternalInput").ap()
    mask_d = nc.dram_tensor("masks", [128, 1536], F32, kind="ExternalInput").ap()
    out = nc.dram_tensor("out", [S_LEN, D], F32, kind="ExternalOutput").ap()
    sk = "ExternalOutput" if debug else "Internal"
    qkT_d = nc.dram_tensor("qkT_d", [8, 128, S_LEN], BF16, kind=sk).ap()
    V_d = nc.dram_tensor("V_d", [S_LEN, VROW], BF16, kind=sk).ap()
    mixT_d = nc.dram_tensor("mixT_d", [8, 128, S_LEN], BF16, kind=sk).ap()

    wo_b = nc.dram_tensor("wo_b", [D, D], BF16, kind="Internal").ap()
    wg_b = nc.dram_tensor("wg_b", [D, D_FF], BF16, kind="Internal").ap()
    wu_b = nc.dram_tensor("wu_b", [D, D_FF], BF16, kind="Internal").ap()
    wd_b = nc.dram_tensor("wd_b", [D_FF, D], BF16, kind="Internal").ap()
    wscr = (w_out, w_gate, w_up, w_down, wo_b, wg_b, wu_b, wd_b)

    from contextlib import ExitStack

    with ExitStack() as gstack:
        def sem(name):
            return gstack.enter_context(nc.semaphore(name))

        sems = {e: sem("s_" + e) for e in ("pe", "act", "dve", "pool")}
        dsems = {"sp": [sem("d_sp%d" % i) for i in range(8)],
                 "pool": [sem("d_pl%d" % i) for i in range(8)]}
        S = Sched(nc, sems, dsems)

        def gsb(name, shape, dt):
            return gstack.enter_context(nc.sbuf_tensor(name, shape, dt))

        cols = gsb("cols_sb", [128, C_N], F32)
        gq8 = gsb("gq8", [128, 1], F32)
        ident = gsb("ident_bf", [128, 128], BF16)
        masks = gsb("masks_bf", [128, 1536], BF16)
        ones_bf = gsb("ones_bf", [128, 128], BF16)
        bones_bf = gsb("bones_bf", [128, 128], BF16)
        ones_f = gsb("ones_f", [128, 64], F32)

        with ExitStack() as st:
            stage = st.enter_context(nc.sbuf_tensor("stage", [128, 1536 + 128], F32))
            b_cols, b_stage = S.buf(), S.buf()
            b_c = S.buf()
            S.add("sp", lambda e: e.dma_start(out=cols[:, :], in_=cols_d), writes=[b_cols], dma=True)
            S.add("sp", lambda e: e.dma_start(out=stage[:, 0:1536], in_=mask_d), writes=[b_stage], dma=True)
            S.add("sp", lambda e: e.dma_start(out=stage[:, 1536:1664], in_=ident_d), writes=[b_stage], dma=True)
            S.add("dve", lambda e: e.tensor_copy(out=masks[:, :], in_=stage[:, 0:1536]), reads=[b_stage], writes=[b_c])
            S.add("dve", lambda e: e.tensor_copy(out=ident[:, :], in_=stage[:, 1536:1664]), reads=[b_stage], writes=[b_c])
            S.add("dve", lambda e: e.tensor_scalar(out=gq8[:, :], in0=cols[:, C_GQ:C_GQ + 1], scalar1=0.125,
                                                     scalar2=None, op0=ALU.mult), reads=[b_cols], writes=[b_c])
            S.add("pool", lambda e: e.memset(ones_bf[:, :], 1.0), writes=[b_c])
            S.add("pool", lambda e: e.memset(bones_bf[:, :], 0.0), writes=[b_c])
            S.add("pool", lambda e: e.memset(bones_bf[0:64, 0:64], 1.0), writes=[b_c])
            S.add("pool", lambda e: e.memset(bones_bf[64:128, 64:128], 1.0), writes=[b_c])
            S.add("pool", lambda e: e.memset(ones_f[:, :], 1.0), writes=[b_c])
            S.flush()

        if "a" in phases:
            phase_a(nc, S, x, w_in, qkT_d, V_d, mixT_d, cols, gq8, ident, ones_bf, bones_bf,
                    wscr if "c" in phases else None)
        if "b" in phases:
            phase_b(nc, S, qkT_d, V_d, mixT_d, cols, masks, ones_bf, ones_f, None, ident)
        if "c" in phases:
            phase_c(nc, S, x, out, wo_b, wg_b, wu_b, wd_b, mixT_d, cols, ident,
                    precast=None if "a" in phases else wscr)
    return nc


def phase_a(nc, S, x, w_in, qkT_d, V_d, mixT_d, cols, gq8, ident, ones_bf, bones_bf, wscr=None):
    NT = _LIMITS.get('a', S_LEN // 512)
    with ExitStack_() as st:
        def sb(name, shape, dt):
            return st.enter_context(nc.sbuf_tensor(name, shape, dt))

        def ps(name, shape, dt=F32):
            return st.enter_context(nc.psum_tensor(name, shape, dt))

        Win = sb("Win", [128, 8, D_IN], BF16)
        xbuf = [sb("xbuf%d" % i, [128, 4, 1024], F32) for i in range(2)]
        junk = sb("junk", [128, 1024], BF16)
        ss = sb("ss", [128, 2, 4], F32)
        rstd = sb("rstd", [128, 2, 4], F32)
        h = sb("h", [128, 4, 1024], BF16)
        hT = [sb("hT%d" % i, [128, 8, 512], BF16) for i in range(2)]
        qko = [sb("qko%d" % i, [128, 8, 512], BF16) for i in range(2)]
        vst = [sb("vst%d" % i, [128, 4, VROW], BF16) for i in range(2)]
        mixo = [sb("mixo%d" % i, [128, 4, 512], BF16) for i in range(2)]
        usb = [sb("usb%d" % i, [128, 512], F32) for i in range(4)]
        gbsb = [sb("gbsb%d" % i, [128, 512], F32) for i in range(4)]
        pbuf = [sb("pbuf%d" % i, [128, 514], F32) for i in range(4)]
        cbuf = [sb("cbuf%d" % i, [128, 512], F32) for i in range(4)]
        ybuf = [sb("ybuf%d" % i, [128, 512], F32) for i in range(4)]
        NR = 3
        sqt = [sb("sqt%d" % i, [128, 512], BF16) for i in range(NR)]
        rst = [sb("rst%d" % i, [128, 512], F32) for i in range(NR)]
        rsc = sb("rsc", [128, 512], F32)

        psT = [ps("psT%d" % i, [128, 1024], BF16) for i in range(2)]
        SSqk = ps("SSqk", [128, 512])
        SSc = SSqk
        NM = 5
        M = [ps("M%d" % i, [128, 512]) for i in range(NM)]
        sqc = [sb("sqc%d" % i, [128, 512], BF16) for i in range(4)]

        B = S.buf
        b_Win = [B() for _ in range(8)]
        b_x = [[B() for _ in range(4)] for _ in range(2)]
        b_junk, b_ss, b_rstd = B(), [B(), B()], [B(), B()]
        b_h = [B() for _ in range(4)]
        b_hT = [[B() for _ in range(8)] for _ in range(2)]
        b_qko, b_vst, b_mixo = [B(), B()], [B(), B()], [B(), B()]
        b_usb, b_gbsb, b_p, b_c, b_y = ([B() for _ in range(4)] for _ in range(5))
        b_usb, b_gbsb, b_p, b_c, b_y = list(b_usb), list(b_gbsb), list(b_p), list(b_c), list(b_y)
        b_sq, b_rs = [B() for _ in range(NR)], [B() for _ in range(NR)]
        b_rsc = B()
        b_psT = [B(), B()]
        b_SSqk = B()
        b_SSc = b_SSqk
        b_sqc = [B() for _ in range(4)]
        b_M = [B() for _ in range(NM)]
        b_init = B()

        w_in_v = w_in.rearrange("(k p) n -> p k n", p=128)
        b_Win = [B() for _ in range(24)]
        order = []
        for f in range(4):
            order += [f, 8 + f, 4 + f]
        order += list(range(12, 24))
        for g in order:
            S.add("pool", lambda e, g=g: e.dma_start(out=Win[:, :, g * 128:(g + 1) * 128], in_=w_in_v[:, :, g * 128:(g + 1) * 128]),
                  writes=[b_Win[g]], dma=True)
        for s in range(2):
            S.add("pool", lambda e, s=s: e.memset(vst[s][:, :, :], 1.0), writes=[b_vst[s]])
        for f in range(4):
            S.add("pool", lambda e, f=f: e.memset(pbuf[f][:, 0:2], 0.0), writes=[b_p[f]])

        mrot = [0]
        rrot = [0]

        def next_bank():
            i = mrot[0] % NM
            mrot[0] += 1
            return i

        def load_x(t):
            s = t % 2
            src = x[t * 512:(t + 1) * 512, :].rearrange("(b p) d -> p b d", p=128)
            S.add("sp", lambda e: e.dma_start(out=xbuf[s][:, :, :], in_=src), writes=b_x[s], dma=True)

        def stats_steps(t):
            s = t % 2
            st_ = []
            for blk in range(4):
                st_.append(lambda blk=blk: S.add("act", lambda e: e.activation(out=junk[:, :], in_=xbuf[s][:, blk, :], func=AF.Square,
                                                                                accum_out=ss[:, s, blk:blk + 1]),
                                                 reads=[b_x[s][blk]], writes=[b_junk, b_ss[s]]))
            st_.append(lambda: S.add("act", lambda e: e.activation(out=rstd[:, s, :], in_=ss[:, s, :], func=AF.Ln, scale=1.0 / D, bias=EPS),
                                     reads=[b_ss[s]], writes=[b_rstd[s]]))
            st_.append(lambda: S.add("act", lambda e: e.activation(out=rstd[:, s, :], in_=rstd[:, s, :], func=AF.Exp, scale=-0.5),
                                     reads=[b_rstd[s]], writes=[b_rstd[s]]))
            for blk in range(4):
                if blk % 2 == 0:
                    st_.append(lambda blk=blk: S.add("act", lambda e: e.activation(out=h[:, blk, :], in_=xbuf[s][:, blk, :], func=AF.Copy,
                                                                                    scale=rstd[:, s, blk:blk + 1]),
                                                     reads=[b_x[s][blk], b_rstd[s]], writes=[b_h[blk]]))
                else:
                    st_.append(lambda blk=blk: S.add("dve", lambda e: e.tensor_scalar(out=h[:, blk, :], in0=xbuf[s][:, blk, :],
                                                                                       scalar1=rstd[:, s, blk:blk + 1], scalar2=None,
                                                                                       op0=ALU.mult),
                                                     reads=[b_x[s][blk], b_rstd[s]], writes=[b_h[blk]]))
            return st_

        def transpose_steps(t):
            s = t % 2
            st_ = []
            for kp in range(4):
                def rnd(kp=kp):
                    pb = kp % 2
                    for kk in range(2):
                        kc = 2 * kp + kk
                        for blk in range(4):
                            S.add("pe", lambda e, kc=kc, blk=blk, pb=pb, kk=kk: e.transpose(
                                out=psT[pb][:, kk * 512 + blk * 128: kk * 512 + (blk + 1) * 128],
                                in_=h[:, blk, kc * 128:(kc + 1) * 128], identity=ident[:, :]),
                                reads=[b_h[blk]], writes=[b_psT[pb]])
                    kc0, kc1 = 2 * kp, 2 * kp + 1
                    for (kc, lo) in ((kc0, 0), (kc1, 512)):
                        if pb == 0:
                            S.add("act", lambda e, kc=kc, lo=lo: e.activation(out=hT[s][:, kc, :], in_=psT[0][:, lo:lo + 512],
                                                                              func=AF.Copy, scale=cols[:, C_GMIX + kc:C_GMIX + kc + 1]),
                                  reads=[b_psT[0]], writes=[b_hT[s][kc]])
                        else:
                            S.add("dve", lambda e, kc=kc, lo=lo: e.tensor_scalar(out=hT[s][:, kc, :], in0=psT[1][:, lo:lo + 512],
                                                                                 scalar1=cols[:, C_GMIX + kc:C_GMIX + kc + 1],
                                                                                 scalar2=None, op0=ALU.mult),
                                  reads=[b_psT[1]], writes=[b_hT[s][kc]])
                st_.append(rnd)
            return st_

        asteps = []

        def pop_astep():
            if asteps:
                asteps.pop(0)()

        def group_fm(t, col0):
            s = t % 2
            bi = next_bank()
            for kc in range(8):
                S.add("pe", lambda e, kc=kc: e.matmul(M[bi][:, :], lhsT=Win[:, kc, col0:col0 + 128], rhs=hT[s][:, kc, :],
                                                       start=(kc == 0), stop=(kc == 7)),
                      reads=[b_Win[col0 // 128], b_hT[s][kc]], writes=[b_M[bi]])
            pop_astep()
            return bi

        pending = []

        def run_pending():
            while pending:
                pending.pop(0)()

        def conv_part(t, f):
            s = t % 2
            bu = group_fm(t, f * 128)
            S.add("act", lambda e: e.activation(out=usb[f][:, :], in_=M[bu][:, :], func=AF.Copy),
                  reads=[b_M[bu]], writes=[b_usb[f]])
            run_pending()
            bc = group_fm(t, 1024 + f * 128)
            S.add("dve", lambda e: e.tensor_tensor(out=pbuf[f][:, 2:514], in0=M[bc][:, :], in1=usb[f][:, :], op=ALU.mult),
                  reads=[b_M[bc], b_usb[f]], writes=[b_p[f]])
            bg = group_fm(t, 512 + f * 128)
            S.add("dve", lambda e: e.tensor_copy(out=gbsb[f][:, :], in_=M[bg][:, :]),
                  reads=[b_M[bg]], writes=[b_gbsb[f]])
            cw = C_CW + 3 * f
            S.add("act", lambda e: e.activation(out=cbuf[f][:, :], in_=pbuf[f][:, 0:512], func=AF.Copy,
                                                scale=cols[:, cw:cw + 1]),
                  reads=[b_p[f]], writes=[b_c[f]])
            S.add("dve", lambda e: e.scalar_tensor_tensor(out=cbuf[f][:, :], in0=pbuf[f][:, 1:513], scalar=cols[:, cw + 1:cw + 2],
                                                           in1=cbuf[f][:, :], op0=ALU.mult, op1=ALU.add),
                  reads=[b_p[f], b_c[f]], writes=[b_c[f]])
            S.add("dve", lambda e: e.scalar_tensor_tensor(out=cbuf[f][:, :], in0=pbuf[f][:, 2:514], scalar=cols[:, cw + 2:cw + 3],
                                                           in1=cbuf[f][:, :], op0=ALU.mult, op1=ALU.add),
                  reads=[b_p[f], b_c[f]], writes=[b_c[f]])
            S.add("act", lambda e: e.activation(out=pbuf[f][:, 0:2], in_=pbuf[f][:, 512:514], func=AF.Copy),
                  reads=[b_p[f]], writes=[b_p[f]])
            S.add("dve", lambda e: e.tensor_tensor(out=ybuf[f][:, :], in0=cbuf[f][:, :], in1=gbsb[f][:, :], op=ALU.mult),
                  reads=[b_c[f], b_gbsb[f]], writes=[b_y[f]])
            S.add("act", lambda e: e.activation(out=sqc[f][:, :], in_=ybuf[f][:, :], func=AF.Square),
                  reads=[b_y[f]], writes=[b_sqc[f]])

            def part2():
                if f == 3:
                    for ff in range(4):
                        S.add("pe", lambda e, ff=ff: e.matmul(SSc[:, :], lhsT=ones_bf[:, :], rhs=sqc[ff][:, :],
                                                               start=(ff == 0), stop=(ff == 3)),
                              reads=[b_sqc[ff]], writes=[b_SSc])
                    S.add("act", lambda e: e.activation(out=rsc[:, :], in_=SSc[:, :], func=AF.Ln, scale=1.0 / 512, bias=EPS),
                          reads=[b_SSc], writes=[b_rsc])
                    S.add("act", lambda e: e.activation(out=rsc[:, :], in_=rsc[:, :], func=AF.Exp, scale=-0.5),
                          reads=[b_rsc], writes=[b_rsc])
                    for ff in range(4):
                        S.add("dve", lambda e, ff=ff: e.scalar_tensor_tensor(
                            out=mixo[s][:, ff, :], in0=ybuf[ff][:, :], scalar=cols[:, C_GCO + ff:C_GCO + ff + 1],
                            in1=rsc[:, :], op0=ALU.mult, op1=ALU.mult),
                            reads=[b_y[ff], b_rsc], writes=[b_mixo[s]])
                    dst = mixT_d[0:4, :, t * 512:(t + 1) * 512].rearrange("f p t -> p f t")
                    S.add("pool", lambda e: e.dma_start(out=dst, in_=mixo[s][:, :, :]), reads=[b_mixo[s]], dma=True)
            pending.append(part2)

        def qk_part(t, ft):
            s = t % 2
            bi = group_fm(t, 1536 + ft * 128)
            ri = rrot[0] % NR
            rrot[0] += 1
            S.add("act", lambda e: e.activation(out=sqt[ri][:, :], in_=M[bi][:, :], func=AF.Square),
                  reads=[b_M[bi]], writes=[b_sq[ri]])
            run_pending()
            gcol = gq8[:, 0:1] if ft < 4 else cols[:, C_GK:C_GK + 1]

            def part2():
                S.add("pe", lambda e: e.matmul(SSqk[:, :], lhsT=bones_bf[:, :], rhs=sqt[ri][:, :], start=True, stop=True),
                      reads=[b_sq[ri]], writes=[b_SSqk])
                S.add("act", lambda e: e.activation(out=rst[ri][:, :], in_=SSqk[:, :], func=AF.Ln, scale=1.0 / 64, bias=EPS),
                      reads=[b_SSqk], writes=[b_rs[ri]])
                S.add("act", lambda e: e.activation(out=rst[ri][:, :], in_=rst[ri][:, :], func=AF.Exp, scale=-0.5),
                      reads=[b_rs[ri]], writes=[b_rs[ri]])
                S.add("dve", lambda e: e.scalar_tensor_tensor(out=qko[s][:, ft, :], in0=M[bi][:, :], scalar=gcol,
                                                              in1=rst[ri][:, :], op0=ALU.mult, op1=ALU.mult),
                      reads=[b_M[bi], b_rs[ri]], writes=[b_qko[s]])
                if ft == 7:
                    dst = qkT_d[:, :, t * 512:(t + 1) * 512].rearrange("j p t -> p j t")
                    S.add("pool", lambda e: e.dma_start(out=dst, in_=qko[s][:, :, :]), reads=[b_qko[s]], dma=True)
            pending.append(part2)

        def v_part(t, blk):
            s = t % 2
            bi = next_bank()
            for kc in range(8):
                S.add("pe", lambda e, kc=kc: e.matmul(M[bi][:, :], lhsT=hT[s][:, kc, blk * 128:(blk + 1) * 128],
                                                       rhs=Win[:, kc, 2560:3072], start=(kc == 0), stop=(kc == 7)),
                      reads=[b_Win[20], b_Win[21], b_Win[22], b_Win[23], b_hT[s][kc]], writes=[b_M[bi]])
            pop_astep()
            dstv = vst[s][:, blk, :].rearrange("p (h e) -> p h e", e=65)[:, :, 0:64]
            srcv = M[bi][:, :].rearrange("p (h d) -> p h d", d=64)
            S.add("act", lambda e: e.activation(out=dstv, in_=srcv, func=AF.Copy), reads=[b_M[bi]], writes=[b_vst[s]])
            run_pending()
            if blk == 3:
                dst = V_d[t * 512:(t + 1) * 512, :].rearrange("(b p) e -> p b e", p=128)
                S.add("pool", lambda e: e.dma_start(out=dst, in_=vst[s][:, :, :]), reads=[b_vst[s]], dma=True)

        pc_list = precast_list(wscr) if wscr is not None else []
        load_x(0)
        if NT > 1:
            load_x(1)
        for f_ in stats_steps(0) + transpose_steps(0):
            f_()
        for t in range(NT):
            if t + 1 < NT:
                asteps.extend(stats_steps(t + 1) + transpose_steps(t + 1))
            for f in range(4):
                conv_part(t, f)
            for ft in range(8):
                qk_part(t, ft)
            for blk in range(4):
                v_part(t, blk)
            run_pending()
            while asteps:
                asteps.pop(0)()
            if t + 2 < NT:
                load_x(t + 2)
            if wscr is not None and t >= 1:
                emit_precast(S, None, n=2, lst=pc_list)
        if wscr is not None:
            emit_precast(S, None, lst=pc_list)
        S.flush()


def ExitStack_():
    from contextlib import ExitStack
    return ExitStack()


def precast_list(wscr):
    w_out, w_gate, w_up, w_down, wo_b, wg_b, wu_b, wd_b = wscr
    lst = []
    for (src_, dst_, rows) in ((w_out, wo_b, D), (w_gate, wg_b, D), (w_up, wu_b, D), (w_down, wd_b, D_FF)):
        step = 256
        for r in range(0, rows, step):
            lst.append((dst_[r:r + step, :], src_[r:r + step, :]))
    return lst


def emit_precast(S, wscr, n=None, lst=None):
    lst = precast_list(wscr) if lst is None else lst
    k = 0
    while lst and (n is None or k < n):
        d_, s_ = lst.pop(0)
        S.add("pool", lambda e, d_=d_, s_=s_: e.dma_start(out=d_, in_=s_), dma=True)
        k += 1


def phase_b(nc, S, qkT_d, V_d, mixT_d, cols, masks, ones_bf, ones_f, wscr=None, ident=None):
    NCH = _LIMITS.get('b', S_LEN // 2048)
    with ExitStack_() as st:
        def sb(name, shape, dt):
            return st.enter_context(nc.sbuf_tensor(name, shape, dt))

        def ps(name, shape, dt=F32):
            return st.enter_context(nc.psum_tensor(name, shape, dt))

        qj = [sb("qz%d" % i, [128, 2, 2048], BF16) for i in range(2)]
        kj = [sb("kj%d" % i, [128, 2, 2048], BF16) for i in range(2)]
        vnat = sb("vnat", [128, 17, VROW], BF16)
        vd4 = sb("vd4", [128, 20, VROW], BF16)
        vd16 = [sb("vd16_%d" % i, [128, 16, VROW], BF16) for i in range(2)]
        NE = 3
        Et = [sb("E%d" % i, [128, 512], BF16) for i in range(NE)]
        num = sb("num", [128, 2048], F32)
        yT = sb("yT", [64, 8, 2048], F32)
        bcs = sb("bcs", [64, 2048], F32)
        ysqacc = sb("ysqacc", [64, 2048], F32)
        rsa = sb("rsa", [64, 512], F32)
        mao = sb("mao", [64, 8, 512], BF16)

        acc = ps("acc", [128, 2048])
        NST = 3
        ST = [ps("ST%d" % i, [128, 512]) for i in range(NST)]
        SSa = [ps("SSa%d" % i, [128, 512]) for i in range(1)]

        B = S.buf
        b_qj, b_kj = [B(), B()], [B(), B()]
        b_vnat, b_vd4, b_vd16 = B(), B(), [B(), B()]
        b_E = [B() for _ in range(NE)]
        b_num, b_yT = B(), [B() for _ in range(8)]
        b_bcs, b_ysqacc, b_rsa, b_mao = B(), B(), B(), B()
        b_rden = [B(), B()]
        rden_d = nc.dram_tensor("rden_d", [2, 2048], F32, kind="Internal").ap()
        b_acc, b_ST, b_SSa = [B() for _ in range(4)], [B() for _ in range(NST)], [B()]
        pending_epi = []
        steps = []
        urgent = []

        erot = [0]
        srot = [0]
        prot = [0]
        for s_ in range(2):
            S.add("pool", lambda e, s_=s_: e.memset(qj[s_][64:128, 0, :], 0.0), writes=[b_qj[s_]])
            S.add("pool", lambda e, s_=s_: e.memset(qj[s_][0:64, 1, :], 0.0), writes=[b_qj[s_]])

        def sl(a, n, step):
            return slice(a, a + (n - 1) * step + 1, step)

        def load_pair(c, j):
            s = prot[0] % 2
            prot[0] += 1
            t0 = c * 2048
            S.add("sp", lambda e: e.dma_start(out=qj[s][0:64, 0, :], in_=qkT_d[j, 0:64, t0:t0 + 2048]), writes=[b_qj[s]], dma=True)
            S.add("sp", lambda e: e.dma_start(out=qj[s][64:128, 1, :], in_=qkT_d[j, 64:128, t0:t0 + 2048]), writes=[b_qj[s]], dma=True)
            if c > 0:
                S.add("sp", lambda e: e.dma_start(out=kj[s][:, :, :],
                                                  in_=qkT_d[4 + j, :, t0 - 2048:t0 + 2048].rearrange("p (c t) -> p c t", c=2)),
                      writes=[b_kj[s]], dma=True)
            else:
                S.add("sp", lambda e: e.dma_start(out=kj[s][:, 1, :], in_=qkT_d[4 + j, :, t0:t0 + 2048]),
                      writes=[b_kj[s]], dma=True)
            return s

        for c in range(NCH):
            t0 = c * 2048
            cp = c % 2
            pp = (c - 1) % 2
            ps_first = load_pair(c, 0)
            if c == 0:
                S.add("sp", lambda e: e.dma_start(out=vnat[:, 1:17, :], in_=V_d[0:2048, :].rearrange("(b p) e -> p b e", p=128)),
                      writes=[b_vnat], dma=True)
                S.add("sp", lambda e: e.dma_start(out=vd4[:, 4:20, :].rearrange("p (t r) e -> p t r e", r=4),
                                                  in_=V_d[0:2048, :].rearrange("(t p r) e -> p t r e", p=128, r=4)),
                      writes=[b_vd4], dma=True)
            else:
                S.add("sp", lambda e, t0=t0: e.dma_start(out=vnat[:, 0:17, :],
                                                         in_=V_d[t0 - 128:t0 + 2048, :].rearrange("(b p) e -> p b e", p=128)),
                      writes=[b_vnat], dma=True)
                S.add("sp", lambda e, t0=t0: e.dma_start(out=vd4[:, 0:20, :].rearrange("p (t r) e -> p t r e", r=4),
                                                         in_=V_d[t0 - 512:t0 + 2048, :].rearrange("(t p r) e -> p t r e", p=128, r=4)),
                      writes=[b_vd4], dma=True)
            S.add("sp", lambda e, t0=t0, cp=cp: e.dma_start(out=vd16[cp][:, :, :],
                                                            in_=V_d[t0:t0 + 2048, :].rearrange("(p r) e -> p r e", r=16)),
                  writes=[b_vd16[cp]], dma=True)

            items = []
            if c > 0:
                items.append((1, 1, 0, sl(1920, 128, 1), vnat, b_vnat, 0, [(0, "U")]))
            for b in range(16):
                qb = [(128 * b, "L")]
                if b < 15:
                    qb.append((128 * (b + 1), "U"))
                items.append((1, 1, 1, sl(128 * b, 128, 1), vnat, b_vnat, b + 1, qb))
            if c > 0:
                for r in range(4):
                    items.append((4, 4, 0, sl(1536 + r, 128, 4), vd4, b_vd4, r, [(r, "U")]))
            for t in range(4):
                for r in range(4):
                    qb = [(512 * t + r, "L")]
                    if t < 3:
                        qb.append((512 * (t + 1) + r, "U"))
                    items.append((4, 4, 1, sl(512 * t + r, 128, 4), vd4, b_vd4, 4 + 4 * t + r, qb))
            if c > 0:
                for r in range(16):
                    items.append((16, 16, 0, sl(r, 128, 16), vd16[pp], b_vd16[pp], r, [(r, "U")]))
            for r in range(16):
                items.append((16, 16, 1, sl(r, 128, 16), vd16[cp], b_vd16[cp], r, [(r, "L")]))

            def cls_of(it):
                kinds = "".join(k for (_q, k) in it[7])
                return kinds
            banks = []
            i = 0
            while i < len(items):
                cl = cls_of(items[i])
                cap = 2 if cl == "LU" else 4
                grp = [items[i]]
                jj = i + 1
                while jj < len(items) and len(grp) < cap and cls_of(items[jj]) == cl:
                    grp.append(items[jj])
                    jj += 1
                banks.append((cl, grp))
                i = jj
            nb = len(banks)

            def make_head(hd, ps_):
                started = [False] * 4

                def emit_st(bi_, hsel=hd % 2, ps_=ps_):
                    cl, grp = banks[bi_]
                    si = srot[0] % NST
                    srot[0] += 1
                    off = 0
                    for (dil, qstep, kc_, ks, vt, vb, vslot, qb) in grp:
                        npart = len(qb)
                        q0 = qb[0][0]
                        if npart == 1:
                            rhs = qj[ps_][:, hsel, slice(q0, q0 + 127 * qstep + 1, qstep)]
                            oap = ST[si][:, off:off + 128]
                        elif dil == 1:
                            rhs = qj[ps_][:, hsel, q0:q0 + 256]
                            oap = ST[si][:, off:off + 256]
                        else:
                            t_, r_ = q0 // 512, q0 % 512
                            rhs = qj[ps_][:, hsel, :].rearrange("p (t m r) -> p t m r", t=4, r=4)[:, t_:t_ + 2, :, r_]
                            oap = ST[si][:, off:off + 256].rearrange("p (a m) -> p a m", a=2)
                        S.add("pe", lambda e, kc_=kc_, ks=ks, rhs=rhs, oap=oap, first=(off == 0): e.matmul(
                            oap, lhsT=kj[ps_][:, kc_, ks], rhs=rhs, start=first, stop=False, skip_group_check=True),
                            reads=[b_kj[ps_], b_qj[ps_]], writes=[b_ST[si]])
                        off += 128 * npart
                    moff = {"LU": 0, "U": 512, "L": 1024}[cl]
                    S.add("pe", lambda e, n=off, moff=moff: e.matmul(ST[si][:, 0:n], lhsT=ident[:, :], rhs=masks[:, moff:moff + n],
                                                                      start=False, stop=True, skip_group_check=True),
                          writes=[b_ST[si]])
                    return si

                def emit_rest(bi_, si, hd=hd, started=started):
                    cl, grp = banks[bi_]
                    n = sum(len(it[7]) for it in grp) * 128
                    ei = erot[0] % NE
                    erot[0] += 1
                    S.add("act", lambda e: e.activation(out=Et[ei][:, 0:n], in_=ST[si][:, 0:n], func=AF.Exp),
                          reads=[b_ST[si]], writes=[b_E[ei]])
                    off = 0
                    for (dil, qstep, kc_, ks, vt, vb, vslot, qb) in grp:
                        vap = vt[:, vslot, hd * 65:(hd + 1) * 65]
                        segs = []
                        if dil == 16:
                            for (q0, _k) in qb:
                                for k in range(4):
                                    segs.append((k, 512 * k + q0, off + 32 * k, 32, 16))
                                off += 128
                        elif dil == 1 and len(qb) == 2 and (qb[0][0] // 512) == (qb[1][0] // 512):
                            segs.append((qb[0][0] // 512, qb[0][0], off, 256, 1))
                            off += 256
                        else:
                            for (q0, _k) in qb:
                                segs.append((q0 // 512, q0, off, 128, qstep))
                                off += 128
                        for (bk, oc, eo, nq, ostep) in segs:
                            first = not started[bk]
                            started[bk] = True
                            oslice = slice(oc, oc + (nq - 1) * ostep + 1, ostep)
                            S.add("pe", lambda e, vap=vap, eo=eo, nq=nq, oslice=oslice, first=first: e.matmul(
                                acc[0:65, oslice], lhsT=vap, rhs=Et[ei][:, eo:eo + nq],
                                start=first, stop=False, skip_group_check=True),
                                reads=[vb, b_E[ei]], writes=[b_acc[bk]])


                def head_end():
                    while urgent:
                        urgent.pop(0)()
                    while steps:
                        f_ = steps.pop(0)
                        if f_ is not None:
                            f_()
                    def evac(k):
                        S.add("act", lambda e: e.activation(out=num[0:65, 512 * k:512 * (k + 1)],
                                                            in_=acc[0:65, 512 * k:512 * (k + 1)], func=AF.Copy),
                              reads=[b_acc[k]], writes=[b_num])
                    evac(0)
                    for k in (1, 2, 3):
                        urgent.append(lambda k=k, evac=evac: evac(k))
                    rs_ = hd % 2

                    def mk_steps(hd=hd, rs_=rs_):
                        st_ = []
                        for k in range(4):
                            cs = slice(512 * k, 512 * (k + 1))
                            st_.append(lambda cs=cs: S.add("act", lambda e: e.activation(out=num[64:65, cs], in_=num[64:65, cs], func=AF.Ln),
                                                           reads=[b_num], writes=[b_num]))
                            st_.append(None)
                        for k in range(4):
                            cs = slice(512 * k, 512 * (k + 1))
                            st_.append(lambda cs=cs: S.add("act", lambda e: e.activation(out=num[64:65, cs], in_=num[64:65, cs], func=AF.Exp,
                                                                                          scale=-1.0),
                                                           reads=[b_num], writes=[b_num]))
                            st_.append(None)
                        st_.append(lambda: S.add("sp", lambda e: e.dma_start(out=rden_d[rs_:rs_ + 1, :], in_=num[64:65, :]),
                                                 reads=[b_num], writes=[b_rden[rs_]], dma=True))
                        st_.append(lambda: S.add("sp", lambda e: e.dma_start(out=bcs[0:64, :],
                                                                             in_=rden_d[rs_:rs_ + 1, :].partition_broadcast(64)),
                                                 reads=[b_rden[rs_]], writes=[b_bcs], dma=True))
                        st_ += [None] * 28
                        for k in range(4):
                            cs = slice(512 * k, 512 * (k + 1))
                            st_.append(lambda cs=cs: S.add("dve", lambda e: e.tensor_tensor(out=yT[:, hd, cs], in0=num[0:64, cs],
                                                                                           in1=bcs[0:64, cs], op=ALU.mult),
                                                           reads=[b_num, b_bcs], writes=[b_yT[hd]]))
                        st_ += [None] * 4
                        for k in range(4):
                            cs = slice(512 * k, 512 * (k + 1))
                            if hd == 0:
                                st_.append(lambda cs=cs: S.add("act", lambda e: e.activation(out=ysqacc[:, cs], in_=yT[:, hd, cs], func=AF.Square),
                                                               reads=[b_yT[hd]], writes=[b_ysqacc]))
                            else:
                                st_.append(lambda cs=cs: S.add("act", lambda e: e.activation(out=bcs[:, cs], in_=yT[:, hd, cs], func=AF.Square),
                                                               reads=[b_yT[hd]], writes=[b_bcs]))
                            st_.append(None)
                        if hd != 0:
                            for k in range(4):
                                cs = slice(512 * k, 512 * (k + 1))
                                st_.append(lambda cs=cs: S.add("dve", lambda e: e.tensor_tensor(out=ysqacc[:, cs], in0=ysqacc[:, cs],
                                                                                               in1=bcs[:, cs], op=ALU.add),
                                                               reads=[b_bcs, b_ysqacc], writes=[b_ysqacc]))
                        return st_
                    steps.extend(mk_steps())


                return emit_st, emit_rest, head_end

            slots = {0: ps_first}
            ctx = {}
            inflight = []
            for hd in range(8):
                j = hd // 2
                for bi_ in range(nb):
                    if bi_ == 0:
                        if hd % 2 == 0 and j < 3:
                            slots[j + 1] = load_pair(c, j + 1)
                        ctx[hd] = make_head(hd, slots[j])
                    si = ctx[hd][0](bi_)
                    inflight.append((hd, bi_, si))
                    if len(inflight) > 2:
                        h0, b0_, s0 = inflight.pop(0)
                        ctx[h0][1](b0_, s0)
                        if b0_ == nb - 1:
                            ctx[h0][2]()
                    if urgent:
                        urgent.pop(0)()
                    if bi_ >= 2:
                        for _ in range(4):
                            if steps:
                                f_ = steps.pop(0)
                                if f_ is not None:
                                    f_()
            while inflight:
                h0, b0_, s0 = inflight.pop(0)
                ctx[h0][1](b0_, s0)
                if b0_ == nb - 1:
                    ctx[h0][2]()

            def mk_epi(t0=t0):
                st_ = []
                for k in range(4):
                    cs = slice(512 * k, 512 * (k + 1))
                    def ssq_step(cs=cs):
                        S.add("pe", lambda e: e.matmul(SSa[0][0:64, :], lhsT=ones_f[0:64, 0:64], rhs=ysqacc[:, cs], start=True, stop=True),
                              reads=[b_ysqacc], writes=[b_SSa[0]])
                        S.add("act", lambda e: e.activation(out=rsa[:, :], in_=SSa[0][0:64, :], func=AF.Ln, scale=1.0 / 512, bias=EPS),
                              reads=[b_SSa[0]], writes=[b_rsa])
                        S.add("act", lambda e: e.activation(out=rsa[:, :], in_=rsa[:, :], func=AF.Exp, scale=-0.5),
                              reads=[b_rsa], writes=[b_rsa])
                    st_.append(ssq_step)
                    st_.append(None)
                    def stt2(h0, cs=cs):
                        for hd in (h0, h0 + 1):
                            S.add("dve", lambda e, hd=hd: e.scalar_tensor_tensor(
                                out=mao[:, hd, :], in0=yT[:, hd, cs], scalar=cols[0:64, C_GAO + hd:C_GAO + hd + 1],
                                in1=rsa[:, :], op0=ALU.mult, op1=ALU.mult),
                                reads=[b_yT[hd], b_rsa], writes=[b_mao])
                    for h0 in (0, 2, 4, 6):
                        st_.append(lambda h0=h0, stt2=stt2: stt2(h0))
                    dst = mixT_d[4:8, :, t0 + 512 * k:t0 + 512 * (k + 1)].rearrange("f (h d) t -> d (f h) t", h=2)
                    st_.append(lambda dst=dst: S.add("pool", lambda e: e.dma_start(out=dst, in_=mao[:, :, :]), reads=[b_mao], dma=True))
                return st_
            steps.extend(mk_epi())
        while urgent:
            urgent.pop(0)()
        while steps:
            f_ = steps.pop(0)
            if f_ is not None:
                f_()
        S.flush()


def phase_c(nc, S, x, out, w_out, w_gate, w_up, w_down, mixT_d, cols, ident, precast=None):
    if precast is not None:
        emit_precast(S, precast)
        S.flush()
    TCB = 2
    TC = TCB * 128
    NT = _LIMITS.get('c', S_LEN // TC)
    with ExitStack_() as st:
        def sb(name, shape, dt):
            return st.enter_context(nc.sbuf_tensor(name, shape, dt))

        def ps(name, shape, dt=F32):
            return st.enter_context(nc.psum_tensor(name, shape, dt))

        Wo = sb("Wo", [128, 8, D], BF16)
        Wg = sb("Wg", [128, 8, D_FF], BF16)
        Wu = sb("Wu", [128, 8, D_FF], BF16)
        Wd = sb("Wd", [128, NFT, D], BF16)
        xbuf = [sb("xc%d" % i, [128, TCB, 1024], F32) for i in range(2)]
        mixT = [sb("mixT%d" % i, [128, 8, TC], BF16) for i in range(2)]
        h2 = sb("h2", [128, TCB, 1024], BF16)
        h2T = [sb("h2T%d" % i, [128, 8, TC], BF16) for i in range(2)]
        actT = sb("actT", [128, NFT, TC], BF16)
        sg = [sb("sg%d" % i, [128, TC], F32) for i in range(2)]
        junk = sb("junkc", [128, 1024], BF16)
        ss = sb("ssc", [128, 2, TCB], F32)
        rstd = sb("rstdc", [128, 2, TCB], F32)

        psT = [ps("psTc%d" % i, [128, 1024], BF16) for i in range(2)]
        NM = 6
        M = [ps("Mc%d" % i, [128, 512]) for i in range(NM)]

        B = S.buf
        b_Wo = [B() for _ in range(8)]
        b_Wg = [B() for _ in range(8)]
        b_Wu = [B() for _ in range(8)]
        b_Wd = [B() for _ in range(NFT)]
        b_x = [[B() for _ in range(TCB)] for _ in range(2)]
        b_mixT = [B(), B()]
        b_h2 = [B() for _ in range(TCB)]
        b_h2T = [[B() for _ in range(8)] for _ in range(2)]
        b_actT = [B() for _ in range(NFT)]
        b_sg = [B(), B()]
        b_junk, b_ss, b_rstd = B(), [B(), B()], [B(), B()]
        b_psT = [B(), B()]
        b_M = [B() for _ in range(NM)]

        wo_v = w_out.rearrange("(k p) n -> p k n", p=128)
        wg_v = w_gate.rearrange("(k p) n -> p k n", p=128)
        wu_v = w_up.rearrange("(k p) n -> p k n", p=128)
        wd_v = w_down.rearrange("(k p) n -> p k n", p=128)
        b_Wg = [B() for _ in range(NFT // 2)]
        b_Wu = [B() for _ in range(NFT // 2)]

        def weight_loads():
            for kc in range(8):
                S.add("sp", lambda e, kc=kc: e.dma_start(out=Wo[:, kc, :], in_=wo_v[:, kc, :]), writes=[b_Wo[kc]], dma=True)
            for cg in range(NFT // 2):
                cs = slice(cg * 256, (cg + 1) * 256)
                S.add("sp", lambda e, cs=cs: e.dma_start(out=Wg[:, :, cs], in_=wg_v[:, :, cs]), writes=[b_Wg[cg]], dma=True)
                S.add("sp", lambda e, cs=cs: e.dma_start(out=Wu[:, :, cs], in_=wu_v[:, :, cs]), writes=[b_Wu[cg]], dma=True)
            for fc in range(NFT):
                S.add("sp", lambda e, fc=fc: e.dma_start(out=Wd[:, fc, :], in_=wd_v[:, fc, :]), writes=[b_Wd[fc]], dma=True)

        mrot = [0]
        grot = [0]

        def next_bank():
            i = mrot[0] % NM
            mrot[0] += 1
            return i

        def loads(t):
            s = t % 2
            a = t * TC
            S.add("sp", lambda e: e.dma_start(out=mixT[s][:, :, :], in_=mixT_d[:, :, a:a + TC].rearrange("k p t -> p k t")),
                  writes=[b_mixT[s]], dma=True)
            for blk in range(TCB):
                src = x[a + blk * 128:a + (blk + 1) * 128, :]
                S.add("sp", lambda e, blk=blk, src=src: e.dma_start(out=xbuf[s][:, blk, :], in_=src),
                      writes=[b_x[s][blk]], dma=True)

        def wout_part(t):
            s = t % 2
            for blk in range(TCB):
                for half in range(2):
                    bi = next_bank()
                    hs = slice(half * 512, (half + 1) * 512)
                    for kc in range(8):
                        S.add("pe", lambda e, kc=kc, blk=blk, hs=hs, bi=bi: e.matmul(
                            M[bi][:, :], lhsT=mixT[s][:, kc, blk * 128:(blk + 1) * 128], rhs=Wo[:, kc, hs],
                            start=(kc == 0), stop=(kc == 7)),
                            reads=[b_mixT[s], b_Wo[kc]], writes=[b_M[bi]])
                    S.add("dve", lambda e, blk=blk, hs=hs, bi=bi: e.tensor_tensor(out=xbuf[s][:, blk, hs], in0=M[bi][:, :],
                                                                                   in1=xbuf[s][:, blk, hs], op=ALU.add),
                          reads=[b_M[bi], b_x[s][blk]], writes=[b_x[s][blk]])
            for blk in range(TCB):
                S.add("act", lambda e, blk=blk: e.activation(out=junk[:, :], in_=xbuf[s][:, blk, :], func=AF.Square,
                                                              accum_out=ss[:, s, blk:blk + 1]),
                      reads=[b_x[s][blk]], writes=[b_junk, b_ss[s]])
            S.add("act", lambda e: e.activation(out=rstd[:, s, :], in_=ss[:, s, :], func=AF.Sqrt, scale=1.0 / D, bias=EPS),
                  reads=[b_ss[s]], writes=[b_rstd[s]])
            S.add("dve", lambda e: e.reciprocal(out=rstd[:, s, :], in_=rstd[:, s, :]), reads=[b_rstd[s]], writes=[b_rstd[s]])
            for blk in range(TCB):
                if blk % 2 == 0:
                    S.add("act", lambda e, blk=blk: e.activation(out=h2[:, blk, :], in_=xbuf[s][:, blk, :], func=AF.Copy,
                                                                  scale=rstd[:, s, blk:blk + 1]),
                          reads=[b_x[s][blk], b_rstd[s]], writes=[b_h2[blk]])
                else:
                    S.add("dve", lambda e, blk=blk: e.tensor_scalar(out=h2[:, blk, :], in0=xbuf[s][:, blk, :],
                                                                     scalar1=rstd[:, s, blk:blk + 1], scalar2=None, op0=ALU.mult),
                          reads=[b_x[s][blk], b_rstd[s]], writes=[b_h2[blk]])

        def transposes(t):
            s = t % 2
            for kp in range(4):
                pb = kp % 2
                for kk in range(2):
                    kc = 2 * kp + kk
                    for blk in range(TCB):
                        S.add("pe", lambda e, kc=kc, blk=blk, pb=pb, kk=kk: e.transpose(
                            out=psT[pb][:, kk * 512 + blk * 128: kk * 512 + (blk + 1) * 128],
                            in_=h2[:, blk, kc * 128:(kc + 1) * 128], identity=ident[:, :]),
                            reads=[b_h2[blk]], writes=[b_psT[pb]])
                kc0, kc1 = 2 * kp, 2 * kp + 1
                S.add("act", lambda e, kc=kc0, pb=pb: e.activation(out=h2T[s][:, kc, :], in_=psT[pb][:, 0:TC],
                                                                   func=AF.Copy, scale=cols[:, C_GFFN + kc:C_GFFN + kc + 1]),
                      reads=[b_psT[pb]], writes=[b_h2T[s][kc0]])
                S.add("act", lambda e, kc=kc1, pb=pb: e.activation(out=h2T[s][:, kc, :], in_=psT[pb][:, 512:512 + TC],
                                                                   func=AF.Copy, scale=cols[:, C_GFFN + kc:C_GFFN + kc + 1]),
                      reads=[b_psT[pb]], writes=[b_h2T[s][kc1]])

        def gateup(t, ft):
            s = t % 2
            bi = next_bank()
            fs = slice(ft * 128, (ft + 1) * 128)
            for kc in range(8):
                S.add("pe", lambda e, kc=kc: e.matmul(M[bi][:, 0:TC], lhsT=Wg[:, kc, fs], rhs=h2T[s][:, kc, :],
                                                       start=(kc == 0), stop=(kc == 7)),
                      reads=[b_Wg[ft // 2], b_h2T[s][kc]], writes=[b_M[bi]])
            for kc in range(8):
                S.add("pe", lambda e, kc=kc: e.matmul(M[bi][:, TC:2 * TC], lhsT=Wu[:, kc, fs], rhs=h2T[s][:, kc, :],
                                                       start=(kc == 0), stop=(kc == 7)),
                      reads=[b_Wu[ft // 2], b_h2T[s][kc]], writes=[b_M[bi]])
            gi = grot[0] % 2
            grot[0] += 1
            S.add("act", lambda e: e.activation(out=sg[gi][:, :], in_=M[bi][:, 0:TC], func=AF.Silu),
                  reads=[b_M[bi]], writes=[b_sg[gi]])
            S.add("dve", lambda e: e.tensor_tensor(out=actT[:, ft, :], in0=M[bi][:, TC:2 * TC], in1=sg[gi][:, :], op=ALU.mult),
                  reads=[b_M[bi], b_sg[gi]], writes=[b_actT[ft]])

        def down(t):
            s = t % 2
            a = t * TC
            for blk in range(TCB):
                for half in range(2):
                    bi = next_bank()
                    hs = slice(half * 512, (half + 1) * 512)
                    for fc in range(NFT):
                        S.add("pe", lambda e, fc=fc, blk=blk, hs=hs, bi=bi: e.matmul(
                            M[bi][:, :], lhsT=actT[:, fc, blk * 128:(blk + 1) * 128], rhs=Wd[:, fc, hs],
                            start=(fc == 0), stop=(fc == NFT - 1)),
                            reads=[b_actT[fc], b_Wd[fc]], writes=[b_M[bi]])
                    S.add("dve", lambda e, blk=blk, hs=hs, bi=bi: e.tensor_tensor(out=xbuf[s][:, blk, hs], in0=M[bi][:, :],
                                                                                   in1=xbuf[s][:, blk, hs], op=ALU.add),
                          reads=[b_M[bi], b_x[s][blk]], writes=[b_x[s][blk]])
                dst = out[a + blk * 128:a + (blk + 1) * 128, :]
                S.add("pool", lambda e, blk=blk, dst=dst: e.dma_start(out=dst, in_=xbuf[s][:, blk, :]),
                      reads=[b_x[s][blk]], dma=True)

        loads(0)
        weight_loads()
        loads(1)
        wout_part(0)
        transposes(0)
        for t in range(NT):
            for ft in range(NFT):
                gateup(t, ft)
                if ft == 7 and t + 1 < NT:
                    wout_part(t + 1)
                if ft == 15 and t + 1 < NT:
                    transposes(t + 1)
            down(t)
            if t + 2 < NT:
                loads(t + 2)
        S.flush(final=True)


_NC_CACHE = {}


def _host_consts(g_mix, conv_w, g_q, g_k, g_conv_out, g_attn_out, g_ffn):
    cols = np.zeros((128, C_N), np.float32)
    cols[:, C_GMIX:C_GMIX + 8] = g_mix.reshape(8, 128).T
    cols[:, C_GFFN:C_GFFN + 8] = g_ffn.reshape(8, 128).T
    cols[:, C_GCO:C_GCO + 4] = g_conv_out.reshape(4, 128).T
    cw = conv_w.reshape(3, 4, 128)
    cols[:, C_CW:C_CW + 12] = np.transpose(cw, (2, 1, 0)).reshape(128, 12)
    cols[:, C_GQ] = np.tile(g_q.reshape(64), 2)
    cols[:, C_GK] = np.tile(g_k.reshape(64), 2)
    gao = g_attn_out.reshape(8, 64).T
    cols[0:64, C_GAO:C_GAO + 8] = gao
    cols[64:128, C_GAO:C_GAO + 8] = gao
    return cols


def _static_consts():
    ident = np.eye(128, dtype=np.float32)
    k = np.arange(128)[:, None]
    q = np.arange(128)[None, :]
    U = (k >= q).astype(np.float32)
    L = (k <= q).astype(np.float32)
    masks = (np.concatenate([L, U, L, U, U, U, U, U, L, L, L, L], axis=1) - 1.0) * 30000.0
    return ident, masks


def kernel(x, g_mix, w_in, conv_w, g_q, g_k, g_conv_out, g_attn_out, w_out, g_ffn, w_gate, w_up, w_down,
           _debug=False, _cores=None, _phases="abc"):
    x = np.asarray(x, np.float32)
    f = lambda a: np.ascontiguousarray(np.asarray(a, np.float32))
    cols = _host_consts(f(g_mix), f(conv_w), f(g_q), f(g_k), f(g_conv_out), f(g_attn_out), f(g_ffn))
    ident, masks = _static_consts()
    key = (bool(_debug), _phases)
    if key not in _NC_CACHE:
        _NC_CACHE[key] = build(debug=_debug, phases=_phases)
    nc = _NC_CACHE[key]
    cores = list(range(NCORES)) if _cores is None else _cores
    shared = {"w_in": f(w_in)[0], "w_out": f(w_out)[0], "w_gate": f(w_gate)[0], "w_up": f(w_up)[0],
              "w_down": f(w_down)[0], "cols": cols, "ident": ident, "masks": masks}
    in_maps = []
    for b in cores:
        m = dict(shared)
        m["x"] = np.ascontiguousarray(x[b])
        in_maps.append(m)
    res = run_bass_kernel_spmd(nc, in_maps, core_ids=list(range(len(cores))))
    if _debug:
        return res.results
    return np.stack([r["out"] for r in res.results], axis=0).astype(np.float32)
```
